# Optimizing a Trainium2 kernel written in Bass

```python
import jax, jax.numpy as jnp
from jax import lax
import numpy as np

D_MODEL = 1024
BATCH = 4
SEQ = 8192
DEPTH = 4

PLE_DIM = 256
N_BRANCH = 4
BRANCH_WIDTH = 256
HEAD_DIM = 64
N_HEADS = BRANCH_WIDTH // HEAD_DIM
CONV_WIDTH = 3
ATTN_BLOCK = 128
GLA_CHUNK = 64
SPATIAL_CHUNK = 128
EPS = 1e-6
MASK_VALUE = -1e30
IN_COLS = 15 * BRANCH_WIDTH + N_HEADS + N_BRANCH * D_MODEL

kernel_name = "hybrid_conv_fox_hgrn2_gmlp_gated_merge"


def _split_points():
    W = BRANCH_WIDTH
    sizes = [W] * 4 + [W] * 4 + [N_HEADS] + [W] * 4 + [W] * 3 + [N_BRANCH * D_MODEL]
    return [int(s) for s in np.cumsum(sizes)[:-1]]


def rms_norm(x, g):
    xf = x.astype(jnp.float32)
    return xf * lax.rsqrt(jnp.mean(xf * xf, axis=-1, keepdims=True) + EPS) * g.astype(jnp.float32)


def group_rms_norm(x, g):
    Bn, S, W = x.shape
    xg = x.astype(jnp.float32).reshape(Bn, S, N_HEADS, HEAD_DIM)
    xg = xg * lax.rsqrt(jnp.mean(xg * xg, axis=-1, keepdims=True) + EPS)
    return (xg * g.astype(jnp.float32).reshape(N_HEADS, HEAD_DIM)).reshape(Bn, S, W)


def short_conv_mixer(x_in, b, c, w, bias):
    S = x_in.shape[1]
    z = c.astype(jnp.float32) * x_in.astype(jnp.float32)
    zp = jnp.pad(z, ((0, 0), (CONV_WIDTH - 1, 0), (0, 0)))
    wf = w.astype(jnp.float32)
    y = zp[:, 0:S] * wf[0]
    for tap in range(1, CONV_WIDTH):
        y = y + zp[:, tap:tap + S] * wf[tap]
    return b.astype(jnp.float32) * (y + bias.astype(jnp.float32))


def forgetting_attention(q, k, v, f_logit, gq, gk):
    Bn, S, _ = q.shape
    f32 = jnp.float32

    def heads(t):
        return t.astype(f32).reshape(Bn, S, N_HEADS, HEAD_DIM).transpose(0, 2, 1, 3)

    qh = rms_norm(heads(q), gq)
    kh = rms_norm(heads(k), gk)
    vh = heads(v)
    cum = jnp.cumsum(jax.nn.log_sigmoid(f_logit.astype(f32)).transpose(0, 2, 1), axis=-1)
    nb = S // ATTN_BLOCK
    qb = qh.reshape(Bn, N_HEADS, nb, ATTN_BLOCK, HEAD_DIM).transpose(2, 0, 1, 3, 4)
    cb = cum.reshape(Bn, N_HEADS, nb, ATTN_BLOCK).transpose(2, 0, 1, 3)
    kpos = jnp.arange(S)
    scale = HEAD_DIM ** -0.5

    def block(args):
        qi, ci, bi = args
        logits = jnp.einsum('bhqd,bhkd->bhqk', qi, kh) * scale + (ci[..., None] - cum[:, :, None, :])
        qpos = bi * ATTN_BLOCK + jnp.arange(ATTN_BLOCK)
        mask = qpos[:, None] >= kpos[None, :]
        probs = jax.nn.softmax(jnp.where(mask, logits, MASK_VALUE), axis=-1)
        return jnp.einsum('bhqk,bhkd->bhqd', probs, vh)

    o = lax.map(block, (qb, cb, jnp.arange(nb)))
    return o.transpose(1, 0, 3, 2, 4).reshape(Bn, S, N_HEADS * HEAD_DIM)


def hgrn2_recurrence(q, f_logit, i_in, lb, gain):
    Bn, S, W = q.shape
    f32 = jnp.float32
    qf = jax.nn.silu(q.astype(f32))
    fl = f_logit.astype(f32)
    lbf = lb.astype(f32)
    log_g = jnp.log(lbf + (1.0 - lbf) * jax.nn.sigmoid(fl))
    kf = (1.0 - lbf) * jax.nn.sigmoid(-fl)
    vf = i_in.astype(f32)
    nc = S // GLA_CHUNK

    def chunks(t):
        return t.reshape(Bn, nc, GLA_CHUNK, N_HEADS, HEAD_DIM).transpose(1, 0, 3, 2, 4)

    causal = jnp.tril(jnp.ones((GLA_CHUNK, GLA_CHUNK), dtype=bool))[:, :, None]

    def step(state, inp):
        qc, kc, vc, gc = inp
        b = jnp.cumsum(gc, axis=2)
        o_inter = jnp.einsum('bhtk,bhkv->bhtv', qc * jnp.exp(b), state)
        diff = b[:, :, :, None, :] - b[:, :, None, :, :]
        decay = jnp.where(causal, jnp.exp(jnp.where(causal, diff, 0.0)), 0.0)
        scores = jnp.einsum('bhtk,bhsk,bhtsk->bhts', qc, kc, decay)
        o_intra = jnp.einsum('bhts,bhsv->bhtv', scores, vc)
        b_last = b[:, :, -1]
        new_state = jnp.exp(b_last)[..., None] * state + jnp.einsum(
            'bhsk,bhsv->bhkv', kc * jnp.exp(b_last[:, :, None] - b), vc)
        return new_state, o_inter + o_intra

    state0 = jnp.zeros((Bn, N_HEADS, HEAD_DIM, HEAD_DIM), f32)
    _, o = lax.scan(step, state0, (chunks(qf), chunks(kf), chunks(vf), chunks(log_g)))
    o = o.transpose(1, 0, 3, 2, 4).reshape(Bn, S, W)
    return group_rms_norm(o, gain)


def spatial_gating_mixer(u, v, gv, w_s, b_s):
    Bn, S, W = u.shape
    vn = group_rms_norm(v, gv).reshape(Bn, S // SPATIAL_CHUNK, SPATIAL_CHUNK, N_HEADS, HEAD_DIM)
    causal = jnp.tril(jnp.ones((SPATIAL_CHUNK, SPATIAL_CHUNK), dtype=jnp.float32))
    w = w_s.astype(jnp.float32) * causal
    s = jnp.einsum('gts,bnsgc->bntgc', w, vn) + b_s.astype(jnp.float32).T[None, None, :, :, None]
    return u.astype(jnp.float32) * s.reshape(Bn, S, W)


def setup_inputs(seed: int = 0) -> dict:
    key = jax.random.key(seed)
    ks = jax.random.split(key, 24)
    W = BRANCH_WIDTH
    n = jax.random.normal
    f32 = jnp.float32
    return {
        "x": n(ks[0], (BATCH, SEQ, D_MODEL), f32),
        "p": n(ks[1], (DEPTH, BATCH, SEQ, PLE_DIM), f32),
        "norm_mix": 1.0 + 0.02 * n(ks[2], (DEPTH, D_MODEL), f32),
        "w_in": n(ks[3], (DEPTH, D_MODEL, IN_COLS), f32) * D_MODEL ** -0.5,
        "conv_w": n(ks[4], (DEPTH, CONV_WIDTH, W), f32) * CONV_WIDTH ** -0.5,
        "conv_b": 0.02 * n(ks[5], (DEPTH, W), f32),
        "fgate_bias": jnp.linspace(1.0, 4.0, N_HEADS, dtype=f32) + 0.1 * n(ks[6], (DEPTH, N_HEADS), f32),
        "q_norm": 1.0 + 0.02 * n(ks[7], (DEPTH, HEAD_DIM), f32),
        "k_norm": 1.0 + 0.02 * n(ks[8], (DEPTH, HEAD_DIM), f32),
        "lb_logits": 0.5 * n(ks[9], (DEPTH, W), f32),
        "hgrn_norm": 1.0 + 0.02 * n(ks[10], (DEPTH, W), f32),
        "sgu_norm": 1.0 + 0.02 * n(ks[11], (DEPTH, W), f32),
        "spatial_w": 0.5 * n(ks[12], (DEPTH, N_HEADS, SPATIAL_CHUNK, SPATIAL_CHUNK), f32) * SPATIAL_CHUNK ** -0.5,
        "spatial_b": 1.0 + 0.02 * n(ks[13], (DEPTH, N_HEADS, SPATIAL_CHUNK), f32),
        "w_up": n(ks[14], (DEPTH, N_BRANCH, W, D_MODEL), f32) * W ** -0.5,
        "merge_b": 0.02 * n(ks[15], (DEPTH, N_BRANCH, D_MODEL), f32),
        "w_o": n(ks[16], (DEPTH, D_MODEL, D_MODEL), f32) * (0.5 * D_MODEL ** -0.5),
        "norm_ple": 1.0 + 0.02 * n(ks[17], (DEPTH, D_MODEL), f32),
        "w_ple_gate": n(ks[18], (DEPTH, D_MODEL, D_MODEL), f32) * D_MODEL ** -0.5,
        "w_ple_proj": n(ks[19], (DEPTH, PLE_DIM, D_MODEL), f32) * (0.5 * PLE_DIM ** -0.5),
    }


def reference(x, p, norm_mix, w_in, conv_w, conv_b, fgate_bias, q_norm, k_norm, lb_logits,
              hgrn_norm, sgu_norm, spatial_w, spatial_b, w_up, merge_b, w_o, norm_ple,
              w_ple_gate, w_ple_proj):
    dt = x.dtype
    Bn, S, _ = x.shape
    splits = _split_points()
    lb_p = jax.nn.softmax(lb_logits.astype(jnp.float32), axis=0)
    lower_bounds = jnp.clip(jnp.cumsum(lb_p, axis=0) - lb_p[0], 0.0, 1.0)
    for li in range(DEPTH):
        h = rms_norm(x, norm_mix[li]).astype(dt)
        z = h @ w_in[li]
        (a_x, a_b, a_c, a_g,
         b_q, b_k, b_v, b_g, b_f,
         c_q, c_f, c_i, c_g,
         d_u, d_v, d_g, m_logits) = jnp.split(z, splits, axis=-1)

        y_a = short_conv_mixer(a_x, a_b, a_c, conv_w[li], conv_b[li]).astype(dt) * jax.nn.silu(a_g)
        y_b = forgetting_attention(b_q, b_k, b_v, b_f + fgate_bias[li], q_norm[li], k_norm[li]).astype(dt) * jax.nn.silu(b_g)
        y_c = hgrn2_recurrence(c_q, c_f, c_i, lower_bounds[li], hgrn_norm[li]).astype(dt) * jax.nn.silu(c_g)
        y_d = spatial_gating_mixer(d_u, d_v, sgu_norm[li], spatial_w[li], spatial_b[li]).astype(dt) * jax.nn.silu(d_g)

        branches = (y_a, y_b, y_c, y_d)
        gate_logits = m_logits.reshape(Bn, S, N_BRANCH, D_MODEL)
        merged = jax.nn.sigmoid(gate_logits[:, :, 0] + merge_b[li, 0]) * (branches[0] @ w_up[li, 0])
        for bi in range(1, N_BRANCH):
            merged = merged + jax.nn.sigmoid(gate_logits[:, :, bi] + merge_b[li, bi]) * (branches[bi] @ w_up[li, bi])
        x = x + merged @ w_o[li]

        hp = rms_norm(x, norm_ple[li]).astype(dt)
        x = x + jax.nn.sigmoid(hp @ w_ple_gate[li]) * (p[li] @ w_ple_proj[li])
    return x
```

```python
import numpy as np
from contextlib import ExitStack
import concourse.bass as bass
import concourse.mybir as mybir
from concourse.bass_utils import run_bass_kernel_spmd

F32 = mybir.dt.float32
BF16 = mybir.dt.bfloat16
AF = mybir.ActivationFunctionType
ALU = mybir.AluOpType
AX = mybir.AxisListType

D = 1024
S = 4096
RG = [[0, 1], [2, 3], [4, 5], [6, 7]]
DEPTH = 4
W = 256
NG = S // 512
NT = S // 128
NCH = S // 64
INC = 7940
EPS = 1e-6
A_X, A_B, A_C, A_G = 0, 256, 512, 768
B_Q, B_K, B_V, B_G, B_F = 1024, 1280, 1536, 1792, 2048
C_Q, C_F, C_I, C_G = 2052, 2308, 2564, 2820
D_U, D_V, D_G = 3076, 3332, 3588
M_G = 3844
NA = 3844


class TK:
    __slots__ = ("w", "r")

    def __init__(self):
        self.w = {}
        self.r = {}


class Sem:
    def __init__(self, h):
        self.h = h
        self.cnt = 0


class Buf:
    def __init__(self, t, sem=None):
        self.t = t
        self.tk = TK()
        self.sem = sem

    def __getitem__(self, k):
        return self.t[k]


class Ctx:
    def __init__(self, nc, es):
        self.nc = nc
        self.es = es
        self.E = {"pe": nc.tensor, "act": nc.scalar, "dve": nc.vector, "pool": nc.gpsimd, "sp": nc.sync}
        self.sem = {e: es.enter_context(nc.semaphore("s_" + e)) for e in ("pe", "act", "dve", "pool")}
        self.cnt = {e: 0 for e in self.sem}
        self.seen = {e: {} for e in self.E}
        self.semobj = dict(self.sem)
        self.nbuf = 0
        self.sems = [Sem(es.enter_context(nc.semaphore(f"d{i}"))) for i in range(64)]
        for sm in self.sems:
            self.semobj[id(sm)] = sm.h
        self.semi = 0
        self.pes = es
        self.ccsem = self.sems.pop()

    def sb(self, shape, dt, dma=False, name=None):
        self.nbuf += 1
        t = self.pes.enter_context(self.nc.sbuf_tensor(name or f"b{self.nbuf}", list(shape), dt))
        b = Buf(t)
        if dma:
            self.add_sem(b)
        return b

    def add_sem(self, b):
        b.sem = self.sems[self.semi]
        self.semi += 1
        return b

    def barrier(self):
        allk = {e: self.cnt[e] for e in self.sem}
        for sm in self.sems + [self.ccsem]:
            allk[id(sm)] = sm.cnt
        for e in self.E:
            self._wait(e, allk)

    def ps(self, shape, dt):
        self.nbuf += 1
        t = self.es.enter_context(self.nc.psum_tensor(f"p{self.nbuf}", list(shape), dt))
        return Buf(t)

    def _wait(self, e, deps):
        eng = self.E[e]
        seen = self.seen[e]
        for key, val in deps.items():
            if e == "pe" and key == "pe":
                continue
            if seen.get(key, 0) >= val:
                continue
            eng.wait_ge(self.semobj[key], val)
            seen[key] = val

    @staticmethod
    def _merge(d, src):
        for k, v in src.items():
            if d.get(k, 0) < v:
                d[k] = v

    def op(self, e, fn, R=(), W=(), Wd=(), inc=True):
        deps = {}
        for b in R:
            self._merge(deps, b.tk.w)
        for b in W:
            self._merge(deps, b.tk.w)
            self._merge(deps, b.tk.r)
        for b in Wd:
            self._merge(deps, b.tk.r)
            self._merge(deps, {k: v for k, v in b.tk.w.items() if k != e})
        self._wait(e, deps)
        ins = fn()
        if inc:
            self.cnt[e] += 1
            ins.then_inc(self.sem[e], 1)
            tick = self.cnt[e]
        else:
            tick = self.cnt[e] + 1
        for b in R:
            if b.tk.r.get(e, 0) < tick:
                b.tk.r[e] = tick
        for b in W:
            b.tk.w = {e: tick}
            b.tk.r = {}
        for b in Wd:
            b.tk.w[e] = tick
        return ins

    def dma(self, q, out, in_, src, dst, disjoint=False, sem_owner=None, **kw):
        owner = (sem_owner or (dst if dst.sem is not None else src)).sem
        key = id(owner)
        deps = {}
        self._merge(deps, src.tk.w)
        self._merge(deps, dst.tk.r)
        if disjoint:
            self._merge(deps, {k: v for k, v in dst.tk.w.items() if k != key})
        else:
            self._merge(deps, dst.tk.w)
        self._wait(q, deps)
        owner.cnt += 16
        self.E[q].dma_start(out=out, in_=in_, **kw).then_inc(owner.h, 16)
        if src.tk.r.get(key, 0) < owner.cnt:
            src.tk.r[key] = owner.cnt
        if disjoint:
            dst.tk.w[key] = owner.cnt
        else:
            dst.tk.w = {key: owner.cnt}
            dst.tk.r = {}

    def collective(self, src, dst):
        deps = {}
        self._merge(deps, src.tk.w)
        self._merge(deps, dst.tk.r)
        self._merge(deps, dst.tk.w)
        self._wait("pool", deps)
        sm = self.ccsem
        sm.cnt += 1
        self.nc.gpsimd.collective_compute("AllGather", ALU.bypass, replica_groups=RG,
                                          ins=[src.t.opt()], outs=[dst.t.opt()]).then_inc(sm.h, 1)
        key = id(sm)
        src.tk.r[key] = sm.cnt
        dst.tk.w = {key: sm.cnt}
        dst.tk.r = {}

    def wait_all(self, q, bufs):
        deps = {}
        for b in bufs:
            self._merge(deps, b.tk.w)
            self._merge(deps, b.tk.r)
        self._wait(q, deps)


class Phase:
    def __init__(self, C):
        self.C = C
        self.es = ExitStack()
        self.es.__enter__()
        C.pes = self.es
        self.semi0 = C.semi

    def close(self):
        C = self.C
        C.barrier()
        C.pes = C.es
        C.semi = self.semi0
        self.es.__exit__(None, None, None)


class Rot:
    def __init__(self, items):
        self.items = items
        self.i = 0

    def next(self):
        b = self.items[self.i % len(self.items)]
        self.i += 1
        return b


def build_nc(n_layers=DEPTH):
    nc = bass.Bass("TRN2", target_bir_lowering=False)
    with ExitStack() as es:
        es.enter_context(nc.allow_non_contiguous_dma("small strided parameter loads"))
        _build(nc, es, n_layers)
    return nc


def _build(nc, es, n_layers):
    C = Ctx(nc, es)
    op, dma = C.op, C.dma

    def dram_in(name, shape):
        return Buf(nc.dram_tensor(name, list(shape), F32, kind="ExternalInput").ap())

    x_in = dram_in("x", [S, D])
    p_in = dram_in("p", [DEPTH, S, W])
    norm_mix = dram_in("norm_mix", [DEPTH, D])
    w_in = dram_in("w_in", [DEPTH, D, INC])
    conv_w = dram_in("conv_w", [DEPTH, 3, W])
    conv_b = dram_in("conv_b", [DEPTH, W])
    fgate_bias = dram_in("fgate_bias", [DEPTH, 4])
    q_norm = dram_in("q_norm", [DEPTH, 64])
    k_norm = dram_in("k_norm", [DEPTH, 64])
    lb_logits = dram_in("lb_logits", [DEPTH, W])
    hgrn_norm = dram_in("hgrn_norm", [DEPTH, W])
    sgu_norm = dram_in("sgu_norm", [DEPTH, W])
    spatial_w = dram_in("spatial_w", [DEPTH, 4, 128, 128])
    spatial_b = dram_in("spatial_b", [DEPTH, 4, 128])
    w_up = dram_in("w_up", [DEPTH, 4, W, D])
    merge_b = dram_in("merge_b", [DEPTH, 4, D])
    w_o = dram_in("w_o", [DEPTH, D, D])
    norm_ple = dram_in("norm_ple", [DEPTH, D])
    w_pg = dram_in("w_ple_gate", [DEPTH, D, D])
    w_pp = dram_in("w_ple_proj", [DEPTH, W, D])
    y_out = Buf(nc.dram_tensor("y", [S, D], F32, kind="ExternalOutput").ap())

    def scratch(name, shape, dt):
        return Buf(nc.dram_tensor(name, list(shape), dt).ap())

    xres = scratch("xres", [S, D], F32)
    hT_d = scratch("hT_d", [D, S], BF16)
    y_d = [scratch(f"ybr{i}", [W, S], BF16) for i in range(4)]
    qaug_d = scratch("qaug", [4, 67, S], BF16)
    xs1a = scratch("xs1a", [134, S], BF16); xg1a = scratch("xg1a", [268, S], BF16)
    xs1b = scratch("xs1b", [134, S], BF16); xg1b = scratch("xg1b", [268, S], BF16)
    xs1c = scratch("xs1c", [256, S], BF16); xg1c = scratch("xg1c", [512, S], BF16)
    xs2 = scratch("xs2", [128, 132], F32)
    xg2 = scratch("xg2", [256, 132], F32)
    xs3 = scratch("xs3", [64, 256], F32)
    xg3 = scratch("xg3", [128, 256], F32)
    def kaug_h(h):
        b = xs1a if h < 2 else xs1b
        return b, b.t[67 * (h % 2):67 * (h % 2) + 67, :]

    def kaugP_h(h):
        b = xg1a if h < 2 else xg1b
        return b, b.t[67 * (h % 2):67 * (h % 2) + 67, :]

    def qaug_h(h):
        return qaug_d, qaug_d.t[h]
    vtok_d = Buf(xs1c.t[:, :].rearrange("r (q f) -> (r q) f", f=256)); vtok_d.tk = xs1c.tk
    vtokP_d = Buf(xg1c.t[0:256, :].rearrange("r (q f) -> (r q) f", f=256)); vtokP_d.tk = xg1c.tk
    flag_in = dram_in("flag", [128, 1])
    gB_d = scratch("gB", [W, S], BF16)
    gC_d = scratch("gC", [W, S], BF16)
    Qp_d = scratch("Qp", [W, S], BF16)
    Kp_d = scratch("Kp", [W, S], BF16)
    vH_d = scratch("vH", [S, W], BF16)
    mT_d = scratch("mT", [D, S], BF16)
    ut_d = Buf(nc.dram_tensor("ut_d", [64, NCH * 256], F32).ap().rearrange("p (c f) -> p c f", f=256))

    ident_bf = C.sb([128, 128], BF16)
    ident_f = C.sb([128, 128], F32)
    blockones = C.sb([128, 128], BF16)
    tri = C.sb([64, 64], F32)
    maskT = C.sb([128, 128], BF16)
    resetm = C.sb([128, 512], F32)
    ones_f = C.sb([128, 512], F32)
    onesb = C.sb([4, 512], BF16, dma=True)
    eps_t = C.sb([128, 1], F32)
    one_t = C.sb([128, 1], F32)

    pool = nc.gpsimd
    op("pool", lambda: pool.memset(ident_bf[:], 0.0), W=[ident_bf])
    op("pool", lambda: pool.affine_select(out=ident_bf[:], in_=ident_bf[:], pattern=[[-1, 128]], compare_op=ALU.not_equal, fill=1.0, base=0, channel_multiplier=1), W=[ident_bf])
    op("pool", lambda: pool.memset(ident_f[:], 0.0), W=[ident_f])
    op("pool", lambda: pool.affine_select(out=ident_f[:], in_=ident_f[:], pattern=[[-1, 128]], compare_op=ALU.not_equal, fill=1.0, base=0, channel_multiplier=1), W=[ident_f])
    op("pool", lambda: pool.memset(blockones[:], 0.0), W=[blockones])
    op("pool", lambda: pool.memset(blockones[0:64, 0:64], 1.0), W=[blockones])
    op("pool", lambda: pool.memset(blockones[64:128, 64:128], 1.0), W=[blockones])
    op("pool", lambda: pool.memset(tri[:], 1.0), W=[tri])
    op("pool", lambda: pool.affine_select(out=tri[:], in_=tri[:], pattern=[[1, 64]], compare_op=ALU.is_ge, fill=0.0, base=0, channel_multiplier=-1), W=[tri])
    op("pool", lambda: pool.memset(maskT[:], 0.0), W=[maskT])
    op("pool", lambda: pool.affine_select(out=maskT[:], in_=maskT[:], pattern=[[1, 128]], compare_op=ALU.is_ge, fill=-30000.0, base=0, channel_multiplier=-1), W=[maskT])
    op("pool", lambda: pool.memset(resetm[:], 1.0), W=[resetm])
    op("pool", lambda: pool.memset(resetm[:].rearrange("p (c t) -> p c t", t=64)[:, :, 0:1], 0.0), W=[resetm])
    op("pool", lambda: pool.memset(ones_f[:], 1.0), W=[ones_f])
    op("pool", lambda: pool.memset(onesb[:], 1.0), W=[onesb])
    op("pool", lambda: pool.memset(eps_t[:], EPS), W=[eps_t])
    op("pool", lambda: pool.memset(one_t[:], 1.0), W=[one_t])
    for h in range(4):
        for c4 in range(S // 512):
            dma("sp", kaug_h(h)[1][64:67, c4 * 512:(c4 + 1) * 512], onesb[0:3, :], onesb, kaug_h(h)[0], disjoint=True)

    flag_t = C.sb([128, 1], F32, dma=True)
    negbig = C.sb([128, 1], F32)
    dma("sp", flag_t[:], flag_in.t[:, :], flag_in, flag_t)
    op("dve", lambda: nc.vector.tensor_scalar(out=negbig[:], in0=flag_t[:], scalar1=-1.0, scalar2=30000.0, op0=ALU.add, op1=ALU.mult), R=[flag_t], W=[negbig])
    GF = C.sb([128, 2, 2], F32)
    YA0 = C.sb([128, 2, 2], F32)
    biasP = C.sb([128, NT, 4], F32)
    lbl = C.sb([128, 2, 4], F32, dma=True)
    for l4 in range(DEPTH):
        dma("sp", lbl[:, :, l4], lb_logits.t[l4].rearrange("(j p) -> p j", p=128), lb_logits, lbl, disjoint=(l4 > 0))
    lbe = C.sb([128, 2, 4], F32)
    lbs = C.sb([128, 2], F32)
    lbp = C.sb([128, 2, 4], F32)
    lbc = C.sb([128, 2, 4], F32)
    LB = C.sb([128, 2, 4], F32)
    OML = C.sb([128, 2, 4], F32)
    NOML = C.sb([128, 2, 4], F32)
    op("act", lambda: nc.scalar.activation(out=lbe[:], in_=lbl[:], func=AF.Exp), R=[lbl], W=[lbe])
    op("dve", lambda: nc.vector.reduce_sum(out=lbs[:], in_=lbe[:], axis=AX.X), R=[lbe], W=[lbs])
    op("dve", lambda: nc.vector.reciprocal(out=lbs[:], in_=lbs[:]), W=[lbs])
    op("dve", lambda: nc.vector.tensor_tensor(out=lbp[:], in0=lbe[:], in1=lbs[:].unsqueeze(2).to_broadcast([128, 2, 4]), op=ALU.mult), R=[lbe, lbs], W=[lbp])
    op("dve", lambda: nc.vector.memset(lbc[:, :, 0:1], 0.0), W=[lbc])
    op("dve", lambda: nc.vector.tensor_copy(out=lbc[:, :, 1:2], in_=lbp[:, :, 1:2]), R=[lbp], W=[lbc])
    op("dve", lambda: nc.vector.tensor_tensor(out=lbc[:, :, 2:3], in0=lbc[:, :, 1:2], in1=lbp[:, :, 2:3], op=ALU.add), R=[lbp], W=[lbc])
    op("dve", lambda: nc.vector.tensor_tensor(out=lbc[:, :, 3:4], in0=lbc[:, :, 2:3], in1=lbp[:, :, 3:4], op=ALU.add), R=[lbp], W=[lbc])
    op("dve", lambda: nc.vector.tensor_scalar(out=LB[:], in0=lbc[:], scalar1=0.0, scalar2=1.0, op0=ALU.max, op1=ALU.min), R=[lbc], W=[LB])
    op("dve", lambda: nc.vector.tensor_scalar(out=OML[:], in0=LB[:], scalar1=-1.0, scalar2=1.0, op0=ALU.mult, op1=ALU.add), R=[LB], W=[OML])
    op("dve", lambda: nc.vector.tensor_scalar(out=NOML[:], in0=LB[:], scalar1=-1.0, scalar2=None, op0=ALU.add), R=[LB], W=[NOML])

    WBIG = C.sb([128, 8 * NA], BF16, name="wbig")
    WB_A = Buf(WBIG.t)
    WB_T = Buf(WBIG.t)
    WB2 = C.sb([128, 10240], BF16, name="wb2")
    GMALL = C.sb([128, DEPTH, 16], F32, dma=True)
    for l4 in range(DEPTH):
        dma("sp", GMALL[:, l4, 0:8], norm_mix.t[l4].rearrange("(k p) -> p k", p=128), norm_mix, GMALL, disjoint=(l4 > 0))
        dma("sp", GMALL[:, l4, 8:16], norm_ple.t[l4].rearrange("(k p) -> p k", p=128), norm_ple, GMALL, disjoint=True)
    stg = Rot([C.sb([128, 1024], F32, dma=True) for _ in range(2)])
    banks = Rot([C.ps([128, 512], F32) for _ in range(4)])
    obanks = Rot([C.ps([128, 512], F32) for _ in range(2)])
    abanks = Rot(banks.items + obanks.items)
    tbanks = Rot([C.ps([128, 1024], BF16) for _ in range(2)])
    PRM = C.sb([128, 80], F32, dma=True)
    SG = C.sb([128, 256], F32, dma=True)
    WSP = C.sb([128, 4, 128], F32, dma=True)
    WT = C.sb([128, 4, 128], BF16)
    BT = C.sb([128, 2, 128], F32, dma=True)
    cposT = C.sb([128, NT, 4], F32)
    CV = C.sb([64, 3, 4, NCH], F32)

    cast_engs = Rot(["dve", "pool", "dve", "act"])

    def cast_mul(e, out, in_, sc):
        if e == "act":
            return nc.scalar.mul(out=out, in_=in_, mul=sc) if sc is not None else nc.scalar.copy(out=out, in_=in_)
        eng = nc.vector if e == "dve" else nc.gpsimd
        if sc is None:
            return eng.tensor_copy(out=out, in_=in_)
        return eng.tensor_scalar(out=out, in0=in_, scalar1=sc, scalar2=None, op0=ALU.mult)

    def load_w(dst_ap_fn, src_rows_fn, ncols, gain_col, wbufs, ks=range(8), engs=None, defer=None):
        for k in ks:
            c0 = 0
            while c0 < ncols:
                n = min(1024, ncols - c0)

                def emit(k=k, c0=c0, n=n):
                    st = stg.next()
                    srcb, srcap = src_rows_fn(k, c0, n)
                    dma("sp", st[:, 0:n], srcap, srcb, st)
                    e = cast_engs.next() if engs is None else engs.next()
                    g = gain_col(k) if gain_col is not None else None
                    d = dst_ap_fn(k, c0, n)
                    op(e, (lambda: cast_mul(e, d, st[:, 0:n], g)), R=[st, GMALL], Wd=list(wbufs))
                if defer is None:
                    emit()
                else:
                    defer.append(emit)
                c0 += n

    def drain(lst, n=None):
        cnt = 0
        while lst and (n is None or cnt < n):
            lst.pop(0)()
            cnt += 1

    def rms_rstd(src_ap, srcb, n, junk, st1):
        st = st1.next()
        op("act", lambda: nc.scalar.activation(out=junk[:, 0:n], in_=src_ap, func=AF.Square, accum_out=st[:, 0:1]), R=[srcb], W=[junk, st])
        op("act", lambda: nc.scalar.activation(out=st[:, 1:2], in_=st[:, 0:1], func=AF.Sqrt, bias=eps_t[:], scale=1.0 / n), R=[eps_t], W=[st])
        op("dve", lambda: nc.vector.reciprocal(out=st[:, 2:3], in_=st[:, 1:2]), W=[st])
        return st

    for l in range(n_layers):
        xsrc = x_in if l == 0 else xres
        xdst = y_out if l == n_layers - 1 else xres
        dma("sp", PRM[:, 0:8], norm_mix.t[l].rearrange("(k p) -> p k", p=128), norm_mix, PRM)
        dma("sp", PRM[:, 8:16], norm_ple.t[l].rearrange("(k p) -> p k", p=128), norm_ple, PRM, disjoint=True)
        for tp in range(3):
            dma("sp", PRM[:, 16:22].rearrange("p (j t) -> p j t", t=3)[:, :, tp], conv_w.t[l, tp].rearrange("(j p) -> p j", p=128), conv_w, PRM, disjoint=True)
        dma("sp", PRM[:, 22:24], conv_b.t[l].rearrange("(j p) -> p j", p=128), conv_b, PRM, disjoint=True)
        for hh in range(2):
            dma("sp", PRM[64 * hh:64 * hh + 64, 24:25], q_norm.t[l].rearrange("(p o) -> p o", o=1), q_norm, PRM, disjoint=True)
            dma("sp", PRM[64 * hh:64 * hh + 64, 25:26], k_norm.t[l].rearrange("(p o) -> p o", o=1), k_norm, PRM, disjoint=True)
        dma("sp", PRM[0:64, 32:36], hgrn_norm.t[l].rearrange("(h p) -> p h", p=64), hgrn_norm, PRM, disjoint=True)
        for br_ in range(4):
            dma("sp", PRM[:, 36 + 8 * br_:44 + 8 * br_], merge_b.t[l, br_].rearrange("(j p) -> p j", p=128), merge_b, PRM, disjoint=True)
        dma("sp", PRM[0:4, 68:69], fgate_bias.t[l].rearrange("(p o) -> p o", o=1), fgate_bias, PRM, disjoint=True)
        op("dve", lambda: nc.vector.tensor_scalar(out=PRM[:, 26:27], in0=PRM[:, 24:25], scalar1=0.125, scalar2=None, op0=ALU.mult), R=[PRM], Wd=[PRM])
        op("dve", lambda: nc.vector.tensor_scalar(out=PRM[0:4, 69:70], in0=PRM[0:4, 68:69], scalar1=-1.0, scalar2=None, op0=ALU.mult), R=[PRM], Wd=[PRM])
        gmix = lambda k, l=l: GMALL[:, l, k:k + 1]
        gple = lambda k, l=l: GMALL[:, l, 8 + k:9 + k]
        convw = lambda j, t: PRM[:, 16 + 3 * j + t:17 + 3 * j + t]
        convb = lambda j: PRM[:, 22 + j:23 + j]
        gq8 = PRM[:, 26:27]
        gk = PRM[:, 25:26]
        hgain = lambda h: PRM[0:64, 32 + h:33 + h]
        mergeb = lambda b, j: PRM[:, 36 + 8 * b + j:37 + 8 * b + j]
        nfgb = PRM[0:4, 69:70]
        lb_c = lambda j: LB[:, j, l:l + 1]
        oml_c = lambda j: OML[:, j, l:l + 1]
        noml_c = lambda j: NOML[:, j, l:l + 1]
        dma("sp", SG[:], sgu_norm.t[l:l + 1, :].partition_broadcast(128), sgu_norm, SG)
        dma("sp", WSP[:], spatial_w.t[l].rearrange("g t s -> t g s"), spatial_w, WSP)
        for gh in range(4):
            hh, j = gh % 2, gh // 2
            dma("sp", BT[64 * hh:64 * hh + 64, j, :], spatial_b.t[l, gh:gh + 1, :].partition_broadcast(64), spatial_b, BT, disjoint=(gh > 0))
        for gh in range(4):
            bk = abanks.next()
            op("pe", lambda bk=bk, gh=gh: nc.tensor.transpose(out=bk[:, 0:128], in_=WSP[:, gh, :], identity=ident_f[:]), R=[WSP, ident_f], W=[bk])
            op("dve", lambda bk=bk, gh=gh: nc.vector.scalar_tensor_tensor(out=WT[:, gh, :], in0=maskT[:], scalar=-1.0, in1=bk[:, 0:128], op0=ALU.is_gt, op1=ALU.mult), R=[bk, maskT], Wd=[WT])

        WA = WBIG[:, 0:8 * NA].rearrange("p (k c) -> p k c", k=8)

        def load_wa(lw, ks, defer=None, engs=None):
            load_w(lambda k, c0, n: WA[:, k, c0:c0 + n],
                   lambda k, c0, n: (w_in, w_in.t[lw, k * 128:(k + 1) * 128, c0:c0 + n]),
                   NA, (lambda k: GMALL[:, lw, k:k + 1]), ([WB_A] if max(ks) < 5 else [WB_A, WB_T]), ks=ks, defer=defer, engs=engs)
        if l == 0:
            load_wa(0, range(0, 5))
        else:
            drain(pre_a)
        load_wa(l, range(5, 8))
        pre_a = []

        ph = Phase(C)
        xin_r = Rot([C.sb([128, 1024], F32, dma=True) for _ in range(2)])
        junk = C.sb([128, 1024], BF16)
        hb_r = Rot([C.sb([128, 1024], BF16) for _ in range(2)])
        st1 = Rot([C.sb([128, 4], F32) for _ in range(4)])
        hTg_r = Rot([C.sb([128, 8, 512], BF16, dma=True) for _ in range(2)])
        f32w = Rot([C.sb([128, 512], F32) for _ in range(8)])
        bfw = Rot([C.sb([128, 512], BF16, dma=True) for _ in range(8)])
        zc_r = [C.sb([128, 514], F32, dma=True) for _ in range(2)]
        cp_r = Rot([C.sb([4, 512], F32) for _ in range(2)])
        cpx = Rot([C.sb([4, 512], F32) for _ in range(3)])
        cpb = Rot([C.sb([4, 512], BF16, dma=True) for _ in range(6)])
        ug_r = Rot([C.sb([128, 2, 512], F32) for _ in range(2)])
        ydb_r = Rot([C.sb([128, 2, 512], BF16, dma=True) for _ in range(2)])
        tokw = Rot([C.sb([128, 256], F32) for _ in range(4)])
        tokb = Rot([C.sb([128, 256], BF16, dma=True) for _ in range(4)])
        vnb_r = Rot([C.sb([128, 256], BF16) for _ in range(4)])
        vhb = Rot([C.sb([64, 2, 256], BF16, dma=True) for _ in range(3)])

        for j in range(2):
            op("pool", lambda j=j: nc.gpsimd.memset(zc_r[j][:, 0:2], 0.0), W=[zc_r[j]])
        prev_cp = None

        def norm_tile(ti):
            tok = slice(ti * 128, (ti + 1) * 128)
            xin = xin_r.next()
            dma("sp", xin[:], xsrc.t[tok, :], xsrc, xin)
            st = rms_rstd(xin[:], xin, 1024, junk, st1)
            hb = hb_r.next()
            op("dve", lambda: nc.vector.tensor_scalar(out=hb[:], in0=xin[:], scalar1=st[:, 2:3], scalar2=None, op0=ALU.mult), R=[xin, st], W=[hb])
            return hb

        hb_next = norm_tile(0)
        for g in range(NG):
            cols = slice(g * 512, (g + 1) * 512)
            hTg = hTg_r.next()
            for t in range(4):
                hb = hb_next
                if g * 4 + t + 1 < NT:
                    hb_next = norm_tile(g * 4 + t + 1)
                tb = tbanks.next()
                for k in range(8):
                    op("pe", lambda k=k, tb=tb, hb=hb: nc.tensor.transpose(out=tb[:, k * 128:(k + 1) * 128], in_=hb[:, k * 128:(k + 1) * 128], identity=ident_bf[:]), R=[hb, ident_bf], W=[tb], inc=(k == 7))
                op("act", lambda tb=tb, hTg=hTg, t=t: nc.scalar.copy(out=hTg[:, :, t * 128:(t + 1) * 128], in_=tb[:].rearrange("p (k t) -> p k t", k=8)), R=[tb], Wd=[hTg])
            dma("pool", hT_d.t.rearrange("(k p) s -> p k s", p=128)[:, :, cols], hTg[:], hTg, hT_d, disjoint=True)

            def fm(c0, M=128):
                bk = abanks.next()
                for k in range(8):
                    op("pe", lambda k=k, bk=bk: nc.tensor.matmul(bk[0:M, :], lhsT=WA[:, k, c0:c0 + M], rhs=hTg[:, k, :], start=(k == 0), stop=(k == 7)), R=[WB_A, WB_T, hTg], W=[bk], inc=(k == 7))
                return bk

            def store(dst, row0, nrows, srcbuf, src_ap):
                dma("pool", dst.t[row0:row0 + nrows, cols], src_ap, srcbuf, dst, disjoint=True)

            for j in range(2):
                bx = fm(A_X + 128 * j)
                tmp = f32w.next()
                op("act", lambda: nc.scalar.copy(out=tmp[:], in_=bx[:]), R=[bx], W=[tmp])
                bc = fm(A_C + 128 * j)
                zc = zc_r[j]
                op("dve", lambda: nc.vector.tensor_tensor(out=zc[:, 2:514], in0=tmp[:], in1=bc[:], op=ALU.mult), R=[tmp, bc], W=[zc])
                a1 = f32w.next()
                a2 = f32w.next()
                op("dve", lambda: nc.vector.tensor_scalar(out=a1[:], in0=zc[:, 2:514], scalar1=convw(j, 2), scalar2=convb(j), op0=ALU.mult, op1=ALU.add), R=[zc, PRM], W=[a1])
                op("dve", lambda: nc.vector.scalar_tensor_tensor(out=a2[:], in0=zc[:, 1:513], scalar=convw(j, 1), in1=a1[:], op0=ALU.mult, op1=ALU.add), R=[zc, a1, PRM], W=[a2])
                op("dve", lambda: nc.vector.scalar_tensor_tensor(out=a1[:], in0=zc[:, 0:512], scalar=convw(j, 0), in1=a2[:], op0=ALU.mult, op1=ALU.add), R=[zc, a2, PRM], W=[a1])
                op("pool", lambda: nc.gpsimd.tensor_copy(out=zc[:, 0:2], in_=zc[:, 512:514]), W=[zc])
                bb = fm(A_B + 128 * j)
                op("dve", lambda: nc.vector.tensor_tensor(out=a2[:], in0=a1[:], in1=bb[:], op=ALU.mult), R=[a1, bb], W=[a2])
                bg = fm(A_G + 128 * j)
                sg_ = f32w.next()
                op("act", lambda: nc.scalar.activation(out=sg_[:], in_=bg[:], func=AF.Silu), R=[bg], W=[sg_])
                yb_ = bfw.next()
                op("dve", lambda: nc.vector.tensor_tensor(out=yb_[:], in0=a2[:], in1=sg_[:], op=ALU.mult), R=[a2, sg_], W=[yb_])
                if g == 0:
                    op("dve", lambda: nc.vector.tensor_tensor(out=GF[:, j, :], in0=bb[:, 0:2], in1=sg_[:, 0:2], op=ALU.mult), R=[bb, sg_], Wd=[GF])
                    op("dve", lambda: nc.vector.tensor_tensor(out=YA0[:, j, :], in0=a2[:, 0:2], in1=sg_[:, 0:2], op=ALU.mult), R=[a2, sg_], Wd=[YA0])
                if g == NG - 1:
                    dma("pool", xs2.t[:, 128 + 2 * j:130 + 2 * j], zc[:, 512:514], zc, xs2, disjoint=True)
                store(y_d[0], 128 * j, 128, yb_, yb_[:])

            for (c0, gcol, dstf) in ((B_Q, gq8, qaug_h), (B_K, gk, kaug_h)):
                for j in range(2):
                    bq = fm(c0 + 128 * j)
                    sq = bfw.next()
                    op("act", lambda: nc.scalar.activation(out=sq[:], in_=bq[:], func=AF.Square), R=[bq], W=[sq])
                    b2 = abanks.next()
                    op("pe", lambda: nc.tensor.matmul(b2[:], lhsT=blockones[:], rhs=sq[:], start=True, stop=True), R=[blockones, sq], W=[b2])
                    rt = f32w.next()
                    op("act", lambda: nc.scalar.activation(out=rt[:], in_=b2[:], func=AF.Ln, bias=eps_t[:], scale=1.0 / 64), R=[b2, eps_t], W=[rt])
                    op("act", lambda: nc.scalar.activation(out=rt[:], in_=rt[:], func=AF.Exp, scale=-0.5), W=[rt])
                    qn = bfw.next()
                    op("dve", lambda: nc.vector.scalar_tensor_tensor(out=qn[:], in0=bq[:], scalar=gcol, in1=rt[:], op0=ALU.mult, op1=ALU.mult), R=[bq, rt, PRM], W=[qn])
                    for hh in range(2):
                        dstb, dstap = dstf(2 * j + hh)
                        dma("pool", dstap[0:64, cols], qn[64 * hh:64 * hh + 64, :], qn, dstb, disjoint=True)
            for j in range(2):
                bg = fm(B_G + 128 * j)
                gb = bfw.next()
                op("act", lambda: nc.scalar.activation(out=gb[:], in_=bg[:], func=AF.Silu), R=[bg], W=[gb])
                store(gB_d, 128 * j, 128, gb, gb[:])
            bf_ = fm(B_F, M=4)
            e_ = cpx.next()
            op("act", lambda: nc.scalar.activation(out=e_[:], in_=bf_[0:4, :], func=AF.Exp, bias=nfgb, scale=-1.0), R=[bf_, PRM], W=[e_])
            sp_ = cpx.next()
            op("act", lambda: nc.scalar.activation(out=sp_[:], in_=e_[:], func=AF.Ln, bias=one_t[0:4, :], scale=1.0), R=[e_, one_t], W=[sp_])
            cp = cp_r.next()
            init = 0.0 if prev_cp is None else prev_cp[:, 511:512]
            rr_ = [ones_f, sp_] + ([prev_cp] if prev_cp is not None else [])
            op("dve", lambda: nc.vector.tensor_tensor_scan(out=cp[:], data0=ones_f[0:4, :], data1=sp_[:], initial=init, op0=ALU.mult, op1=ALU.add), R=rr_, W=[cp])
            prev_cp = cp
            hi, mid, lo = cpb.next(), cpb.next(), cpb.next()
            r1, r2 = cpx.next(), e_
            op("dve", lambda: nc.vector.tensor_scalar(out=hi[:], in0=cp[:], scalar1=-1.0, scalar2=None, op0=ALU.mult), R=[cp], W=[hi])
            op("dve", lambda: nc.vector.scalar_tensor_tensor(out=r1[:], in0=cp[:], scalar=-1.0, in1=hi[:], op0=ALU.mult, op1=ALU.subtract), R=[cp, hi], W=[r1])
            op("dve", lambda: nc.vector.tensor_copy(out=mid[:], in_=r1[:]), R=[r1], W=[mid])
            op("dve", lambda: nc.vector.tensor_tensor(out=r2[:], in0=r1[:], in1=mid[:], op=ALU.subtract), R=[r1, mid], W=[r2])
            op("dve", lambda: nc.vector.tensor_copy(out=lo[:], in_=r2[:]), R=[r2], W=[lo])
            for i, bb_ in enumerate((hi, mid, lo)):
                dma("pool", qaug_d.t[:, 64 + i, cols], bb_[:], bb_, qaug_d, disjoint=True)
            def cpos_transposes(cp=cp, g=g):
                for t in range(4):
                    bk = abanks.next()
                    op("pe", lambda: nc.tensor.transpose(out=bk[:, 0:4], in_=cp[0:4, t * 128:(t + 1) * 128], identity=ident_f[0:4, 0:4]), R=[cp, ident_f], W=[bk])
                    op("act", lambda: nc.scalar.copy(out=cposT[:, g * 4 + t, :], in_=bk[:, 0:4]), R=[bk], Wd=[cposT])

            for j in range(2):
                bff = fm(C_F + 128 * j)
                sig = f32w.next()
                op("act", lambda: nc.scalar.activation(out=sig[:], in_=bff[:], func=AF.Sigmoid), R=[bff], W=[sig])
                gl = f32w.next()
                op("dve", lambda: nc.vector.tensor_scalar(out=gl[:], in0=sig[:], scalar1=oml_c(j), scalar2=lb_c(j), op0=ALU.mult, op1=ALU.add), R=[sig, OML, LB], W=[gl])
                op("act", lambda: nc.scalar.activation(out=gl[:], in_=gl[:], func=AF.Ln), W=[gl])
                kf = f32w.next()
                op("pool", lambda: nc.gpsimd.tensor_scalar(out=kf[:], in0=sig[:], scalar1=noml_c(j), scalar2=oml_c(j), op0=ALU.mult, op1=ALU.add), R=[sig, OML, NOML], W=[kf])
                b_ = f32w.next()
                op("dve", lambda: nc.vector.tensor_tensor_scan(out=b_[:], data0=resetm[:], data1=gl[:], initial=0.0, op0=ALU.mult, op1=ALU.add), R=[resetm, gl], W=[b_])
                b3 = b_[:].rearrange("p (c t) -> p c t", t=64)
                bm = f32w.next()
                bm3 = bm[:].rearrange("p (c t) -> p c t", t=64)
                op("dve", lambda: nc.vector.tensor_tensor(out=bm3, in0=b3, in1=b3[:, :, 32:33].to_broadcast([128, 8, 64]), op=ALU.subtract), R=[b_], W=[bm])
                e1 = f32w.next()
                op("act", lambda: nc.scalar.activation(out=e1[:], in_=bm[:], func=AF.Exp), R=[bm], W=[e1])
                e2 = gl
                op("act", lambda: nc.scalar.activation(out=e2[:], in_=bm[:], func=AF.Exp, scale=-1.0), R=[bm], W=[e2])
                for hh in range(2):
                    hd = 2 * j + hh
                    pr = slice(64 * hh, 64 * hh + 64)
                    ch = slice(g * 8, g * 8 + 8)
                    op("act", lambda: nc.scalar.activation(out=CV[:, 0, hd, ch], in_=b3[pr, :, 32], func=AF.Exp), R=[b_], Wd=[CV])
                    op("act", lambda: nc.scalar.activation(out=CV[:, 1, hd, ch], in_=b3[pr, :, 63], func=AF.Exp), R=[b_], Wd=[CV])
                    op("act", lambda: nc.scalar.activation(out=CV[:, 2, hd, ch], in_=bm3[pr, :, 63], func=AF.Exp), R=[bm], Wd=[CV])
                bqq = fm(C_Q + 128 * j)
                sq_ = sig
                op("act", lambda: nc.scalar.activation(out=sq_[:], in_=bqq[:], func=AF.Silu), R=[bqq], W=[sq_])
                Qp = bfw.next()
                op("dve", lambda: nc.vector.tensor_tensor(out=Qp[:], in0=sq_[:], in1=e1[:], op=ALU.mult), R=[sq_, e1], W=[Qp])
                store(Qp_d, 128 * j, 128, Qp, Qp[:])
                Kp = bfw.next()
                op("pool", lambda: nc.gpsimd.tensor_tensor(out=Kp[:], in0=kf[:], in1=e2[:], op=ALU.mult), R=[kf, e2], W=[Kp])
                store(Kp_d, 128 * j, 128, Kp, Kp[:])
                bgc = fm(C_G + 128 * j)
                gc = bfw.next()
                op("act", lambda: nc.scalar.activation(out=gc[:], in_=bgc[:], func=AF.Silu), R=[bgc], W=[gc])
                store(gC_d, 128 * j, 128, gc, gc[:])

            ug = ug_r.next()
            for j in range(2):
                bu = fm(D_U + 128 * j)
                us = f32w.next()
                op("act", lambda: nc.scalar.copy(out=us[:], in_=bu[:]), R=[bu], W=[us])
                bgd = fm(D_G + 128 * j)
                gd = f32w.next()
                op("act", lambda: nc.scalar.activation(out=gd[:], in_=bgd[:], func=AF.Silu), R=[bgd], W=[gd])
                op("pool", lambda: nc.gpsimd.tensor_tensor(out=ug[:, j, :], in0=us[:], in1=gd[:], op=ALU.mult), R=[us, gd], Wd=[ug])
            ydb = ydb_r.next()
            vnbs = []
            for t in range(4):
                tsl = slice(t * 128, (t + 1) * 128)
                bv = abanks.next()
                for k in range(8):
                    op("pe", lambda k=k: nc.tensor.matmul(bv[:, 0:256], lhsT=hTg[:, k, tsl], rhs=WA[:, k, D_V:D_V + 256], start=(k == 0), stop=(k == 7)), R=[WB_A, WB_T, hTg], W=[bv], inc=(k == 7))
                sqv = tokw.next()
                op("act", lambda: nc.scalar.activation(out=sqv[:], in_=bv[:, 0:256], func=AF.Square), R=[bv], W=[sqv])
                stv = st1.next()
                op("dve", lambda: nc.vector.reduce_sum(out=stv[:, 0:4], in_=sqv[:].rearrange("p (h c) -> p h c", c=64), axis=AX.X), R=[sqv], W=[stv])
                op("act", lambda: nc.scalar.activation(out=stv[:, 0:4], in_=stv[:, 0:4], func=AF.Sqrt, bias=eps_t[:], scale=1.0 / 64), R=[eps_t], W=[stv])
                op("dve", lambda: nc.vector.reciprocal(out=stv[:, 0:4], in_=stv[:, 0:4]), W=[stv])
                vn = tokw.next()
                op("dve", lambda: nc.vector.tensor_tensor(out=vn[:].rearrange("p (h c) -> p h c", c=64), in0=bv[:, 0:256].rearrange("p (h c) -> p h c", c=64), in1=stv[:, 0:4].unsqueeze(2).to_broadcast([128, 4, 64]), op=ALU.mult), R=[bv, stv], W=[vn])
                vnb = vnb_r.next()
                op("dve", lambda: nc.vector.tensor_tensor(out=vnb[:], in0=vn[:], in1=SG[:], op=ALU.mult), R=[vn, SG], W=[vnb])
                vnbs.append(vnb)
            for t in range(4):
                tsl = slice(t * 128, (t + 1) * 128)
                tok = slice(g * 512 + t * 128, g * 512 + (t + 1) * 128)
                bv2 = abanks.next()
                for k in range(8):
                    op("pe", lambda k=k: nc.tensor.matmul(bv2[:, 0:256], lhsT=hTg[:, k, tsl], rhs=WA[:, k, B_V:B_V + 256], start=(k == 0), stop=(k == 7)), R=[WB_A, WB_T, hTg], W=[bv2], inc=(k == 7))
                vb = tokb.next()
                op("act", lambda: nc.scalar.copy(out=vb[:], in_=bv2[:, 0:256]), R=[bv2], W=[vb])
                dma("pool", vtok_d.t[tok, :], vb[:], vb, vtok_d, disjoint=True)
                bv3 = abanks.next()
                for c in range(2):
                    csl = slice(t * 128 + c * 64, t * 128 + (c + 1) * 64)
                    for k in range(8):
                        op("pe", lambda k=k, c=c, csl=csl: nc.tensor.matmul(bv3[0:64, c * 256:(c + 1) * 256], lhsT=hTg[:, k, csl], rhs=WA[:, k, C_I:C_I + 256], start=(k == 0), stop=(k == 7)), R=[WB_A, WB_T, hTg], W=[bv3], inc=(k == 7 and c == 1))
                vh = vhb.next()
                op("act", lambda: nc.scalar.copy(out=vh[:].rearrange("p c f -> p (c f)"), in_=bv3[0:64, :]), R=[bv3], W=[vh])
                dma("pool", vH_d.t[tok, :].rearrange("(c s) f -> s c f", s=64), vh[:], vh, vH_d, disjoint=True)
            for t in range(4):
                tsl = slice(t * 128, (t + 1) * 128)
                vnb = vnbs[t]
                bs = abanks.next()
                for gh in range(4):
                    hh, j = gh % 2, gh // 2
                    op("pe", lambda gh=gh, hh=hh, j=j: nc.tensor.matmul(bs[64 * hh:64 * hh + 64, j * 128:(j + 1) * 128], lhsT=vnb[:, 64 * gh:64 * gh + 64], rhs=WT[:, gh, :], start=True, stop=True), R=[vnb, WT], W=[bs], inc=(gh == 3))
                ts_ = tokw.next()
                op("dve", lambda: nc.vector.tensor_tensor(out=ts_[:], in0=bs[:, 0:256], in1=BT[:].rearrange("p j t -> p (j t)"), op=ALU.add), R=[bs, BT], W=[ts_])
                op("dve", lambda: nc.vector.tensor_tensor(out=ydb[:, :, tsl], in0=ts_[:].rearrange("p (j t) -> p j t", j=2), in1=ug[:, :, tsl], op=ALU.mult), R=[ts_, ug], Wd=[ydb])
            for j in range(2):
                store(y_d[3], 128 * j, 128, ydb, ydb[:, j, :])
            cpos_transposes()

        dma("pool", xs2.t[:, 0:128], cposT[:].rearrange("p t h -> p (t h)"), cposT, xs2, disjoint=True, sem_owner=flag_t)
        ph.close()
        C.collective(xs1a, xg1a)
        C.collective(xs1b, xg1b)
        C.collective(xs1c, xg1c)
        C.collective(xs2, xg2)
        def phase_H(final):
            ph = Phase(C)
            Sst = C.sb([64, 4, 64], F32, dma=True)
            Sin = C.sb([64, 4, 64], F32, dma=True)
            Sbf_r = Rot([C.sb([64, 4, 64], BF16) for _ in range(10)])
            Qg_r = Rot([C.sb([64, 4, 512], BF16, dma=True) for _ in range(2)])
            Kg_r = Rot([C.sb([64, 4, 512], BF16, dma=True) for _ in range(2)])
            Gg_r = Rot([C.sb([64, 4, 512], BF16, dma=True) for _ in range(2)])
            Vg_r = Rot([C.sb([64, 8, 256], BF16, dma=True) for _ in range(2)])
            Kt_r = Rot([C.sb([64, 4, 64], BF16) for _ in range(8)])
            UT_r = Rot([C.sb([64, 8, 256], F32, dma=True) for _ in range(2)])
            ST_r = Rot([C.sb([64, 8, 64], BF16) for _ in range(2)])
            hq_r = Rot([C.sb([64, 512], BF16) for _ in range(2)])
            hr_r = Rot([C.sb([64, 512], F32) for _ in range(2)])
            hy_r = Rot([C.sb([64, 512], F32) for _ in range(2)])
            hyb_r = Rot([C.sb([64, 512], BF16, dma=True) for _ in range(2)])
            def scan_group(g, final):
                cols = slice(g * 512, (g + 1) * 512)
                Kg, Vg = Kg_r.next(), Vg_r.next()
                dma("sp", Kg[:], Kp_d.t[:, cols].rearrange("(h p) t -> p h t", p=64), Kp_d, Kg)
                dma("sp", Vg[:], vH_d.t[cols, :].rearrange("(c s) f -> s c f", s=64), vH_d, Vg)
                UT = UT_r.next()
                Sbfs = []
                if not final:
                    Kts = []
                    for c in range(8):
                        csl = slice(c * 64, (c + 1) * 64)
                        tb = tbanks.next()
                        for h in range(4):
                            op("pe", lambda h=h: nc.tensor.transpose(out=tb[0:64, h * 64:(h + 1) * 64], in_=Kg[:, h, csl], identity=ident_bf[0:64, 0:64]), R=[Kg, ident_bf], W=[tb], inc=(h == 3))
                        Kt = Kt_r.next()
                        op("act", lambda: nc.scalar.copy(out=Kt[:].rearrange("p h k -> p (h k)"), in_=tb[0:64, 0:256]), R=[tb], W=[Kt])
                        Kts.append(Kt)
                    for c in range(8):
                        chn = g * 8 + c
                        Kt = Kts[c]
                        UB = abanks.next()
                        for h in range(4):
                            op("pe", lambda h=h: nc.tensor.matmul(UB[0:64, h * 64:(h + 1) * 64], lhsT=Kt[:, h, :], rhs=Vg[:, c, 64 * h:64 * h + 64], start=True, stop=True), R=[Kt, Vg], W=[UB], inc=(h == 3))
                        op("dve", lambda: nc.vector.tensor_tensor(out=UT[:, c, :].rearrange("p (h v) -> p h v", h=4), in0=UB[0:64, 0:256].rearrange("p (h v) -> p h v", h=4), in1=CV[:, 2, :, chn:chn + 1].to_broadcast([64, 4, 64]), op=ALU.mult), R=[UB, CV], Wd=[UT])
                    dma("pool", ut_d.t[:, g * 8:(g + 1) * 8, :], UT[:], UT, ut_d, disjoint=True)
                else:
                    Qg, Gg = Qg_r.next(), Gg_r.next()
                    dma("sp", Qg[:], Qp_d.t[:, cols].rearrange("(h p) t -> p h t", p=64), Qp_d, Qg)
                    dma("sp", Gg[:], gC_d.t[:, cols].rearrange("(h p) t -> p h t", p=64), gC_d, Gg)
                    dma("sp", UT[:], ut_d.t[:, g * 8:(g + 1) * 8, :], ut_d, UT)
                for c in range(8):
                    chn = g * 8 + c
                    if final:
                        Sbf = Sbf_r.next()
                        op("pool", lambda: nc.gpsimd.tensor_tensor(out=Sbf[:], in0=Sst[:], in1=CV[:, 0, :, chn:chn + 1].to_broadcast([64, 4, 64]), op=ALU.mult), R=[Sst, CV], W=[Sbf])
                        Sbfs.append(Sbf)
                    op("dve", lambda: nc.vector.tensor_tensor(out=Sst[:], in0=Sst[:], in1=CV[:, 1, :, chn:chn + 1].to_broadcast([64, 4, 64]), op=ALU.mult), R=[CV], W=[Sst])
                    op("dve", lambda: nc.vector.tensor_tensor(out=Sst[:], in0=Sst[:], in1=UT[:, c, :].rearrange("p (h v) -> p h v", h=4), op=ALU.add), R=[UT], W=[Sst])
                if not final:
                    return
                for hp in range(2):
                    hs = (2 * hp, 2 * hp + 1)
                    SBk, STh, OBk, hq, b2, hr, hy, hyb = {}, {}, {}, {}, {}, {}, {}, {}
                    for h in hs:
                        SBk[h] = abanks.next()
                        for c in range(8):
                            csl = slice(c * 64, (c + 1) * 64)
                            op("pe", lambda c=c, csl=csl: nc.tensor.matmul(SBk[h][0:64, csl], lhsT=Kg[:, h, csl], rhs=Qg[:, h, csl], start=True, stop=True), R=[Kg, Qg], W=[SBk[h]], inc=(c == 7))
                    for h in hs:
                        STh[h] = ST_r.next()
                        op("dve", lambda: nc.vector.tensor_tensor(out=STh[h][:], in0=SBk[h][0:64, :].rearrange("p (c t) -> p c t", t=64), in1=tri[:].unsqueeze(1).to_broadcast([64, 8, 64]), op=ALU.mult), R=[SBk[h], tri], W=[STh[h]])
                    for h in hs:
                        OBk[h] = abanks.next()
                        for c in range(8):
                            csl = slice(c * 64, (c + 1) * 64)
                            op("pe", lambda c=c, csl=csl: nc.tensor.matmul(OBk[h][0:64, csl], lhsT=Sbfs[c][:, h, :], rhs=Qg[:, h, csl], start=True, stop=False), R=[Sbfs[c], Qg], W=[OBk[h]], inc=False)
                            op("pe", lambda c=c, csl=csl: nc.tensor.matmul(OBk[h][0:64, csl], lhsT=Vg[:, c, 64 * h:64 * h + 64], rhs=STh[h][:, c, :], start=False, stop=True), R=[Vg, STh[h]], W=[OBk[h]], inc=(c == 7))
                    for h in hs:
                        hq[h] = hq_r.next()
                        op("act", lambda: nc.scalar.activation(out=hq[h][:], in_=OBk[h][0:64, :], func=AF.Square), R=[OBk[h]], W=[hq[h]])
                    for h in hs:
                        b2[h] = abanks.next()
                        op("pe", lambda: nc.tensor.matmul(b2[h][0:64, :], lhsT=blockones[0:64, 0:64], rhs=hq[h][:], start=True, stop=True), R=[blockones, hq[h]], W=[b2[h]])
                    for h in hs:
                        hr[h] = hr_r.next()
                        op("act", lambda: nc.scalar.activation(out=hr[h][:], in_=b2[h][0:64, :], func=AF.Ln, bias=eps_t[0:64, :], scale=1.0 / 64), R=[b2[h], eps_t], W=[hr[h]])
                        op("act", lambda: nc.scalar.activation(out=hr[h][:], in_=hr[h][:], func=AF.Exp, scale=-0.5), W=[hr[h]])
                    for h in hs:
                        hy[h] = hy_r.next()
                        op("dve", lambda: nc.vector.scalar_tensor_tensor(out=hy[h][:], in0=OBk[h][0:64, :], scalar=hgain(h), in1=hr[h][:], op0=ALU.mult, op1=ALU.mult), R=[OBk[h], hr[h], PRM], W=[hy[h]])
                        hyb[h] = hyb_r.next()
                        op("pool", lambda: nc.gpsimd.tensor_tensor(out=hyb[h][:], in0=hy[h][:], in1=Gg[:, h, :], op=ALU.mult), R=[hy[h], Gg], W=[hyb[h]])
                        dma("pool", y_d[2].t[64 * h:64 * h + 64, cols], hyb[h][:], hyb[h], y_d[2], disjoint=True)

            if not final:
                op("dve", lambda: nc.vector.memset(Sst[:], 0.0), W=[Sst])
                for g in range(NG):
                    scan_group(g, False)
                dma("pool", xs3.t[:, :], Sst[:].rearrange("p h v -> p (h v)"), Sst, xs3)
                C.collective(xs3, xg3)
            else:
                dma("sp", Sin[:].rearrange("p h v -> p (h v)"), xg3.t[0:64, :], xg3, Sin)
                op("dve", lambda: nc.vector.tensor_scalar(out=Sst[:], in0=Sin[:], scalar1=flag_t[0:64, :], scalar2=None, op0=ALU.mult), R=[Sin, flag_t], W=[Sst])
                for g in range(NG):
                    scan_group(g, True)
            ph.close()
        phase_H(False)
        ph = Phase(C)
        R2 = C.sb([128, 132], F32, dma=True)
        Rtot = C.sb([128, 4], F32, dma=True)
        dma("sp", R2[:], xg2.t[0:128, :], xg2, R2)
        dma("sp", Rtot[:], xg2.t[127:128, 4 * NT - 4:4 * NT].partition_broadcast(128), xg2, Rtot)
        op("dve", lambda: nc.vector.tensor_tensor(out=biasP[:], in0=R2[:, 0:128].rearrange("p (t h) -> p t h", h=4), in1=Rtot[:].unsqueeze(1).to_broadcast([128, NT, 4]), op=ALU.subtract), R=[R2, Rtot], W=[biasP])
        op("dve", lambda: nc.vector.tensor_scalar(out=biasP[:], in0=biasP[:], scalar1=negbig[:], scalar2=None, op0=ALU.add), R=[negbig], W=[biasP])
        Hh = C.sb([128, 4], F32)
        op("dve", lambda: nc.vector.tensor_scalar(out=Hh[:], in0=R2[:, 128:132], scalar1=flag_t[:], scalar2=None, op0=ALU.mult), R=[R2, flag_t], W=[Hh])
        for j in range(2):
            cc = C.sb([128, 4], F32)
            yfx = C.sb([128, 2], BF16, dma=True)
            op("dve", lambda: nc.vector.tensor_scalar(out=cc[:, 0:1], in0=Hh[:, 2 * j:2 * j + 1], scalar1=convw(j, 0), scalar2=None, op0=ALU.mult), R=[Hh, PRM], W=[cc])
            op("dve", lambda: nc.vector.scalar_tensor_tensor(out=cc[:, 2:3], in0=Hh[:, 2 * j + 1:2 * j + 2], scalar=convw(j, 1), in1=cc[:, 0:1], op0=ALU.mult, op1=ALU.add), R=[Hh, PRM], W=[cc])
            op("dve", lambda: nc.vector.tensor_scalar(out=cc[:, 3:4], in0=Hh[:, 2 * j + 1:2 * j + 2], scalar1=convw(j, 0), scalar2=None, op0=ALU.mult), R=[Hh, PRM], W=[cc])
            op("dve", lambda: nc.vector.tensor_tensor(out=cc[:, 0:2], in0=cc[:, 2:4], in1=GF[:, j, :], op=ALU.mult), R=[GF], W=[cc])
            op("dve", lambda: nc.vector.tensor_tensor(out=yfx[:], in0=cc[:, 0:2], in1=YA0[:, j, :], op=ALU.add), R=[cc, YA0], W=[yfx])
            dma("pool", y_d[0].t[128 * j:128 * j + 128, 0:2], yfx[:], yfx, y_d[0], disjoint=True)
        ph.close()
        WG = WBIG[:, 0:8 * 2048].rearrange("p (k c) -> p k c", k=8)
        WU = WBIG[:, 8 * 2048:8 * 2048 + 8 * 512].rearrange("p (k c) -> p k c", k=8)

        def load_c1(hf, engs, defer=None, WGd=None, WUd=None, wb=None):
            WGd = WG if WGd is None else WGd
            WUd = WU if WUd is None else WUd
            wb = [WB_A] if wb is None else wb
            for br in range(4):
                load_w(lambda k, c0, n, br=br: WGd[:, k, br * 512 + c0:br * 512 + c0 + n],
                       lambda k, c0, n, br=br: (w_in, w_in.t[l, k * 128:(k + 1) * 128, M_G + br * 1024 + hf * 512 + c0:M_G + br * 1024 + hf * 512 + c0 + n]),
                       512, gmix, wb, engs=engs, defer=defer)
            load_w(lambda k, c0, n: WUd[:, k, c0:c0 + n],
                   lambda k, c0, n: (w_up, w_up.t[l, k // 2, (k % 2) * 128:(k % 2) * 128 + 128, hf * 512 + c0:hf * 512 + c0 + n]), 512, None, wb, engs=engs, defer=defer)

        pre_c1 = []
        load_c1(0, Rot(["dve", "pool"]), pre_c1)
        ph = Phase(C)
        Kaug = C.sb([128, S], BF16, dma=True)
        Vp = C.sb([128, S], BF16, dma=True)
        KaugP = C.sb([128, S], BF16, dma=True)
        VpP = C.sb([128, S], BF16, dma=True)
        VpP3 = VpP.t.rearrange("p (t f) -> p t f", f=128)
        op("pool", lambda: nc.gpsimd.memset(VpP3[:, :, 64:128], 1.0), Wd=[VpP])
        Vp3 = Vp.t.rearrange("p (t f) -> p t f", f=128)
        qa_r = Rot([C.sb([67, 512], BF16, dma=True) for _ in range(2)])
        gbt_r = Rot([C.sb([64, 512], BF16, dma=True) for _ in range(2)])
        P_r = Rot([C.sb([128, 512], BF16) for _ in range(4)])
        rd_r = Rot([C.sb([64, 512], F32) for _ in range(2)])
        of_r = Rot([C.sb([64, 512], F32) for _ in range(2)])
        yo_r = Rot([C.sb([64, 512], BF16, dma=True) for _ in range(2)])
        op("pool", lambda: nc.gpsimd.memset(Vp3[:, :, 64:128], 1.0), Wd=[Vp])
        for h in range(4):
            dma("sp", Kaug.t[0:67, :], kaug_h(h)[1], kaug_h(h)[0], Kaug)
            dma("sp", KaugP.t[0:67, :], kaugP_h(h)[1], kaugP_h(h)[0], KaugP)
            for c4 in range(S // 2048):
                dma("sp", Vp3[:, c4 * 16:(c4 + 1) * 16, 0:64], vtok_d.t[c4 * 2048:(c4 + 1) * 2048, 64 * h:64 * h + 64].rearrange("(t p) f -> p t f", p=128), vtok_d, Vp, disjoint=True)
                dma("sp", VpP3[:, c4 * 16:(c4 + 1) * 16, 0:64], vtokP_d.t[c4 * 2048:(c4 + 1) * 2048, 64 * h:64 * h + 64].rearrange("(t p) f -> p t f", p=128), vtokP_d, VpP, disjoint=True)
            pending = []

            def flush():
                while pending:
                    pending.pop(0)()

            for qg in range(NG):
                cols = slice(qg * 512, (qg + 1) * 512)
                qa = qa_r.next()
                dma("sp", qa[:], qaug_d.t[h, :, cols], qaug_d, qa)
                gbt = gbt_r.next()
                dma("sp", gbt[:], gB_d.t[64 * h:64 * h + 64, cols], gB_d, gbt)
                OB = obanks.next()
                nkt = NT + 4 * qg + 4
                for kt_all in range(nkt):
                    prevh = kt_all < NT
                    kt = kt_all if prevh else kt_all - NT
                    i = -1 if prevh else kt - 4 * qg
                    q0 = 0 if i < 0 else 128 * i
                    N = 512 - q0
                    SB = banks.next()
                    ksl = slice(kt * 128, (kt + 1) * 128)
                    Ksrc = KaugP if prevh else Kaug
                    Vsrc, Vsrc3 = (VpP, VpP3) if prevh else (Vp, Vp3)
                    bsrc = biasP if prevh else cposT
                    if i < 0:
                        op("pe", lambda: nc.tensor.matmul(SB[:, 0:N], lhsT=Ksrc.t[0:67, ksl], rhs=qa[:, q0:512], start=True, stop=True), R=[Ksrc, qa], W=[SB])
                    else:
                        op("pe", lambda: nc.tensor.matmul(SB[:, 0:N], lhsT=Kaug.t[0:67, ksl], rhs=qa[:, q0:512], start=True, stop=False), R=[Kaug, qa], W=[SB], inc=False)
                        op("pe", lambda: nc.tensor.matmul(SB[:, 0:128], lhsT=ident_bf[:], rhs=maskT[:], start=False, stop=True), R=[ident_bf, maskT], W=[SB])
                    while len(pending) >= 2:
                        pending.pop(0)()
                    last = (kt_all == nkt - 1)
                    first = (kt_all == 0)

                    def stage2(kt=kt, q0=q0, N=N, SB=SB, last=last, first=first, OB=OB, gbt=gbt, cols=cols, Vsrc=Vsrc, Vsrc3=Vsrc3, bsrc=bsrc):
                        Pt = P_r.next()
                        op("act", lambda: nc.scalar.activation(out=Pt[:, 0:N], in_=SB[:, 0:N], func=AF.Exp, bias=bsrc[:, kt, h:h + 1], scale=1.0), R=[SB, bsrc], W=[Pt])
                        op("pe", lambda: nc.tensor.matmul(OB[:, q0:512], lhsT=Vsrc3[:, kt, :], rhs=Pt[:, 0:N], start=first, stop=last), R=[Vsrc, Pt], W=[OB], inc=last)
                        if last:
                            rd = rd_r.next()
                            op("dve", lambda: nc.vector.reciprocal(out=rd[:], in_=OB[64:128, :]), R=[OB], W=[rd])
                            of = of_r.next()
                            op("dve", lambda: nc.vector.tensor_tensor(out=of[:], in0=OB[0:64, :], in1=rd[:], op=ALU.mult), R=[OB, rd], W=[of])
                            yo = yo_r.next()
                            op("pool", lambda: nc.gpsimd.tensor_tensor(out=yo[:], in0=of[:], in1=gbt[:], op=ALU.mult), R=[of, gbt], W=[yo])
                            dma("pool", y_d[1].t[64 * h:64 * h + 64, cols], yo[:], yo, y_d[1], disjoint=True)
                    pending.append(stage2)
                drain(pre_c1, 2)
            flush()

        ph.close()
        phase_H(True)
        WO = WBIG[:, 20480:20480 + 8 * 1024].rearrange("p (k c) -> p k c", k=8)
        WPG = WB2[:, 0:8 * 1024].rearrange("p (k c) -> p k c", k=8)
        WPP = WB2[:, 8 * 1024:10 * 1024].rearrange("p (k c) -> p k c", k=2)
        ph = Phase(C)
        WC1B = C.sb([128, 20480], BF16)
        WG1 = WC1B[:, 0:8 * 2048].rearrange("p (k c) -> p k c", k=8)
        WU1 = WC1B[:, 8 * 2048:8 * 2048 + 8 * 512].rearrange("p (k c) -> p k c", k=8)
        pre_c1b = []
        load_c1(1, Rot(["act", "dve"]), pre_c1b, WG1, WU1, [WC1B])
        hTg_r = Rot([C.sb([128, 8, 512], BF16, dma=True) for _ in range(2)])
        Yg_r = Rot([C.sb([128, 8, 512], BF16, dma=True) for _ in range(2)])
        mg_r = Rot([C.sb([128, 4, 512], BF16, dma=True) for _ in range(2)])
        acc_r = Rot([C.sb([128, 512], F32) for _ in range(3)])
        f32w = Rot([C.sb([128, 512], F32) for _ in range(6)])
        for hf in range(2):
            WGh, WUh, WBh = (WG, WU, WB_A) if hf == 0 else (WG1, WU1, WC1B)
            if hf == 1:
                drain(pre_c1b)
                pre_c2 = []
                load_w(lambda k, c0, n: WO[:, k, c0:c0 + n], lambda k, c0, n: (w_o, w_o.t[l, k * 128:(k + 1) * 128, c0:c0 + n]), 1024, None, [WB_T], engs=Rot(["act", "dve"]), defer=pre_c2)
                load_w(lambda k, c0, n: WPG[:, k, c0:c0 + n], lambda k, c0, n: (w_pg, w_pg.t[l, k * 128:(k + 1) * 128, c0:c0 + n]), 1024, gple, [WB2], engs=Rot(["act", "dve"]), defer=pre_c2)
                load_w(lambda k, c0, n: WPP[:, k, c0:c0 + n], lambda k, c0, n: (w_pp, w_pp.t[l, k * 128:(k + 1) * 128, c0:c0 + n]), 1024, None, [WB2], ks=range(2), engs=Rot(["act", "dve"]), defer=pre_c2)
            else:
                drain(pre_c1)
            for g in range(NG):
                cols = slice(g * 512, (g + 1) * 512)
                hTg = hTg_r.next()
                dma("sp", hTg[:], hT_d.t.rearrange("(k p) s -> p k s", p=128)[:, :, cols], hT_d, hTg)
                Yg = Yg_r.next()
                for br in range(4):
                    dma("sp", Yg[:, 2 * br:2 * br + 2, :], y_d[br].t[:, cols].rearrange("(k p) t -> p k t", p=128), y_d[br], Yg, disjoint=(br > 0))
                mg = mg_r.next()
                for db in range(4):
                    dsl = slice(db * 128, (db + 1) * 128)
                    acc = None
                    for br in range(4):
                        gbk = abanks.next()
                        for k in range(8):
                            op("pe", lambda k=k: nc.tensor.matmul(gbk[:], lhsT=WGh[:, k, br * 512 + db * 128:br * 512 + db * 128 + 128], rhs=hTg[:, k, :], start=(k == 0), stop=(k == 7)), R=[WBh, hTg], W=[gbk], inc=(k == 7))
                        sg_ = f32w.next()
                        op("act", lambda: nc.scalar.activation(out=sg_[:], in_=gbk[:], func=AF.Sigmoid, bias=mergeb(br, hf * 4 + db), scale=1.0), R=[gbk, PRM], W=[sg_])
                        ubk = abanks.next()
                        for kk in range(2):
                            op("pe", lambda kk=kk: nc.tensor.matmul(ubk[:], lhsT=WUh[:, 2 * br + kk, dsl], rhs=Yg[:, 2 * br + kk, :], start=(kk == 0), stop=(kk == 1)), R=[WBh, Yg], W=[ubk], inc=(kk == 1))
                        if br == 0:
                            acc = acc_r.next()
                            op("dve", lambda: nc.vector.tensor_tensor(out=acc[:], in0=sg_[:], in1=ubk[:], op=ALU.mult), R=[sg_, ubk], W=[acc])
                        else:
                            tt = f32w.next()
                            op("dve", lambda: nc.vector.tensor_tensor(out=tt[:], in0=sg_[:], in1=ubk[:], op=ALU.mult), R=[sg_, ubk], W=[tt])
                            if br == 1:
                                op("dve", lambda: nc.vector.tensor_tensor(out=acc[:], in0=acc[:], in1=tt[:], op=ALU.add), R=[tt], W=[acc])
                            elif br < 3:
                                op("pool", lambda: nc.gpsimd.tensor_tensor(out=acc[:], in0=acc[:], in1=tt[:], op=ALU.add), R=[tt], W=[acc])
                            else:
                                op("pool", lambda: nc.gpsimd.tensor_tensor(out=mg[:, db, :], in0=acc[:], in1=tt[:], op=ALU.add), R=[tt, acc], Wd=[mg])
                dma("pool", mT_d.t.rearrange("(k p) s -> p k s", p=128)[:, hf * 4:hf * 4 + 4, cols], mg[:], mg, mT_d, disjoint=True)
                if hf == 1:
                    drain(pre_c2, 3)
                else:
                    drain(pre_c1b, 5)
        drain(pre_c2)
        ph.close()

        ph = Phase(C)
        if l + 1 < n_layers:
            load_wa(l + 1, range(0, 5), defer=pre_a, engs=Rot(["act", "dve"]))
        mgl_r = Rot([C.sb([128, 8, 512], BF16, dma=True) for _ in range(2)])
        xin_r = Rot([C.sb([128, 1024], F32, dma=True) for _ in range(2)])
        xn_r = Rot([C.sb([128, 1024], F32) for _ in range(2)])
        xo_r = Rot([C.sb([128, 1024], F32, dma=True) for _ in range(2)])
        hb_r = Rot([C.sb([128, 1024], BF16) for _ in range(2)])
        junk = C.sb([128, 1024], BF16)
        st1 = Rot([C.sb([128, 4], F32) for _ in range(4)])
        hpT_r = Rot([C.sb([128, 8, 128], BF16) for _ in range(2)])
        pin_r = Rot([C.sb([128, 256], F32, dma=True) for _ in range(2)])
        pb_r = Rot([C.sb([128, 256], BF16) for _ in range(2)])
        pT_r = Rot([C.sb([128, 2, 128], BF16) for _ in range(2)])
        f32w = Rot([C.sb([128, 512], F32) for _ in range(6)])
        mgs = {}

        def c2_stage1(ti):
            g, t = ti // 4, ti % 4
            if t == 0:
                mgs[g] = mgl_r.next()
                dma("sp", mgs[g][:], mT_d.t.rearrange("(k p) s -> p k s", p=128)[:, :, g * 512:(g + 1) * 512], mT_d, mgs[g])
            mg = mgs[g]
            tsl = slice(t * 128, (t + 1) * 128)
            tok = slice(ti * 128, (ti + 1) * 128)
            xin = xin_r.next()
            dma("sp", xin[:], xsrc.t[tok, :], xsrc, xin)
            pin = pin_r.next()
            dma("sp", pin[:], p_in.t[l, tok, :], p_in, pin)
            xn = xn_r.next()
            for half in range(2):
                hs = slice(half * 512, (half + 1) * 512)
                obk = abanks.next()
                for db in range(8):
                    op("pe", lambda db=db: nc.tensor.matmul(obk[:], lhsT=mg[:, db, tsl], rhs=WO[:, db, hs], start=(db == 0), stop=(db == 7)), R=[WB_T, mg], W=[obk], inc=(db == 7))
                op("dve", lambda: nc.vector.tensor_tensor(out=xn[:, hs], in0=xin[:, hs], in1=obk[:], op=ALU.add), R=[xin, obk], Wd=[xn])
            pb = pb_r.next()
            op("pool", lambda: nc.gpsimd.tensor_copy(out=pb[:], in_=pin[:]), R=[pin], W=[pb])
            return xn, pb

        def c2_stage1b(xn):
            st = rms_rstd(xn[:], xn, 1024, junk, st1)
            hb = hb_r.next()
            op("dve", lambda: nc.vector.tensor_scalar(out=hb[:], in0=xn[:], scalar1=st[:, 2:3], scalar2=None, op0=ALU.mult), R=[xn, st], W=[hb])
            return hb

        xn0, pb0 = c2_stage1(0)
        nxt = (xn0, c2_stage1b(xn0), pb0)
        for ti in range(NT):
            tok = slice(ti * 128, (ti + 1) * 128)
            xn, hb, pb = nxt
            nx = c2_stage1(ti + 1) if ti + 1 < NT else None
            if True:
                tb = tbanks.next()
                for k in range(8):
                    op("pe", lambda k=k: nc.tensor.transpose(out=tb[:, k * 128:(k + 1) * 128], in_=hb[:, k * 128:(k + 1) * 128], identity=ident_bf[:]), R=[hb, ident_bf], W=[tb], inc=(k == 7))
                hpT = hpT_r.next()
                op("act", lambda: nc.scalar.copy(out=hpT[:], in_=tb[:].rearrange("p (k t) -> p k t", k=8)), R=[tb], W=[hpT])
                tb2 = tbanks.next()
                for k in range(2):
                    op("pe", lambda k=k: nc.tensor.transpose(out=tb2[:, k * 128:(k + 1) * 128], in_=pb[:, k * 128:(k + 1) * 128], identity=ident_bf[:]), R=[pb, ident_bf], W=[tb2], inc=(k == 1))
                pT = pT_r.next()
                op("act", lambda: nc.scalar.copy(out=pT[:], in_=tb2[:, 0:256].rearrange("p (k t) -> p k t", k=2)), R=[tb2], W=[pT])
                if nx is not None:
                    nxt = (nx[0], c2_stage1b(nx[0]), nx[1])
                xo = xo_r.next()
                for half in range(2):
                    hs = slice(half * 512, (half + 1) * 512)
                    gbk = abanks.next()
                    for k in range(8):
                        op("pe", lambda k=k: nc.tensor.matmul(gbk[:], lhsT=hpT[:, k, :], rhs=WPG[:, k, hs], start=(k == 0), stop=(k == 7)), R=[WB2, hpT], W=[gbk], inc=(k == 7))
                    sgp = f32w.next()
                    op("act", lambda: nc.scalar.activation(out=sgp[:], in_=gbk[:], func=AF.Sigmoid), R=[gbk], W=[sgp])
                    pbk = abanks.next()
                    for k in range(2):
                        op("pe", lambda k=k: nc.tensor.matmul(pbk[:], lhsT=pT[:, k, :], rhs=WPP[:, k, hs], start=(k == 0), stop=(k == 1)), R=[WB2, pT], W=[pbk], inc=(k == 1))
                    tt = f32w.next()
                    op("dve", lambda: nc.vector.tensor_tensor(out=tt[:], in0=sgp[:], in1=pbk[:], op=ALU.mult), R=[sgp, pbk], W=[tt])
                    if half == 0:
                        op("dve", lambda: nc.vector.tensor_tensor(out=xo[:, hs], in0=xn[:, hs], in1=tt[:], op=ALU.add), R=[xn, tt], Wd=[xo])
                    else:
                        op("pool", lambda: nc.gpsimd.tensor_tensor(out=xo[:, hs], in0=xn[:, hs], in1=tt[:], op=ALU.add), R=[xn, tt], Wd=[xo])
                dma("pool", xdst.t[tok, :], xo[:], xo, xdst, disjoint=True)
            if ti % 4 == 3:
                drain(pre_a, 3)
        ph.close()

    C.wait_all("pool", [y_out, xres])
    C.wait_all("sp", [y_out])


_NC_CACHE = {}


def kernel(**inputs):
    n_cores = 8
    if "nc" not in _NC_CACHE:
        _NC_CACHE["nc"] = build_nc(DEPTH)
    nc = _NC_CACHE["nc"]
    names = ["norm_mix", "w_in", "conv_w", "conv_b", "fgate_bias", "q_norm", "k_norm", "lb_logits", "hgrn_norm",
             "sgu_norm", "spatial_w", "spatial_b", "w_up", "merge_b", "w_o", "norm_ple", "w_ple_gate", "w_ple_proj"]
    shared = {n: np.ascontiguousarray(np.asarray(inputs[n], dtype=np.float32)) for n in names}
    x = np.asarray(inputs["x"], dtype=np.float32)
    p = np.asarray(inputs["p"], dtype=np.float32)
    in_maps = []
    for c in range(n_cores):
        b, hf = c // 2, c % 2
        m = dict(shared)
        m["x"] = np.ascontiguousarray(x[b, hf * S:(hf + 1) * S])
        m["p"] = np.ascontiguousarray(p[:, b, hf * S:(hf + 1) * S])
        m["flag"] = np.full((128, 1), float(hf), dtype=np.float32)
        in_maps.append(m)
    res = run_bass_kernel_spmd(nc, in_maps, core_ids=list(range(n_cores)))
    out = np.empty((4, 2 * S, D), dtype=np.float32)
    for c in range(n_cores):
        out[c // 2, (c % 2) * S:(c % 2 + 1) * S] = res.results[c]["y"]
    return out
```

```python
import numpy as np
from contextlib import ExitStack
import concourse.bass as bass
import concourse.mybir as mybir
from concourse.bass_utils import run_bass_kernel_spmd

F32 = mybir.dt.float32
BF16 = mybir.dt.bfloat16
AF = mybir.ActivationFunctionType
ALU = mybir.AluOpType
AX = mybir.AxisListType

D = 1024
S = 4096
RG = [[0, 1], [2, 3], [4, 5], [6, 7]]
DEPTH = 4
W = 256
NG = S // 512
NT = S // 128
NCH = S // 64
INC = 7940
EPS = 1e-6
A_X, A_B, A_C, A_G = 0, 256, 512, 768
B_Q, B_K, B_V, B_G, B_F = 1024, 1280, 1536, 1792, 2048
C_Q, C_F, C_I, C_G = 2052, 2308, 2564, 2820
D_U, D_V, D_G = 3076, 3332, 3588
M_G = 3844
NA = 3844


class TK:
    __slots__ = ("w", "r")

    def __init__(self):
        self.w = {}
        self.r = {}


class Sem:
    def __init__(self, h):
        self.h = h
        self.cnt = 0


class Buf:
    def __init__(self, t, sem=None):
        self.t = t
        self.tk = TK()
        self.sem = sem

    def __getitem__(self, k):
        return self.t[k]


class Ctx:
    def __init__(self, nc, es):
        self.nc = nc
        self.es = es
        self.E = {"pe": nc.tensor, "act": nc.scalar, "dve": nc.vector, "pool": nc.gpsimd, "sp": nc.sync}
        self.sem = {e: es.enter_context(nc.semaphore("s_" + e)) for e in ("pe", "act", "dve", "pool")}
        self.cnt = {e: 0 for e in self.sem}
        self.seen = {e: {} for e in self.E}
        self.semobj = dict(self.sem)
        self.nbuf = 0
        self.sems = [Sem(es.enter_context(nc.semaphore(f"d{i}"))) for i in range(64)]
        for sm in self.sems:
            self.semobj[id(sm)] = sm.h
        self.semi = 0
        self.pes = es
        self.ccsem = self.sems.pop()

    def sb(self, shape, dt, dma=False, name=None):
        self.nbuf += 1
        t = self.pes.enter_context(self.nc.sbuf_tensor(name or f"b{self.nbuf}", list(shape), dt))
        b = Buf(t)
        if dma:
            self.add_sem(b)
        return b

    def add_sem(self, b):
        b.sem = self.sems[self.semi]
        self.semi += 1
        return b

    def barrier(self):
        allk = {e: self.cnt[e] for e in self.sem}
        for sm in self.sems + [self.ccsem]:
            allk[id(sm)] = sm.cnt
        for e in self.E:
            self._wait(e, allk)

    def ps(self, shape, dt):
        self.nbuf += 1
        t = self.es.enter_context(self.nc.psum_tensor(f"p{self.nbuf}", list(shape), dt))
        return Buf(t)

    def _wait(self, e, deps):
        eng = self.E[e]
        seen = self.seen[e]
        for key, val in deps.items():
            if e == "pe" and key == "pe":
                continue
            if seen.get(key, 0) >= val:
                continue
            eng.wait_ge(self.semobj[key], val)
            seen[key] = val

    @staticmethod
    def _merge(d, src):
        for k, v in src.items():
            if d.get(k, 0) < v:
                d[k] = v

    def op(self, e, fn, R=(), W=(), Wd=(), inc=True):
        deps = {}
        for b in R:
            self._merge(deps, b.tk.w)
        for b in W:
            self._merge(deps, b.tk.w)
            self._merge(deps, b.tk.r)
        for b in Wd:
            self._merge(deps, b.tk.r)
            self._merge(deps, {k: v for k, v in b.tk.w.items() if k != e})
        self._wait(e, deps)
        ins = fn()
        if inc:
            self.cnt[e] += 1
            ins.then_inc(self.sem[e], 1)
            tick = self.cnt[e]
        else:
            tick = self.cnt[e] + 1
        for b in R:
            if b.tk.r.get(e, 0) < tick:
                b.tk.r[e] = tick
        for b in W:
            b.tk.w = {e: tick}
            b.tk.r = {}
        for b in Wd:
            b.tk.w[e] = tick
        return ins

    def dma(self, q, out, in_, src, dst, disjoint=False, sem_owner=None, **kw):
        owner = (sem_owner or (dst if dst.sem is not None else src)).sem
        key = id(owner)
        deps = {}
        self._merge(deps, src.tk.w)
        self._merge(deps, dst.tk.r)
        if disjoint:
            self._merge(deps, {k: v for k, v in dst.tk.w.items() if k != key})
        else:
            self._merge(deps, dst.tk.w)
        self._wait(q, deps)
        owner.cnt += 16
        self.E[q].dma_start(out=out, in_=in_, **kw).then_inc(owner.h, 16)
        if src.tk.r.get(key, 0) < owner.cnt:
            src.tk.r[key] = owner.cnt
        if disjoint:
            dst.tk.w[key] = owner.cnt
        else:
            dst.tk.w = {key: owner.cnt}
            dst.tk.r = {}

    def collective(self, src, dst):
        deps = {}
        self._merge(deps, src.tk.w)
        self._merge(deps, dst.tk.r)
        self._merge(deps, dst.tk.w)
        self._wait("pool", deps)
        sm = self.ccsem
        sm.cnt += 1
        self.nc.gpsimd.collective_compute("AllGather", ALU.bypass, replica_groups=RG,
                                          ins=[src.t.opt()], outs=[dst.t.opt()]).then_inc(sm.h, 1)
        key = id(sm)
        src.tk.r[key] = sm.cnt
        dst.tk.w = {key: sm.cnt}
        dst.tk.r = {}

    def wait_all(self, q, bufs):
        deps = {}
        for b in bufs:
            self._merge(deps, b.tk.w)
            self._merge(deps, b.tk.r)
        self._wait(q, deps)


class Phase:
    def __init__(self, C):
        self.C = C
        self.es = ExitStack()
        self.es.__enter__()
        C.pes = self.es
        self.semi0 = C.semi

    def close(self):
        C = self.C
        C.barrier()
        C.pes = C.es
        C.semi = self.semi0
        self.es.__exit__(None, None, None)


class Rot:
    def __init__(self, items):
        self.items = items
        self.i = 0

    def next(self):
        b = self.items[self.i % len(self.items)]
        self.i += 1
        return b


def build_nc(n_layers=DEPTH):
    nc = bass.Bass("TRN2", target_bir_lowering=False)
    with ExitStack() as es:
        es.enter_context(nc.allow_non_contiguous_dma("small strided parameter loads"))
        _build(nc, es, n_layers)
    return nc


def _build(nc, es, n_layers):
    C = Ctx(nc, es)
    op, dma = C.op, C.dma

    def dram_in(name, shape):
        return Buf(nc.dram_tensor(name, list(shape), F32, kind="ExternalInput").ap())

    x_in = dram_in("x", [S, D])
    p_in = dram_in("p", [DEPTH, S, W])
    norm_mix = dram_in("norm_mix", [DEPTH, D])
    w_in = dram_in("w_in", [DEPTH, D, INC])
    conv_w = dram_in("conv_w", [DEPTH, 3, W])
    conv_b = dram_in("conv_b", [DEPTH, W])
    fgate_bias = dram_in("fgate_bias", [DEPTH, 4])
    q_norm = dram_in("q_norm", [DEPTH, 64])
    k_norm = dram_in("k_norm", [DEPTH, 64])
    lb_logits = dram_in("lb_logits", [DEPTH, W])
    hgrn_norm = dram_in("hgrn_norm", [DEPTH, W])
    sgu_norm = dram_in("sgu_norm", [DEPTH, W])
    spatial_w = dram_in("spatial_w", [DEPTH, 4, 128, 128])
    spatial_b = dram_in("spatial_b", [DEPTH, 4, 128])
    w_up = dram_in("w_up", [DEPTH, 4, W, D])
    merge_b = dram_in("merge_b", [DEPTH, 4, D])
    w_o = dram_in("w_o", [DEPTH, D, D])
    norm_ple = dram_in("norm_ple", [DEPTH, D])
    w_pg = dram_in("w_ple_gate", [DEPTH, D, D])
    w_pp = dram_in("w_ple_proj", [DEPTH, W, D])
    y_out = Buf(nc.dram_tensor("y", [S, D], F32, kind="ExternalOutput").ap())

    def scratch(name, shape, dt):
        return Buf(nc.dram_tensor(name, list(shape), dt).ap())

    xres = scratch("xres", [S, D], F32)
    hT_d = scratch("hT_d", [D, S], BF16)
    y_d = [scratch(f"ybr{i}", [W, S], BF16) for i in range(4)]
    qaug_d = scratch("qaug", [4, 67, S], BF16)
    xs1a = scratch("xs1a", [134, S], BF16); xg1a = scratch("xg1a", [268, S], BF16)
    xs1b = scratch("xs1b", [134, S], BF16); xg1b = scratch("xg1b", [268, S], BF16)
    xs1c = scratch("xs1c", [256, S], BF16); xg1c = scratch("xg1c", [512, S], BF16)
    xs2 = scratch("xs2", [128, 132], F32)
    xg2 = scratch("xg2", [256, 132], F32)
    xs3 = scratch("xs3", [64, 256], F32)
    xg3 = scratch("xg3", [128, 256], F32)
    def kaug_h(h):
        b = xs1a if h < 2 else xs1b
        return b, b.t[67 * (h % 2):67 * (h % 2) + 67, :]

    def kaugP_h(h):
        b = xg1a if h < 2 else xg1b
        return b, b.t[67 * (h % 2):67 * (h % 2) + 67, :]

    def qaug_h(h):
        return qaug_d, qaug_d.t[h]
    vtok_d = Buf(xs1c.t[:, :].rearrange("r (q f) -> (r q) f", f=256)); vtok_d.tk = xs1c.tk
    vtokP_d = Buf(xg1c.t[0:256, :].rearrange("r (q f) -> (r q) f", f=256)); vtokP_d.tk = xg1c.tk
    flag_in = dram_in("flag", [128, 1])
    gB_d = scratch("gB", [W, S], BF16)
    gC_d = scratch("gC", [W, S], BF16)
    Qp_d = scratch("Qp", [W, S], BF16)
    Kp_d = scratch("Kp", [W, S], BF16)
    vH_d = scratch("vH", [S, W], BF16)
    mT_d = scratch("mT", [D, S], BF16)
    ut_d = Buf(nc.dram_tensor("ut_d", [64, NCH * 256], F32).ap().rearrange("p (c f) -> p c f", f=256))

    ident_bf = C.sb([128, 128], BF16)
    ident_f = C.sb([128, 128], F32)
    blockones = C.sb([128, 128], BF16)
    tri = C.sb([64, 64], F32)
    maskT = C.sb([128, 128], BF16)
    resetm = C.sb([128, 512], F32)
    ones_f = C.sb([128, 512], F32)
    onesb = C.sb([4, 512], BF16, dma=True)
    eps_t = C.sb([128, 1], F32)
    one_t = C.sb([128, 1], F32)

    pool = nc.gpsimd
    op("pool", lambda: pool.memset(ident_bf[:], 0.0), W=[ident_bf])
    op("pool", lambda: pool.affine_select(out=ident_bf[:], in_=ident_bf[:], pattern=[[-1, 128]], compare_op=ALU.not_equal, fill=1.0, base=0, channel_multiplier=1), W=[ident_bf])
    op("pool", lambda: pool.memset(ident_f[:], 0.0), W=[ident_f])
    op("pool", lambda: pool.affine_select(out=ident_f[:], in_=ident_f[:], pattern=[[-1, 128]], compare_op=ALU.not_equal, fill=1.0, base=0, channel_multiplier=1), W=[ident_f])
    op("pool", lambda: pool.memset(blockones[:], 0.0), W=[blockones])
    op("pool", lambda: pool.memset(blockones[0:64, 0:64], 1.0), W=[blockones])
    op("pool", lambda: pool.memset(blockones[64:128, 64:128], 1.0), W=[blockones])
    op("pool", lambda: pool.memset(tri[:], 1.0), W=[tri])
    op("pool", lambda: pool.affine_select(out=tri[:], in_=tri[:], pattern=[[1, 64]], compare_op=ALU.is_ge, fill=0.0, base=0, channel_multiplier=-1), W=[tri])
    op("pool", lambda: pool.memset(maskT[:], 0.0), W=[maskT])
    op("pool", lambda: pool.affine_select(out=maskT[:], in_=maskT[:], pattern=[[1, 128]], compare_op=ALU.is_ge, fill=-30000.0, base=0, channel_multiplier=-1), W=[maskT])
    op("pool", lambda: pool.memset(resetm[:], 1.0), W=[resetm])
    op("pool", lambda: pool.memset(resetm[:].rearrange("p (c t) -> p c t", t=64)[:, :, 0:1], 0.0), W=[resetm])
    op("pool", lambda: pool.memset(ones_f[:], 1.0), W=[ones_f])
    op("pool", lambda: pool.memset(onesb[:], 1.0), W=[onesb])
    op("pool", lambda: pool.memset(eps_t[:], EPS), W=[eps_t])
    op("pool", lambda: pool.memset(one_t[:], 1.0), W=[one_t])
    for h in range(4):
        for c4 in range(S // 512):
            dma("sp", kaug_h(h)[1][64:67, c4 * 512:(c4 + 1) * 512], onesb[0:3, :], onesb, kaug_h(h)[0], disjoint=True)

    flag_t = C.sb([128, 1], F32, dma=True)
    negbig = C.sb([128, 1], F32)
    dma("sp", flag_t[:], flag_in.t[:, :], flag_in, flag_t)
    op("dve", lambda: nc.vector.tensor_scalar(out=negbig[:], in0=flag_t[:], scalar1=-1.0, scalar2=30000.0, op0=ALU.add, op1=ALU.mult), R=[flag_t], W=[negbig])
    GF = C.sb([128, 2, 2], F32)
    YA0 = C.sb([128, 2, 2], F32)
    biasP = C.sb([128, NT, 4], F32)
    lbl = C.sb([128, 2, 4], F32, dma=True)
    for l4 in range(DEPTH):
        dma("sp", lbl[:, :, l4], lb_logits.t[l4].rearrange("(j p) -> p j", p=128), lb_logits, lbl, disjoint=(l4 > 0))
    lbe = C.sb([128, 2, 4], F32)
    lbs = C.sb([128, 2], F32)
    lbp = C.sb([128, 2, 4], F32)
    lbc = C.sb([128, 2, 4], F32)
    LB = C.sb([128, 2, 4], F32)
    OML = C.sb([128, 2, 4], F32)
    NOML = C.sb([128, 2, 4], F32)
    op("act", lambda: nc.scalar.activation(out=lbe[:], in_=lbl[:], func=AF.Exp), R=[lbl], W=[lbe])
    op("dve", lambda: nc.vector.reduce_sum(out=lbs[:], in_=lbe[:], axis=AX.X), R=[lbe], W=[lbs])
    op("dve", lambda: nc.vector.reciprocal(out=lbs[:], in_=lbs[:]), W=[lbs])
    op("dve", lambda: nc.vector.tensor_tensor(out=lbp[:], in0=lbe[:], in1=lbs[:].unsqueeze(2).to_broadcast([128, 2, 4]), op=ALU.mult), R=[lbe, lbs], W=[lbp])
    op("dve", lambda: nc.vector.memset(lbc[:, :, 0:1], 0.0), W=[lbc])
    op("dve", lambda: nc.vector.tensor_copy(out=lbc[:, :, 1:2], in_=lbp[:, :, 1:2]), R=[lbp], W=[lbc])
    op("dve", lambda: nc.vector.tensor_tensor(out=lbc[:, :, 2:3], in0=lbc[:, :, 1:2], in1=lbp[:, :, 2:3], op=ALU.add), R=[lbp], W=[lbc])
    op("dve", lambda: nc.vector.tensor_tensor(out=lbc[:, :, 3:4], in0=lbc[:, :, 2:3], in1=lbp[:, :, 3:4], op=ALU.add), R=[lbp], W=[lbc])
    op("dve", lambda: nc.vector.tensor_scalar(out=LB[:], in0=lbc[:], scalar1=0.0, scalar2=1.0, op0=ALU.max, op1=ALU.min), R=[lbc], W=[LB])
    op("dve", lambda: nc.vector.tensor_scalar(out=OML[:], in0=LB[:], scalar1=-1.0, scalar2=1.0, op0=ALU.mult, op1=ALU.add), R=[LB], W=[OML])
    op("dve", lambda: nc.vector.tensor_scalar(out=NOML[:], in0=LB[:], scalar1=-1.0, scalar2=None, op0=ALU.add), R=[LB], W=[NOML])

    WBIG = C.sb([128, 8 * NA], BF16, name="wbig")
    WB_A = Buf(WBIG.t)
    WB_T = Buf(WBIG.t)
    WB2 = C.sb([128, 10240], BF16, name="wb2")
    GMALL = C.sb([128, DEPTH, 16], F32, dma=True)
    for l4 in range(DEPTH):
        dma("sp", GMALL[:, l4, 0:8], norm_mix.t[l4].rearrange("(k p) -> p k", p=128), norm_mix, GMALL, disjoint=(l4 > 0))
        dma("sp", GMALL[:, l4, 8:16], norm_ple.t[l4].rearrange("(k p) -> p k", p=128), norm_ple, GMALL, disjoint=True)
    stg = Rot([C.sb([128, 1024], F32, dma=True) for _ in range(2)])
    banks = Rot([C.ps([128, 512], F32) for _ in range(4)])
    obanks = Rot([C.ps([128, 512], F32) for _ in range(2)])
    abanks = Rot(banks.items + obanks.items)
    tbanks = Rot([C.ps([128, 1024], BF16) for _ in range(2)])
    PRM = C.sb([128, 80], F32, dma=True)
    SG = C.sb([128, 256], F32, dma=True)
    WSP = C.sb([128, 4, 128], F32, dma=True)
    WT = C.sb([128, 4, 128], BF16)
    BT = C.sb([128, 2, 128], F32, dma=True)
    cposT = C.sb([128, NT, 4], F32)
    CV = C.sb([64, 3, 4, NCH], F32)

    cast_engs = Rot(["dve", "pool", "dve", "act"])

    def cast_mul(e, out, in_, sc):
        if e == "act":
            return nc.scalar.mul(out=out, in_=in_, mul=sc) if sc is not None else nc.scalar.copy(out=out, in_=in_)
        eng = nc.vector if e == "dve" else nc.gpsimd
        if sc is None:
            return eng.tensor_copy(out=out, in_=in_)
        return eng.tensor_scalar(out=out, in0=in_, scalar1=sc, scalar2=None, op0=ALU.mult)

    def load_w(dst_ap_fn, src_rows_fn, ncols, gain_col, wbufs, ks=range(8), engs=None, defer=None):
        for k in ks:
            c0 = 0
            while c0 < ncols:
                n = min(1024, ncols - c0)

                def emit(k=k, c0=c0, n=n):
                    st = stg.next()
                    srcb, srcap = src_rows_fn(k, c0, n)
                    dma("sp", st[:, 0:n], srcap, srcb, st)
                    e = cast_engs.next() if engs is None else engs.next()
                    g = gain_col(k) if gain_col is not None else None
                    d = dst_ap_fn(k, c0, n)
                    op(e, (lambda: cast_mul(e, d, st[:, 0:n], g)), R=[st, GMALL], Wd=list(wbufs))
                if defer is None:
                    emit()
                else:
                    defer.append(emit)
                c0 += n

    def drain(lst, n=None):
        cnt = 0
        while lst and (n is None or cnt < n):
            lst.pop(0)()
            cnt += 1

    def rms_rstd(src_ap, srcb, n, junk, st1):
        st = st1.next()
        op("act", lambda: nc.scalar.activation(out=junk[:, 0:n], in_=src_ap, func=AF.Square, accum_out=st[:, 0:1]), R=[srcb], W=[junk, st])
        op("act", lambda: nc.scalar.activation(out=st[:, 1:2], in_=st[:, 0:1], func=AF.Ln, bias=eps_t[:], scale=1.0 / n), R=[eps_t], W=[st])
        op("act", lambda: nc.scalar.activation(out=st[:, 2:3], in_=st[:, 1:2], func=AF.Exp, scale=-0.5), W=[st])
        return st

    for l in range(n_layers):
        xsrc = x_in if l == 0 else xres
        xdst = y_out if l == n_layers - 1 else xres
        dma("sp", PRM[:, 0:8], norm_mix.t[l].rearrange("(k p) -> p k", p=128), norm_mix, PRM)
        dma("sp", PRM[:, 8:16], norm_ple.t[l].rearrange("(k p) -> p k", p=128), norm_ple, PRM, disjoint=True)
        for tp in range(3):
            dma("sp", PRM[:, 16:22].rearrange("p (j t) -> p j t", t=3)[:, :, tp], conv_w.t[l, tp].rearrange("(j p) -> p j", p=128), conv_w, PRM, disjoint=True)
        dma("sp", PRM[:, 22:24], conv_b.t[l].rearrange("(j p) -> p j", p=128), conv_b, PRM, disjoint=True)
        for hh in range(2):
            dma("sp", PRM[64 * hh:64 * hh + 64, 24:25], q_norm.t[l].rearrange("(p o) -> p o", o=1), q_norm, PRM, disjoint=True)
            dma("sp", PRM[64 * hh:64 * hh + 64, 25:26], k_norm.t[l].rearrange("(p o) -> p o", o=1), k_norm, PRM, disjoint=True)
        dma("sp", PRM[0:64, 32:36], hgrn_norm.t[l].rearrange("(h p) -> p h", p=64), hgrn_norm, PRM, disjoint=True)
        for br_ in range(4):
            dma("sp", PRM[:, 36 + 8 * br_:44 + 8 * br_], merge_b.t[l, br_].rearrange("(j p) -> p j", p=128), merge_b, PRM, disjoint=True)
        dma("sp", PRM[0:4, 68:69], fgate_bias.t[l].rearrange("(p o) -> p o", o=1), fgate_bias, PRM, disjoint=True)
        op("dve", lambda: nc.vector.tensor_scalar(out=PRM[:, 26:27], in0=PRM[:, 24:25], scalar1=0.125, scalar2=None, op0=ALU.mult), R=[PRM], Wd=[PRM])
        op("dve", lambda: nc.vector.tensor_scalar(out=PRM[0:4, 69:70], in0=PRM[0:4, 68:69], scalar1=-1.0, scalar2=None, op0=ALU.mult), R=[PRM], Wd=[PRM])
        gmix = lambda k, l=l: GMALL[:, l, k:k + 1]
        gple = lambda k, l=l: GMALL[:, l, 8 + k:9 + k]
        convw = lambda j, t: PRM[:, 16 + 3 * j + t:17 + 3 * j + t]
        convb = lambda j: PRM[:, 22 + j:23 + j]
        gq8 = PRM[:, 26:27]
        gk = PRM[:, 25:26]
        hgain = lambda h: PRM[0:64, 32 + h:33 + h]
        mergeb = lambda b, j: PRM[:, 36 + 8 * b + j:37 + 8 * b + j]
        nfgb = PRM[0:4, 69:70]
        lb_c = lambda j: LB[:, j, l:l + 1]
        oml_c = lambda j: OML[:, j, l:l + 1]
        noml_c = lambda j: NOML[:, j, l:l + 1]
        dma("sp", SG[:], sgu_norm.t[l:l + 1, :].partition_broadcast(128), sgu_norm, SG)
        dma("sp", WSP[:], spatial_w.t[l].rearrange("g t s -> t g s"), spatial_w, WSP)
        for gh in range(4):
            hh, j = gh % 2, gh // 2
            dma("sp", BT[64 * hh:64 * hh + 64, j, :], spatial_b.t[l, gh:gh + 1, :].partition_broadcast(64), spatial_b, BT, disjoint=(gh > 0))
        for gh in range(4):
            bk = abanks.next()
            op("pe", lambda bk=bk, gh=gh: nc.tensor.transpose(out=bk[:, 0:128], in_=WSP[:, gh, :], identity=ident_f[:]), R=[WSP, ident_f], W=[bk])
            op("dve", lambda bk=bk, gh=gh: nc.vector.scalar_tensor_tensor(out=WT[:, gh, :], in0=maskT[:], scalar=-1.0, in1=bk[:, 0:128], op0=ALU.is_gt, op1=ALU.mult), R=[bk, maskT], Wd=[WT])

        WA = WBIG[:, 0:8 * NA].rearrange("p (k c) -> p k c", k=8)

        def load_wa(lw, ks, defer=None, engs=None):
            load_w(lambda k, c0, n: WA[:, k, c0:c0 + n],
                   lambda k, c0, n: (w_in, w_in.t[lw, k * 128:(k + 1) * 128, c0:c0 + n]),
                   NA, (lambda k: GMALL[:, lw, k:k + 1]), ([WB_A] if max(ks) < 5 else [WB_A, WB_T]), ks=ks, defer=defer, engs=engs)
        if l == 0:
            load_wa(0, range(0, 5))
        else:
            drain(pre_a)
        load_wa(l, range(5, 8))
        pre_a = []

        ph = Phase(C)
        xin_r = Rot([C.sb([128, 1024], F32, dma=True) for _ in range(2)])
        junk = C.sb([128, 1024], BF16)
        hb_r = Rot([C.sb([128, 1024], BF16) for _ in range(2)])
        st1 = Rot([C.sb([128, 4], F32) for _ in range(4)])
        hTg_r = Rot([C.sb([128, 8, 512], BF16, dma=True) for _ in range(2)])
        f32w = Rot([C.sb([128, 512], F32) for _ in range(8)])
        bfw = Rot([C.sb([128, 512], BF16, dma=True) for _ in range(8)])
        zc_r = [C.sb([128, 514], F32, dma=True) for _ in range(2)]
        cp_r = Rot([C.sb([4, 512], F32) for _ in range(2)])
        cpx = Rot([C.sb([4, 512], F32) for _ in range(3)])
        cpb = Rot([C.sb([4, 512], BF16, dma=True) for _ in range(6)])
        ug_r = Rot([C.sb([128, 2, 512], F32) for _ in range(2)])
        ydb_r = Rot([C.sb([128, 2, 512], BF16, dma=True) for _ in range(2)])
        tokw = Rot([C.sb([128, 256], F32) for _ in range(4)])
        tokb = Rot([C.sb([128, 256], BF16, dma=True) for _ in range(4)])
        vnb_r = Rot([C.sb([128, 256], BF16) for _ in range(4)])
        vhb = Rot([C.sb([64, 2, 256], BF16, dma=True) for _ in range(3)])

        for j in range(2):
            op("pool", lambda j=j: nc.gpsimd.memset(zc_r[j][:, 0:2], 0.0), W=[zc_r[j]])
        prev_cp = None

        def norm_tile(ti):
            tok = slice(ti * 128, (ti + 1) * 128)
            xin = xin_r.next()
            dma("sp", xin[:], xsrc.t[tok, :], xsrc, xin)
            st = rms_rstd(xin[:], xin, 1024, junk, st1)
            hb = hb_r.next()
            op("dve", lambda: nc.vector.tensor_scalar(out=hb[:], in0=xin[:], scalar1=st[:, 2:3], scalar2=None, op0=ALU.mult), R=[xin, st], W=[hb])
            return hb

        hb_next = norm_tile(0)
        for g in range(NG):
            cols = slice(g * 512, (g + 1) * 512)
            hTg = hTg_r.next()
            for t in range(4):
                hb = hb_next
                if g * 4 + t + 1 < NT:
                    hb_next = norm_tile(g * 4 + t + 1)
                tb = tbanks.next()
                for k in range(8):
                    op("pe", lambda k=k, tb=tb, hb=hb: nc.tensor.transpose(out=tb[:, k * 128:(k + 1) * 128], in_=hb[:, k * 128:(k + 1) * 128], identity=ident_bf[:]), R=[hb, ident_bf], W=[tb], inc=(k == 7))
                op("act", lambda tb=tb, hTg=hTg, t=t: nc.scalar.copy(out=hTg[:, :, t * 128:(t + 1) * 128], in_=tb[:].rearrange("p (k t) -> p k t", k=8)), R=[tb], Wd=[hTg])
            dma("pool", hT_d.t.rearrange("(k p) s -> p k s", p=128)[:, :, cols], hTg[:], hTg, hT_d, disjoint=True)

            def fm(c0, M=128):
                bk = abanks.next()
                for k in range(8):
                    op("pe", lambda k=k, bk=bk: nc.tensor.matmul(bk[0:M, :], lhsT=WA[:, k, c0:c0 + M], rhs=hTg[:, k, :], start=(k == 0), stop=(k == 7)), R=[WB_A, WB_T, hTg], W=[bk], inc=(k == 7))
                return bk

            def store(dst, row0, nrows, srcbuf, src_ap):
                dma("pool", dst.t[row0:row0 + nrows, cols], src_ap, srcbuf, dst, disjoint=True)

            for j in range(2):
                bx = fm(A_X + 128 * j)
                tmp = f32w.next()
                op("act", lambda: nc.scalar.copy(out=tmp[:], in_=bx[:]), R=[bx], W=[tmp])
                bc = fm(A_C + 128 * j)
                zc = zc_r[j]
                op("dve", lambda: nc.vector.tensor_tensor(out=zc[:, 2:514], in0=tmp[:], in1=bc[:], op=ALU.mult), R=[tmp, bc], W=[zc])
                a1 = f32w.next()
                a2 = f32w.next()
                op("dve", lambda: nc.vector.tensor_scalar(out=a1[:], in0=zc[:, 2:514], scalar1=convw(j, 2), scalar2=convb(j), op0=ALU.mult, op1=ALU.add), R=[zc, PRM], W=[a1])
                op("dve", lambda: nc.vector.scalar_tensor_tensor(out=a2[:], in0=zc[:, 1:513], scalar=convw(j, 1), in1=a1[:], op0=ALU.mult, op1=ALU.add), R=[zc, a1, PRM], W=[a2])
                op("dve", lambda: nc.vector.scalar_tensor_tensor(out=a1[:], in0=zc[:, 0:512], scalar=convw(j, 0), in1=a2[:], op0=ALU.mult, op1=ALU.add), R=[zc, a2, PRM], W=[a1])
                op("pool", lambda: nc.gpsimd.tensor_copy(out=zc[:, 0:2], in_=zc[:, 512:514]), W=[zc])
                bb = fm(A_B + 128 * j)
                op("dve", lambda: nc.vector.tensor_tensor(out=a2[:], in0=a1[:], in1=bb[:], op=ALU.mult), R=[a1, bb], W=[a2])
                bg = fm(A_G + 128 * j)
                sg_ = f32w.next()
                op("act", lambda: nc.scalar.activation(out=sg_[:], in_=bg[:], func=AF.Silu), R=[bg], W=[sg_])
                yb_ = bfw.next()
                op("dve", lambda: nc.vector.tensor_tensor(out=yb_[:], in0=a2[:], in1=sg_[:], op=ALU.mult), R=[a2, sg_], W=[yb_])
                if g == 0:
                    op("dve", lambda: nc.vector.tensor_tensor(out=GF[:, j, :], in0=bb[:, 0:2], in1=sg_[:, 0:2], op=ALU.mult), R=[bb, sg_], Wd=[GF])
                    op("dve", lambda: nc.vector.tensor_tensor(out=YA0[:, j, :], in0=a2[:, 0:2], in1=sg_[:, 0:2], op=ALU.mult), R=[a2, sg_], Wd=[YA0])
                if g == NG - 1:
                    dma("pool", xs2.t[:, 128 + 2 * j:130 + 2 * j], zc[:, 512:514], zc, xs2, disjoint=True)
                store(y_d[0], 128 * j, 128, yb_, yb_[:])

            for (c0, gcol, dstf) in ((B_Q, gq8, qaug_h), (B_K, gk, kaug_h)):
                for j in range(2):
                    bq = fm(c0 + 128 * j)
                    sq = bfw.next()
                    op("act", lambda: nc.scalar.activation(out=sq[:], in_=bq[:], func=AF.Square), R=[bq], W=[sq])
                    b2 = abanks.next()
                    op("pe", lambda: nc.tensor.matmul(b2[:], lhsT=blockones[:], rhs=sq[:], start=True, stop=True), R=[blockones, sq], W=[b2])
                    rt = f32w.next()
                    op("act", lambda: nc.scalar.activation(out=rt[:], in_=b2[:], func=AF.Ln, bias=eps_t[:], scale=1.0 / 64), R=[b2, eps_t], W=[rt])
                    op("act", lambda: nc.scalar.activation(out=rt[:], in_=rt[:], func=AF.Exp, scale=-0.5), W=[rt])
                    qn = bfw.next()
                    op("dve", lambda: nc.vector.scalar_tensor_tensor(out=qn[:], in0=bq[:], scalar=gcol, in1=rt[:], op0=ALU.mult, op1=ALU.mult), R=[bq, rt, PRM], W=[qn])
                    for hh in range(2):
                        dstb, dstap = dstf(2 * j + hh)
                        dma("pool", dstap[0:64, cols], qn[64 * hh:64 * hh + 64, :], qn, dstb, disjoint=True)
            for j in range(2):
                bg = fm(B_G + 128 * j)
                gb = bfw.next()
                op("act", lambda: nc.scalar.activation(out=gb[:], in_=bg[:], func=AF.Silu), R=[bg], W=[gb])
                store(gB_d, 128 * j, 128, gb, gb[:])
            bf_ = fm(B_F, M=4)
            e_ = cpx.next()
            op("act", lambda: nc.scalar.activation(out=e_[:], in_=bf_[0:4, :], func=AF.Exp, bias=nfgb, scale=-1.0), R=[bf_, PRM], W=[e_])
            sp_ = cpx.next()
            op("act", lambda: nc.scalar.activation(out=sp_[:], in_=e_[:], func=AF.Ln, bias=one_t[0:4, :], scale=1.0), R=[e_, one_t], W=[sp_])
            cp = cp_r.next()
            init = 0.0 if prev_cp is None else prev_cp[:, 511:512]
            rr_ = [ones_f, sp_] + ([prev_cp] if prev_cp is not None else [])
            op("dve", lambda: nc.vector.tensor_tensor_scan(out=cp[:], data0=ones_f[0:4, :], data1=sp_[:], initial=init, op0=ALU.mult, op1=ALU.add), R=rr_, W=[cp])
            prev_cp = cp
            hi, mid, lo = cpb.next(), cpb.next(), cpb.next()
            r1, r2 = cpx.next(), e_
            op("dve", lambda: nc.vector.tensor_scalar(out=hi[:], in0=cp[:], scalar1=-1.0, scalar2=None, op0=ALU.mult), R=[cp], W=[hi])
            op("dve", lambda: nc.vector.scalar_tensor_tensor(out=r1[:], in0=cp[:], scalar=-1.0, in1=hi[:], op0=ALU.mult, op1=ALU.subtract), R=[cp, hi], W=[r1])
            op("dve", lambda: nc.vector.tensor_copy(out=mid[:], in_=r1[:]), R=[r1], W=[mid])
            op("dve", lambda: nc.vector.tensor_tensor(out=r2[:], in0=r1[:], in1=mid[:], op=ALU.subtract), R=[r1, mid], W=[r2])
            op("dve", lambda: nc.vector.tensor_copy(out=lo[:], in_=r2[:]), R=[r2], W=[lo])
            for i, bb_ in enumerate((hi, mid, lo)):
                dma("pool", qaug_d.t[:, 64 + i, cols], bb_[:], bb_, qaug_d, disjoint=True)
            def cpos_transposes(cp=cp, g=g):
                for t in range(4):
                    bk = abanks.next()
                    op("pe", lambda: nc.tensor.transpose(out=bk[:, 0:4], in_=cp[0:4, t * 128:(t + 1) * 128], identity=ident_f[0:4, 0:4]), R=[cp, ident_f], W=[bk])
                    op("act", lambda: nc.scalar.copy(out=cposT[:, g * 4 + t, :], in_=bk[:, 0:4]), R=[bk], Wd=[cposT])

            for j in range(2):
                bff = fm(C_F + 128 * j)
                sig = f32w.next()
                op("act", lambda: nc.scalar.activation(out=sig[:], in_=bff[:], func=AF.Sigmoid), R=[bff], W=[sig])
                gl = f32w.next()
                op("dve", lambda: nc.vector.tensor_scalar(out=gl[:], in0=sig[:], scalar1=oml_c(j), scalar2=lb_c(j), op0=ALU.mult, op1=ALU.add), R=[sig, OML, LB], W=[gl])
                op("act", lambda: nc.scalar.activation(out=gl[:], in_=gl[:], func=AF.Ln), W=[gl])
                kf = f32w.next()
                op("pool", lambda: nc.gpsimd.tensor_scalar(out=kf[:], in0=sig[:], scalar1=noml_c(j), scalar2=oml_c(j), op0=ALU.mult, op1=ALU.add), R=[sig, OML, NOML], W=[kf])
                b_ = f32w.next()
                op("dve", lambda: nc.vector.tensor_tensor_scan(out=b_[:], data0=resetm[:], data1=gl[:], initial=0.0, op0=ALU.mult, op1=ALU.add), R=[resetm, gl], W=[b_])
                b3 = b_[:].rearrange("p (c t) -> p c t", t=64)
                bm = f32w.next()
                bm3 = bm[:].rearrange("p (c t) -> p c t", t=64)
                op("dve", lambda: nc.vector.tensor_tensor(out=bm3, in0=b3, in1=b3[:, :, 32:33].to_broadcast([128, 8, 64]), op=ALU.subtract), R=[b_], W=[bm])
                e1 = f32w.next()
                op("act", lambda: nc.scalar.activation(out=e1[:], in_=bm[:], func=AF.Exp), R=[bm], W=[e1])
                e2 = gl
                op("act", lambda: nc.scalar.activation(out=e2[:], in_=bm[:], func=AF.Exp, scale=-1.0), R=[bm], W=[e2])
                for hh in range(2):
                    hd = 2 * j + hh
                    pr = slice(64 * hh, 64 * hh + 64)
                    ch = slice(g * 8, g * 8 + 8)
                    op("act", lambda: nc.scalar.activation(out=CV[:, 0, hd, ch], in_=b3[pr, :, 32], func=AF.Exp), R=[b_], Wd=[CV])
                    op("act", lambda: nc.scalar.activation(out=CV[:, 1, hd, ch], in_=b3[pr, :, 63], func=AF.Exp), R=[b_], Wd=[CV])
                    op("act", lambda: nc.scalar.activation(out=CV[:, 2, hd, ch], in_=bm3[pr, :, 63], func=AF.Exp), R=[bm], Wd=[CV])
                bqq = fm(C_Q + 128 * j)
                sq_ = sig
                op("act", lambda: nc.scalar.activation(out=sq_[:], in_=bqq[:], func=AF.Silu), R=[bqq], W=[sq_])
                Qp = bfw.next()
                op("dve", lambda: nc.vector.tensor_tensor(out=Qp[:], in0=sq_[:], in1=e1[:], op=ALU.mult), R=[sq_, e1], W=[Qp])
                store(Qp_d, 128 * j, 128, Qp, Qp[:])
                Kp = bfw.next()
                op("pool", lambda: nc.gpsimd.tensor_tensor(out=Kp[:], in0=kf[:], in1=e2[:], op=ALU.mult), R=[kf, e2], W=[Kp])
                store(Kp_d, 128 * j, 128, Kp, Kp[:])
                bgc = fm(C_G + 128 * j)
                gc = bfw.next()
                op("act", lambda: nc.scalar.activation(out=gc[:], in_=bgc[:], func=AF.Silu), R=[bgc], W=[gc])
                store(gC_d, 128 * j, 128, gc, gc[:])

            ug = ug_r.next()
            for j in range(2):
                bu = fm(D_U + 128 * j)
                us = f32w.next()
                op("act", lambda: nc.scalar.copy(out=us[:], in_=bu[:]), R=[bu], W=[us])
                bgd = fm(D_G + 128 * j)
                gd = f32w.next()
                op("act", lambda: nc.scalar.activation(out=gd[:], in_=bgd[:], func=AF.Silu), R=[bgd], W=[gd])
                op("pool", lambda: nc.gpsimd.tensor_tensor(out=ug[:, j, :], in0=us[:], in1=gd[:], op=ALU.mult), R=[us, gd], Wd=[ug])
            ydb = ydb_r.next()
            vnbs = []
            for t in range(4):
                tsl = slice(t * 128, (t + 1) * 128)
                bv = abanks.next()
                for k in range(8):
                    op("pe", lambda k=k: nc.tensor.matmul(bv[:, 0:256], lhsT=hTg[:, k, tsl], rhs=WA[:, k, D_V:D_V + 256], start=(k == 0), stop=(k == 7)), R=[WB_A, WB_T, hTg], W=[bv], inc=(k == 7))
                sqv = tokw.next()
                op("act", lambda: nc.scalar.activation(out=sqv[:], in_=bv[:, 0:256], func=AF.Square), R=[bv], W=[sqv])
                stv = st1.next()
                op("dve", lambda: nc.vector.reduce_sum(out=stv[:, 0:4], in_=sqv[:].rearrange("p (h c) -> p h c", c=64), axis=AX.X), R=[sqv], W=[stv])
                op("act", lambda: nc.scalar.activation(out=stv[:, 0:4], in_=stv[:, 0:4], func=AF.Ln, bias=eps_t[:], scale=1.0 / 64), R=[eps_t], W=[stv])
                op("act", lambda: nc.scalar.activation(out=stv[:, 0:4], in_=stv[:, 0:4], func=AF.Exp, scale=-0.5), W=[stv])
                vn = tokw.next()
                op("dve", lambda: nc.vector.tensor_tensor(out=vn[:].rearrange("p (h c) -> p h c", c=64), in0=bv[:, 0:256].rearrange("p (h c) -> p h c", c=64), in1=stv[:, 0:4].unsqueeze(2).to_broadcast([128, 4, 64]), op=ALU.mult), R=[bv, stv], W=[vn])
                vnb = vnb_r.next()
                op("dve", lambda: nc.vector.tensor_tensor(out=vnb[:], in0=vn[:], in1=SG[:], op=ALU.mult), R=[vn, SG], W=[vnb])
                vnbs.append(vnb)
            for t in range(4):
                tsl = slice(t * 128, (t + 1) * 128)
                tok = slice(g * 512 + t * 128, g * 512 + (t + 1) * 128)
                bv2 = abanks.next()
                for k in range(8):
                    op("pe", lambda k=k: nc.tensor.matmul(bv2[:, 0:256], lhsT=hTg[:, k, tsl], rhs=WA[:, k, B_V:B_V + 256], start=(k == 0), stop=(k == 7)), R=[WB_A, WB_T, hTg], W=[bv2], inc=(k == 7))
                vb = tokb.next()
                op("act", lambda: nc.scalar.copy(out=vb[:], in_=bv2[:, 0:256]), R=[bv2], W=[vb])
                dma("pool", vtok_d.t[tok, :], vb[:], vb, vtok_d, disjoint=True)
                bv3 = abanks.next()
                for c in range(2):
                    csl = slice(t * 128 + c * 64, t * 128 + (c + 1) * 64)
                    for k in range(8):
                        op("pe", lambda k=k, c=c, csl=csl: nc.tensor.matmul(bv3[0:64, c * 256:(c + 1) * 256], lhsT=hTg[:, k, csl], rhs=WA[:, k, C_I:C_I + 256], start=(k == 0), stop=(k == 7)), R=[WB_A, WB_T, hTg], W=[bv3], inc=(k == 7 and c == 1))
                vh = vhb.next()
                op("act", lambda: nc.scalar.copy(out=vh[:].rearrange("p c f -> p (c f)"), in_=bv3[0:64, :]), R=[bv3], W=[vh])
                dma("pool", vH_d.t[tok, :].rearrange("(c s) f -> s c f", s=64), vh[:], vh, vH_d, disjoint=True)
            for t in range(4):
                tsl = slice(t * 128, (t + 1) * 128)
                vnb = vnbs[t]
                bs = abanks.next()
                for gh in range(4):
                    hh, j = gh % 2, gh // 2
                    op("pe", lambda gh=gh, hh=hh, j=j: nc.tensor.matmul(bs[64 * hh:64 * hh + 64, j * 128:(j + 1) * 128], lhsT=vnb[:, 64 * gh:64 * gh + 64], rhs=WT[:, gh, :], start=True, stop=True), R=[vnb, WT], W=[bs], inc=(gh == 3))
                ts_ = tokw.next()
                op("dve", lambda: nc.vector.tensor_tensor(out=ts_[:], in0=bs[:, 0:256], in1=BT[:].rearrange("p j t -> p (j t)"), op=ALU.add), R=[bs, BT], W=[ts_])
                op("dve", lambda: nc.vector.tensor_tensor(out=ydb[:, :, tsl], in0=ts_[:].rearrange("p (j t) -> p j t", j=2), in1=ug[:, :, tsl], op=ALU.mult), R=[ts_, ug], Wd=[ydb])
            for j in range(2):
                store(y_d[3], 128 * j, 128, ydb, ydb[:, j, :])
            cpos_transposes()

        dma("pool", xs2.t[:, 0:128], cposT[:].rearrange("p t h -> p (t h)"), cposT, xs2, disjoint=True, sem_owner=flag_t)
        ph.close()
        C.collective(xs1a, xg1a)
        C.collective(xs1b, xg1b)
        C.collective(xs1c, xg1c)
        C.collective(xs2, xg2)
        def phase_H(final):
            ph = Phase(C)
            Sst = C.sb([64, 4, 64], F32, dma=True)
            Sin = C.sb([64, 4, 64], F32, dma=True)
            Sbf_r = Rot([C.sb([64, 4, 64], BF16) for _ in range(10)])
            Qg_r = Rot([C.sb([64, 4, 512], BF16, dma=True) for _ in range(2)])
            Kg_r = Rot([C.sb([64, 4, 512], BF16, dma=True) for _ in range(2)])
            Gg_r = Rot([C.sb([64, 4, 512], BF16, dma=True) for _ in range(2)])
            Vg_r = Rot([C.sb([64, 8, 256], BF16, dma=True) for _ in range(2)])
            Kt_r = Rot([C.sb([64, 4, 64], BF16) for _ in range(8)])
            UT_r = Rot([C.sb([64, 8, 256], F32, dma=True) for _ in range(2)])
            ST_r = Rot([C.sb([64, 8, 64], BF16) for _ in range(2)])
            hq_r = Rot([C.sb([64, 512], BF16) for _ in range(2)])
            hr_r = Rot([C.sb([64, 512], F32) for _ in range(2)])
            hy_r = Rot([C.sb([64, 512], F32) for _ in range(2)])
            hyb_r = Rot([C.sb([64, 512], BF16, dma=True) for _ in range(2)])
            def scan_group(g, final):
                cols = slice(g * 512, (g + 1) * 512)
                Kg, Vg = Kg_r.next(), Vg_r.next()
                dma("sp", Kg[:], Kp_d.t[:, cols].rearrange("(h p) t -> p h t", p=64), Kp_d, Kg)
                dma("sp", Vg[:], vH_d.t[cols, :].rearrange("(c s) f -> s c f", s=64), vH_d, Vg)
                UT = UT_r.next()
                Sbfs = []
                if not final:
                    Kts = []
                    for c in range(8):
                        csl = slice(c * 64, (c + 1) * 64)
                        tb = tbanks.next()
                        for h in range(4):
                            op("pe", lambda h=h: nc.tensor.transpose(out=tb[0:64, h * 64:(h + 1) * 64], in_=Kg[:, h, csl], identity=ident_bf[0:64, 0:64]), R=[Kg, ident_bf], W=[tb], inc=(h == 3))
                        Kt = Kt_r.next()
                        op("act", lambda: nc.scalar.copy(out=Kt[:].rearrange("p h k -> p (h k)"), in_=tb[0:64, 0:256]), R=[tb], W=[Kt])
                        Kts.append(Kt)
                    for c in range(8):
                        chn = g * 8 + c
                        Kt = Kts[c]
                        UB = abanks.next()
                        for h in range(4):
                            op("pe", lambda h=h: nc.tensor.matmul(UB[0:64, h * 64:(h + 1) * 64], lhsT=Kt[:, h, :], rhs=Vg[:, c, 64 * h:64 * h + 64], start=True, stop=True), R=[Kt, Vg], W=[UB], inc=(h == 3))
                        op("dve", lambda: nc.vector.tensor_tensor(out=UT[:, c, :].rearrange("p (h v) -> p h v", h=4), in0=UB[0:64, 0:256].rearrange("p (h v) -> p h v", h=4), in1=CV[:, 2, :, chn:chn + 1].to_broadcast([64, 4, 64]), op=ALU.mult), R=[UB, CV], Wd=[UT])
                    dma("pool", ut_d.t[:, g * 8:(g + 1) * 8, :], UT[:], UT, ut_d, disjoint=True)
                else:
                    Qg, Gg = Qg_r.next(), Gg_r.next()
                    dma("sp", Qg[:], Qp_d.t[:, cols].rearrange("(h p) t -> p h t", p=64), Qp_d, Qg)
                    dma("sp", Gg[:], gC_d.t[:, cols].rearrange("(h p) t -> p h t", p=64), gC_d, Gg)
                    dma("sp", UT[:], ut_d.t[:, g * 8:(g + 1) * 8, :], ut_d, UT)
                for c in range(8):
                    chn = g * 8 + c
                    if final:
                        Sbf = Sbf_r.next()
                        op("pool", lambda: nc.gpsimd.tensor_tensor(out=Sbf[:], in0=Sst[:], in1=CV[:, 0, :, chn:chn + 1].to_broadcast([64, 4, 64]), op=ALU.mult), R=[Sst, CV], W=[Sbf])
                        Sbfs.append(Sbf)
                    op("dve", lambda: nc.vector.tensor_tensor(out=Sst[:], in0=Sst[:], in1=CV[:, 1, :, chn:chn + 1].to_broadcast([64, 4, 64]), op=ALU.mult), R=[CV], W=[Sst])
                    op("dve", lambda: nc.vector.tensor_tensor(out=Sst[:], in0=Sst[:], in1=UT[:, c, :].rearrange("p (h v) -> p h v", h=4), op=ALU.add), R=[UT], W=[Sst])
                if not final:
                    return
                for hp in range(2):
                    hs = (2 * hp, 2 * hp + 1)
                    SBk, STh, OBk, hq, b2, hr, hy, hyb = {}, {}, {}, {}, {}, {}, {}, {}
                    for h in hs:
                        SBk[h] = abanks.next()
                        for c in range(8):
                            csl = slice(c * 64, (c + 1) * 64)
                            op("pe", lambda c=c, csl=csl: nc.tensor.matmul(SBk[h][0:64, csl], lhsT=Kg[:, h, csl], rhs=Qg[:, h, csl], start=True, stop=True), R=[Kg, Qg], W=[SBk[h]], inc=(c == 7))
                    for h in hs:
                        STh[h] = ST_r.next()
                        op("dve", lambda: nc.vector.tensor_tensor(out=STh[h][:], in0=SBk[h][0:64, :].rearrange("p (c t) -> p c t", t=64), in1=tri[:].unsqueeze(1).to_broadcast([64, 8, 64]), op=ALU.mult), R=[SBk[h], tri], W=[STh[h]])
                    for h in hs:
                        OBk[h] = abanks.next()
                        for c in range(8):
                            csl = slice(c * 64, (c + 1) * 64)
                            op("pe", lambda c=c, csl=csl: nc.tensor.matmul(OBk[h][0:64, csl], lhsT=Sbfs[c][:, h, :], rhs=Qg[:, h, csl], start=True, stop=False), R=[Sbfs[c], Qg], W=[OBk[h]], inc=False)
                            op("pe", lambda c=c, csl=csl: nc.tensor.matmul(OBk[h][0:64, csl], lhsT=Vg[:, c, 64 * h:64 * h + 64], rhs=STh[h][:, c, :], start=False, stop=True), R=[Vg, STh[h]], W=[OBk[h]], inc=(c == 7))
                    for h in hs:
                        hq[h] = hq_r.next()
                        op("act", lambda: nc.scalar.activation(out=hq[h][:], in_=OBk[h][0:64, :], func=AF.Square), R=[OBk[h]], W=[hq[h]])
                    for h in hs:
                        b2[h] = abanks.next()
                        op("pe", lambda: nc.tensor.matmul(b2[h][0:64, :], lhsT=blockones[0:64, 0:64], rhs=hq[h][:], start=True, stop=True), R=[blockones, hq[h]], W=[b2[h]])
                    for h in hs:
                        hr[h] = hr_r.next()
                        op("act", lambda: nc.scalar.activation(out=hr[h][:], in_=b2[h][0:64, :], func=AF.Ln, bias=eps_t[0:64, :], scale=1.0 / 64), R=[b2[h], eps_t], W=[hr[h]])
                        op("act", lambda: nc.scalar.activation(out=hr[h][:], in_=hr[h][:], func=AF.Exp, scale=-0.5), W=[hr[h]])
                    for h in hs:
                        hy[h] = hy_r.next()
                        op("dve", lambda: nc.vector.scalar_tensor_tensor(out=hy[h][:], in0=OBk[h][0:64, :], scalar=hgain(h), in1=hr[h][:], op0=ALU.mult, op1=ALU.mult), R=[OBk[h], hr[h], PRM], W=[hy[h]])
                        hyb[h] = hyb_r.next()
                        op("pool", lambda: nc.gpsimd.tensor_tensor(out=hyb[h][:], in0=hy[h][:], in1=Gg[:, h, :], op=ALU.mult), R=[hy[h], Gg], W=[hyb[h]])
                        dma("pool", y_d[2].t[64 * h:64 * h + 64, cols], hyb[h][:], hyb[h], y_d[2], disjoint=True)

            if not final:
                op("dve", lambda: nc.vector.memset(Sst[:], 0.0), W=[Sst])
                for g in range(NG):
                    scan_group(g, False)
                dma("pool", xs3.t[:, :], Sst[:].rearrange("p h v -> p (h v)"), Sst, xs3)
                C.collective(xs3, xg3)
            else:
                dma("sp", Sin[:].rearrange("p h v -> p (h v)"), xg3.t[0:64, :], xg3, Sin)
                op("dve", lambda: nc.vector.tensor_scalar(out=Sst[:], in0=Sin[:], scalar1=flag_t[0:64, :], scalar2=None, op0=ALU.mult), R=[Sin, flag_t], W=[Sst])
                for g in range(NG):
                    scan_group(g, True)
            ph.close()
        phase_H(False)
        ph = Phase(C)
        R2 = C.sb([128, 132], F32, dma=True)
        Rtot = C.sb([128, 4], F32, dma=True)
        dma("sp", R2[:], xg2.t[0:128, :], xg2, R2)
        dma("sp", Rtot[:], xg2.t[127:128, 4 * NT - 4:4 * NT].partition_broadcast(128), xg2, Rtot)
        op("dve", lambda: nc.vector.tensor_tensor(out=biasP[:], in0=R2[:, 0:128].rearrange("p (t h) -> p t h", h=4), in1=Rtot[:].unsqueeze(1).to_broadcast([128, NT, 4]), op=ALU.subtract), R=[R2, Rtot], W=[biasP])
        op("dve", lambda: nc.vector.tensor_scalar(out=biasP[:], in0=biasP[:], scalar1=negbig[:], scalar2=None, op0=ALU.add), R=[negbig], W=[biasP])
        Hh = C.sb([128, 4], F32)
        op("dve", lambda: nc.vector.tensor_scalar(out=Hh[:], in0=R2[:, 128:132], scalar1=flag_t[:], scalar2=None, op0=ALU.mult), R=[R2, flag_t], W=[Hh])
        for j in range(2):
            cc = C.sb([128, 4], F32)
            yfx = C.sb([128, 2], BF16, dma=True)
            op("dve", lambda: nc.vector.tensor_scalar(out=cc[:, 0:1], in0=Hh[:, 2 * j:2 * j + 1], scalar1=convw(j, 0), scalar2=None, op0=ALU.mult), R=[Hh, PRM], W=[cc])
            op("dve", lambda: nc.vector.scalar_tensor_tensor(out=cc[:, 2:3], in0=Hh[:, 2 * j + 1:2 * j + 2], scalar=convw(j, 1), in1=cc[:, 0:1], op0=ALU.mult, op1=ALU.add), R=[Hh, PRM], W=[cc])
            op("dve", lambda: nc.vector.tensor_scalar(out=cc[:, 3:4], in0=Hh[:, 2 * j + 1:2 * j + 2], scalar1=convw(j, 0), scalar2=None, op0=ALU.mult), R=[Hh, PRM], W=[cc])
            op("dve", lambda: nc.vector.tensor_tensor(out=cc[:, 0:2], in0=cc[:, 2:4], in1=GF[:, j, :], op=ALU.mult), R=[GF], W=[cc])
            op("dve", lambda: nc.vector.tensor_tensor(out=yfx[:], in0=cc[:, 0:2], in1=YA0[:, j, :], op=ALU.add), R=[cc, YA0], W=[yfx])
            dma("pool", y_d[0].t[128 * j:128 * j + 128, 0:2], yfx[:], yfx, y_d[0], disjoint=True)
        ph.close()
        WG = WBIG[:, 0:8 * 2048].rearrange("p (k c) -> p k c", k=8)
        WU = WBIG[:, 8 * 2048:8 * 2048 + 8 * 512].rearrange("p (k c) -> p k c", k=8)

        def load_c1(hf, engs, defer=None, WGd=None, WUd=None, wb=None):
            WGd = WG if WGd is None else WGd
            WUd = WU if WUd is None else WUd
            wb = [WB_A] if wb is None else wb
            for br in range(4):
                load_w(lambda k, c0, n, br=br: WGd[:, k, br * 512 + c0:br * 512 + c0 + n],
                       lambda k, c0, n, br=br: (w_in, w_in.t[l, k * 128:(k + 1) * 128, M_G + br * 1024 + hf * 512 + c0:M_G + br * 1024 + hf * 512 + c0 + n]),
                       512, gmix, wb, engs=engs, defer=defer)
            load_w(lambda k, c0, n: WUd[:, k, c0:c0 + n],
                   lambda k, c0, n: (w_up, w_up.t[l, k // 2, (k % 2) * 128:(k % 2) * 128 + 128, hf * 512 + c0:hf * 512 + c0 + n]), 512, None, wb, engs=engs, defer=defer)

        pre_c1 = []
        load_c1(0, Rot(["dve", "pool"]), pre_c1)
        ph = Phase(C)
        Kaug = C.sb([128, S], BF16, dma=True)
        Vp = C.sb([128, S], BF16, dma=True)
        KaugP = C.sb([128, S], BF16, dma=True)
        VpP = C.sb([128, S], BF16, dma=True)
        VpP3 = VpP.t.rearrange("p (t f) -> p t f", f=128)
        op("pool", lambda: nc.gpsimd.memset(VpP3[:, :, 64:128], 1.0), Wd=[VpP])
        Vp3 = Vp.t.rearrange("p (t f) -> p t f", f=128)
        qa_r = Rot([C.sb([67, 512], BF16, dma=True) for _ in range(2)])
        gbt_r = Rot([C.sb([64, 512], BF16, dma=True) for _ in range(2)])
        P_r = Rot([C.sb([128, 512], BF16) for _ in range(4)])
        rd_r = Rot([C.sb([64, 512], F32) for _ in range(2)])
        of_r = Rot([C.sb([64, 512], F32) for _ in range(2)])
        yo_r = Rot([C.sb([64, 512], BF16, dma=True) for _ in range(2)])
        op("pool", lambda: nc.gpsimd.memset(Vp3[:, :, 64:128], 1.0), Wd=[Vp])
        for h in range(4):
            dma("sp", Kaug.t[0:67, :], kaug_h(h)[1], kaug_h(h)[0], Kaug)
            dma("sp", KaugP.t[0:67, :], kaugP_h(h)[1], kaugP_h(h)[0], KaugP)
            for c4 in range(S // 2048):
                dma("sp", Vp3[:, c4 * 16:(c4 + 1) * 16, 0:64], vtok_d.t[c4 * 2048:(c4 + 1) * 2048, 64 * h:64 * h + 64].rearrange("(t p) f -> p t f", p=128), vtok_d, Vp, disjoint=True)
                dma("sp", VpP3[:, c4 * 16:(c4 + 1) * 16, 0:64], vtokP_d.t[c4 * 2048:(c4 + 1) * 2048, 64 * h:64 * h + 64].rearrange("(t p) f -> p t f", p=128), vtokP_d, VpP, disjoint=True)
            pending = []

            def flush():
                while pending:
                    pending.pop(0)()

            for qg in range(NG):
                cols = slice(qg * 512, (qg + 1) * 512)
                qa = qa_r.next()
                dma("sp", qa[:], qaug_d.t[h, :, cols], qaug_d, qa)
                gbt = gbt_r.next()
                dma("sp", gbt[:], gB_d.t[64 * h:64 * h + 64, cols], gB_d, gbt)
                OB = obanks.next()
                nkt = NT + 4 * qg + 4
                for kt_all in range(nkt):
                    prevh = kt_all < NT
                    kt = kt_all if prevh else kt_all - NT
                    i = -1 if prevh else kt - 4 * qg
                    q0 = 0 if i < 0 else 128 * i
                    N = 512 - q0
                    SB = banks.next()
                    ksl = slice(kt * 128, (kt + 1) * 128)
                    Ksrc = KaugP if prevh else Kaug
                    Vsrc, Vsrc3 = (VpP, VpP3) if prevh else (Vp, Vp3)
                    bsrc = biasP if prevh else cposT
                    if i < 0:
                        op("pe", lambda: nc.tensor.matmul(SB[:, 0:N], lhsT=Ksrc.t[0:67, ksl], rhs=qa[:, q0:512], start=True, stop=True), R=[Ksrc, qa], W=[SB])
                    else:
                        op("pe", lambda: nc.tensor.matmul(SB[:, 0:N], lhsT=Kaug.t[0:67, ksl], rhs=qa[:, q0:512], start=True, stop=False), R=[Kaug, qa], W=[SB], inc=False)
                        op("pe", lambda: nc.tensor.matmul(SB[:, 0:128], lhsT=ident_bf[:], rhs=maskT[:], start=False, stop=True), R=[ident_bf, maskT], W=[SB])
                    while len(pending) >= 2:
                        pending.pop(0)()
                    last = (kt_all == nkt - 1)
                    first = (kt_all == 0)

                    def stage2(kt=kt, q0=q0, N=N, SB=SB, last=last, first=first, OB=OB, gbt=gbt, cols=cols, Vsrc=Vsrc, Vsrc3=Vsrc3, bsrc=bsrc):
                        Pt = P_r.next()
                        op("act", lambda: nc.scalar.activation(out=Pt[:, 0:N], in_=SB[:, 0:N], func=AF.Exp, bias=bsrc[:, kt, h:h + 1], scale=1.0), R=[SB, bsrc], W=[Pt])
                        op("pe", lambda: nc.tensor.matmul(OB[:, q0:512], lhsT=Vsrc3[:, kt, :], rhs=Pt[:, 0:N], start=first, stop=last), R=[Vsrc, Pt], W=[OB], inc=last)
                        if last:
                            rd = rd_r.next()
                            op("dve", lambda: nc.vector.reciprocal(out=rd[:], in_=OB[64:128, :]), R=[OB], W=[rd])
                            of = of_r.next()
                            op("dve", lambda: nc.vector.tensor_tensor(out=of[:], in0=OB[0:64, :], in1=rd[:], op=ALU.mult), R=[OB, rd], W=[of])
                            yo = yo_r.next()
                            op("pool", lambda: nc.gpsimd.tensor_tensor(out=yo[:], in0=of[:], in1=gbt[:], op=ALU.mult), R=[of, gbt], W=[yo])
                            dma("pool", y_d[1].t[64 * h:64 * h + 64, cols], yo[:], yo, y_d[1], disjoint=True)
                    pending.append(stage2)
                drain(pre_c1, 2)
            flush()

        ph.close()
        phase_H(True)
        WO = WBIG[:, 20480:20480 + 8 * 1024].rearrange("p (k c) -> p k c", k=8)
        WPG = WB2[:, 0:8 * 1024].rearrange("p (k c) -> p k c", k=8)
        WPP = WB2[:, 8 * 1024:10 * 1024].rearrange("p (k c) -> p k c", k=2)
        ph = Phase(C)
        WC1B = C.sb([128, 20480], BF16)
        WG1 = WC1B[:, 0:8 * 2048].rearrange("p (k c) -> p k c", k=8)
        WU1 = WC1B[:, 8 * 2048:8 * 2048 + 8 * 512].rearrange("p (k c) -> p k c", k=8)
        pre_c1b = []
        load_c1(1, Rot(["act", "dve"]), pre_c1b, WG1, WU1, [WC1B])
        hTg_r = Rot([C.sb([128, 8, 512], BF16, dma=True) for _ in range(2)])
        Yg_r = Rot([C.sb([128, 8, 512], BF16, dma=True) for _ in range(2)])
        mg_r = Rot([C.sb([128, 4, 512], BF16, dma=True) for _ in range(2)])
        acc_r = Rot([C.sb([128, 512], F32) for _ in range(3)])
        f32w = Rot([C.sb([128, 512], F32) for _ in range(6)])
        for hf in range(2):
            WGh, WUh, WBh = (WG, WU, WB_A) if hf == 0 else (WG1, WU1, WC1B)
            if hf == 1:
                drain(pre_c1b)
                pre_c2 = []
                load_w(lambda k, c0, n: WO[:, k, c0:c0 + n], lambda k, c0, n: (w_o, w_o.t[l, k * 128:(k + 1) * 128, c0:c0 + n]), 1024, None, [WB_T], engs=Rot(["act", "dve"]), defer=pre_c2)
                load_w(lambda k, c0, n: WPG[:, k, c0:c0 + n], lambda k, c0, n: (w_pg, w_pg.t[l, k * 128:(k + 1) * 128, c0:c0 + n]), 1024, gple, [WB2], engs=Rot(["act", "dve"]), defer=pre_c2)
                load_w(lambda k, c0, n: WPP[:, k, c0:c0 + n], lambda k, c0, n: (w_pp, w_pp.t[l, k * 128:(k + 1) * 128, c0:c0 + n]), 1024, None, [WB2], ks=range(2), engs=Rot(["act", "dve"]), defer=pre_c2)
            else:
                drain(pre_c1)
            for g in range(NG):
                cols = slice(g * 512, (g + 1) * 512)
                hTg = hTg_r.next()
                dma("sp", hTg[:], hT_d.t.rearrange("(k p) s -> p k s", p=128)[:, :, cols], hT_d, hTg)
                Yg = Yg_r.next()
                for br in range(4):
                    dma("sp", Yg[:, 2 * br:2 * br + 2, :], y_d[br].t[:, cols].rearrange("(k p) t -> p k t", p=128), y_d[br], Yg, disjoint=(br > 0))
                mg = mg_r.next()
                for db in range(4):
                    dsl = slice(db * 128, (db + 1) * 128)
                    acc = None
                    for br in range(4):
                        gbk = abanks.next()
                        for k in range(8):
                            op("pe", lambda k=k: nc.tensor.matmul(gbk[:], lhsT=WGh[:, k, br * 512 + db * 128:br * 512 + db * 128 + 128], rhs=hTg[:, k, :], start=(k == 0), stop=(k == 7)), R=[WBh, hTg], W=[gbk], inc=(k == 7))
                        sg_ = f32w.next()
                        op("act", lambda: nc.scalar.activation(out=sg_[:], in_=gbk[:], func=AF.Sigmoid, bias=mergeb(br, hf * 4 + db), scale=1.0), R=[gbk, PRM], W=[sg_])
                        ubk = abanks.next()
                        for kk in range(2):
                            op("pe", lambda kk=kk: nc.tensor.matmul(ubk[:], lhsT=WUh[:, 2 * br + kk, dsl], rhs=Yg[:, 2 * br + kk, :], start=(kk == 0), stop=(kk == 1)), R=[WBh, Yg], W=[ubk], inc=(kk == 1))
                        if br == 0:
                            acc = acc_r.next()
                            op("dve", lambda: nc.vector.tensor_tensor(out=acc[:], in0=sg_[:], in1=ubk[:], op=ALU.mult), R=[sg_, ubk], W=[acc])
                        else:
                            tt = f32w.next()
                            op("dve", lambda: nc.vector.tensor_tensor(out=tt[:], in0=sg_[:], in1=ubk[:], op=ALU.mult), R=[sg_, ubk], W=[tt])
                            if br == 1:
                                op("dve", lambda: nc.vector.tensor_tensor(out=acc[:], in0=acc[:], in1=tt[:], op=ALU.add), R=[tt], W=[acc])
                            elif br < 3:
                                op("pool", lambda: nc.gpsimd.tensor_tensor(out=acc[:], in0=acc[:], in1=tt[:], op=ALU.add), R=[tt], W=[acc])
                            else:
                                op("pool", lambda: nc.gpsimd.tensor_tensor(out=mg[:, db, :], in0=acc[:], in1=tt[:], op=ALU.add), R=[tt, acc], Wd=[mg])
                dma("pool", mT_d.t.rearrange("(k p) s -> p k s", p=128)[:, hf * 4:hf * 4 + 4, cols], mg[:], mg, mT_d, disjoint=True)
                if hf == 1:
                    drain(pre_c2, 3)
                else:
                    drain(pre_c1b, 5)
        drain(pre_c2)
        ph.close()

        ph = Phase(C)
        if l + 1 < n_layers:
            load_wa(l + 1, range(0, 5), defer=pre_a, engs=Rot(["act", "dve"]))
        mgl_r = Rot([C.sb([128, 8, 512], BF16, dma=True) for _ in range(2)])
        xin_r = Rot([C.sb([128, 1024], F32, dma=True) for _ in range(2)])
        xn_r = Rot([C.sb([128, 1024], F32) for _ in range(2)])
        xo_r = Rot([C.sb([128, 1024], F32, dma=True) for _ in range(2)])
        hb_r = Rot([C.sb([128, 1024], BF16) for _ in range(2)])
        junk = C.sb([128, 1024], BF16)
        st1 = Rot([C.sb([128, 4], F32) for _ in range(4)])
        hpT_r = Rot([C.sb([128, 8, 128], BF16) for _ in range(2)])
        pin_r = Rot([C.sb([128, 256], F32, dma=True) for _ in range(2)])
        pb_r = Rot([C.sb([128, 256], BF16) for _ in range(2)])
        pT_r = Rot([C.sb([128, 2, 128], BF16) for _ in range(2)])
        f32w = Rot([C.sb([128, 512], F32) for _ in range(6)])
        mgs = {}

        def c2_stage1(ti):
            g, t = ti // 4, ti % 4
            if t == 0:
                mgs[g] = mgl_r.next()
                dma("sp", mgs[g][:], mT_d.t.rearrange("(k p) s -> p k s", p=128)[:, :, g * 512:(g + 1) * 512], mT_d, mgs[g])
            mg = mgs[g]
            tsl = slice(t * 128, (t + 1) * 128)
            tok = slice(ti * 128, (ti + 1) * 128)
            xin = xin_r.next()
            dma("sp", xin[:], xsrc.t[tok, :], xsrc, xin)
            pin = pin_r.next()
            dma("sp", pin[:], p_in.t[l, tok, :], p_in, pin)
            xn = xn_r.next()
            for half in range(2):
                hs = slice(half * 512, (half + 1) * 512)
                obk = abanks.next()
                for db in range(8):
                    op("pe", lambda db=db: nc.tensor.matmul(obk[:], lhsT=mg[:, db, tsl], rhs=WO[:, db, hs], start=(db == 0), stop=(db == 7)), R=[WB_T, mg], W=[obk], inc=(db == 7))
                op("dve", lambda: nc.vector.tensor_tensor(out=xn[:, hs], in0=xin[:, hs], in1=obk[:], op=ALU.add), R=[xin, obk], Wd=[xn])
            pb = pb_r.next()
            op("pool", lambda: nc.gpsimd.tensor_copy(out=pb[:], in_=pin[:]), R=[pin], W=[pb])
            return xn, pb

        def c2_stage1b(xn):
            st = rms_rstd(xn[:], xn, 1024, junk, st1)
            hb = hb_r.next()
            op("dve", lambda: nc.vector.tensor_scalar(out=hb[:], in0=xn[:], scalar1=st[:, 2:3], scalar2=None, op0=ALU.mult), R=[xn, st], W=[hb])
            return hb

        xn0, pb0 = c2_stage1(0)
        nxt = (xn0, c2_stage1b(xn0), pb0)
        for ti in range(NT):
            tok = slice(ti * 128, (ti + 1) * 128)
            xn, hb, pb = nxt
            nx = c2_stage1(ti + 1) if ti + 1 < NT else None
            if True:
                tb = tbanks.next()
                for k in range(8):
                    op("pe", lambda k=k: nc.tensor.transpose(out=tb[:, k * 128:(k + 1) * 128], in_=hb[:, k * 128:(k + 1) * 128], identity=ident_bf[:]), R=[hb, ident_bf], W=[tb], inc=(k == 7))
                hpT = hpT_r.next()
                op("act", lambda: nc.scalar.copy(out=hpT[:], in_=tb[:].rearrange("p (k t) -> p k t", k=8)), R=[tb], W=[hpT])
                tb2 = tbanks.next()
                for k in range(2):
                    op("pe", lambda k=k: nc.tensor.transpose(out=tb2[:, k * 128:(k + 1) * 128], in_=pb[:, k * 128:(k + 1) * 128], identity=ident_bf[:]), R=[pb, ident_bf], W=[tb2], inc=(k == 1))
                pT = pT_r.next()
                op("act", lambda: nc.scalar.copy(out=pT[:], in_=tb2[:, 0:256].rearrange("p (k t) -> p k t", k=2)), R=[tb2], W=[pT])
                if nx is not None:
                    nxt = (nx[0], c2_stage1b(nx[0]), nx[1])
                xo = xo_r.next()
                for half in range(2):
                    hs = slice(half * 512, (half + 1) * 512)
                    gbk = abanks.next()
                    for k in range(8):
                        op("pe", lambda k=k: nc.tensor.matmul(gbk[:], lhsT=hpT[:, k, :], rhs=WPG[:, k, hs], start=(k == 0), stop=(k == 7)), R=[WB2, hpT], W=[gbk], inc=(k == 7))
                    sgp = f32w.next()
                    op("act", lambda: nc.scalar.activation(out=sgp[:], in_=gbk[:], func=AF.Sigmoid), R=[gbk], W=[sgp])
                    pbk = abanks.next()
                    for k in range(2):
                        op("pe", lambda k=k: nc.tensor.matmul(pbk[:], lhsT=pT[:, k, :], rhs=WPP[:, k, hs], start=(k == 0), stop=(k == 1)), R=[WB2, pT], W=[pbk], inc=(k == 1))
                    tt = f32w.next()
                    op("dve", lambda: nc.vector.tensor_tensor(out=tt[:], in0=sgp[:], in1=pbk[:], op=ALU.mult), R=[sgp, pbk], W=[tt])
                    if half == 0:
                        op("dve", lambda: nc.vector.tensor_tensor(out=xo[:, hs], in0=xn[:, hs], in1=tt[:], op=ALU.add), R=[xn, tt], Wd=[xo])
                    else:
                        op("pool", lambda: nc.gpsimd.tensor_tensor(out=xo[:, hs], in0=xn[:, hs], in1=tt[:], op=ALU.add), R=[xn, tt], Wd=[xo])
                dma("pool", xdst.t[tok, :], xo[:], xo, xdst, disjoint=True)
            if ti % 4 == 3:
                drain(pre_a, 3)
        ph.close()

    C.wait_all("pool", [y_out, xres])
    C.wait_all("sp", [y_out])


_NC_CACHE = {}


def kernel(**inputs):
    n_cores = 8
    if "nc" not in _NC_CACHE:
        _NC_CACHE["nc"] = build_nc(DEPTH)
    nc = _NC_CACHE["nc"]
    names = ["norm_mix", "w_in", "conv_w", "conv_b", "fgate_bias", "q_norm", "k_norm", "lb_logits", "hgrn_norm",
             "sgu_norm", "spatial_w", "spatial_b", "w_up", "merge_b", "w_o", "norm_ple", "w_ple_gate", "w_ple_proj"]
    shared = {n: np.ascontiguousarray(np.asarray(inputs[n], dtype=np.float32)) for n in names}
    x = np.asarray(inputs["x"], dtype=np.float32)
    p = np.asarray(inputs["p"], dtype=np.float32)
    in_maps = []
    for c in range(n_cores):
        b, hf = c // 2, c % 2
        m = dict(shared)
        m["x"] = np.ascontiguousarray(x[b, hf * S:(hf + 1) * S])
        m["p"] = np.ascontiguousarray(p[:, b, hf * S:(hf + 1) * S])
        m["flag"] = np.full((128, 1), float(hf), dtype=np.float32)
        in_maps.append(m)
    res = run_bass_kernel_spmd(nc, in_maps, core_ids=list(range(n_cores)))
    out = np.empty((4, 2 * S, D), dtype=np.float32)
    for c in range(n_cores):
        out[c // 2, (c % 2) * S:(c % 2 + 1) * S] = res.results[c]["y"]
    return out
```

```python
import numpy as np
from contextlib import ExitStack
import concourse.bass as bass
import concourse.mybir as mybir
from concourse.bass_utils import run_bass_kernel_spmd

F32 = mybir.dt.float32
BF16 = mybir.dt.bfloat16
AF = mybir.ActivationFunctionType
ALU = mybir.AluOpType
AX = mybir.AxisListType

D = 1024
S = 4096
RG = [[0, 1], [2, 3], [4, 5], [6, 7]]
DEPTH = 4
W = 256
NG = S // 512
NT = S // 128
NCH = S // 64
INC = 7940
EPS = 1e-6
A_X, A_B, A_C, A_G = 0, 256, 512, 768
B_Q, B_K, B_V, B_G, B_F = 1024, 1280, 1536, 1792, 2048
C_Q, C_F, C_I, C_G = 2052, 2308, 2564, 2820
D_U, D_V, D_G = 3076, 3332, 3588
M_G = 3844
NA = 3844


class TK:
    __slots__ = ("w", "r")

    def __init__(self):
        self.w = {}
        self.r = {}


class Sem:
    def __init__(self, h):
        self.h = h
        self.cnt = 0


class Buf:
    def __init__(self, t, sem=None):
        self.t = t
        self.tk = TK()
        self.sem = sem

    def __getitem__(self, k):
        return self.t[k]


class Ctx:
    def __init__(self, nc, es):
        self.nc = nc
        self.es = es
        self.E = {"pe": nc.tensor, "act": nc.scalar, "dve": nc.vector, "pool": nc.gpsimd, "sp": nc.sync}
        self.sem = {e: es.enter_context(nc.semaphore("s_" + e)) for e in ("pe", "act", "dve", "pool")}
        self.cnt = {e: 0 for e in self.sem}
        self.seen = {e: {} for e in self.E}
        self.semobj = dict(self.sem)
        self.nbuf = 0
        self.sems = [Sem(es.enter_context(nc.semaphore(f"d{i}"))) for i in range(64)]
        for sm in self.sems:
            self.semobj[id(sm)] = sm.h
        self.semi = 0
        self.pes = es
        self.ccsem = self.sems.pop()

    def sb(self, shape, dt, dma=False, name=None):
        self.nbuf += 1
        t = self.pes.enter_context(self.nc.sbuf_tensor(name or f"b{self.nbuf}", list(shape), dt))
        b = Buf(t)
        if dma:
            self.add_sem(b)
        return b

    def add_sem(self, b):
        b.sem = self.sems[self.semi]
        self.semi += 1
        return b

    def barrier(self):
        allk = {e: self.cnt[e] for e in self.sem}
        for sm in self.sems + [self.ccsem]:
            allk[id(sm)] = sm.cnt
        for e in self.E:
            self._wait(e, allk)

    def ps(self, shape, dt):
        self.nbuf += 1
        t = self.es.enter_context(self.nc.psum_tensor(f"p{self.nbuf}", list(shape), dt))
        return Buf(t)

    def _wait(self, e, deps):
        eng = self.E[e]
        seen = self.seen[e]
        for key, val in deps.items():
            if e == "pe" and key == "pe":
                continue
            if seen.get(key, 0) >= val:
                continue
            eng.wait_ge(self.semobj[key], val)
            seen[key] = val

    @staticmethod
    def _merge(d, src):
        for k, v in src.items():
            if d.get(k, 0) < v:
                d[k] = v

    def op(self, e, fn, R=(), W=(), Wd=(), inc=True):
        deps = {}
        for b in R:
            self._merge(deps, b.tk.w)
        for b in W:
            self._merge(deps, b.tk.w)
            self._merge(deps, b.tk.r)
        for b in Wd:
            self._merge(deps, b.tk.r)
            self._merge(deps, {k: v for k, v in b.tk.w.items() if k != e})
        self._wait(e, deps)
        ins = fn()
        if inc:
            self.cnt[e] += 1
            ins.then_inc(self.sem[e], 1)
            tick = self.cnt[e]
        else:
            tick = self.cnt[e] + 1
        for b in R:
            if b.tk.r.get(e, 0) < tick:
                b.tk.r[e] = tick
        for b in W:
            b.tk.w = {e: tick}
            b.tk.r = {}
        for b in Wd:
            b.tk.w[e] = tick
        return ins

    def dma(self, q, out, in_, src, dst, disjoint=False, sem_owner=None, **kw):
        owner = (sem_owner or (dst if dst.sem is not None else src)).sem
        key = id(owner)
        deps = {}
        self._merge(deps, src.tk.w)
        self._merge(deps, dst.tk.r)
        if disjoint:
            self._merge(deps, {k: v for k, v in dst.tk.w.items() if k != key})
        else:
            self._merge(deps, dst.tk.w)
        self._wait(q, deps)
        owner.cnt += 16
        self.E[q].dma_start(out=out, in_=in_, **kw).then_inc(owner.h, 16)
        if src.tk.r.get(key, 0) < owner.cnt:
            src.tk.r[key] = owner.cnt
        if disjoint:
            dst.tk.w[key] = owner.cnt
        else:
            dst.tk.w = {key: owner.cnt}
            dst.tk.r = {}

    def collective(self, src, dst):
        deps = {}
        self._merge(deps, src.tk.w)
        self._merge(deps, dst.tk.r)
        self._merge(deps, dst.tk.w)
        self._wait("pool", deps)
        sm = self.ccsem
        sm.cnt += 1
        self.nc.gpsimd.collective_compute("AllGather", ALU.bypass, replica_groups=RG,
                                          ins=[src.t.opt()], outs=[dst.t.opt()]).then_inc(sm.h, 1)
        key = id(sm)
        src.tk.r[key] = sm.cnt
        dst.tk.w = {key: sm.cnt}
        dst.tk.r = {}

    def wait_all(self, q, bufs):
        deps = {}
        for b in bufs:
            self._merge(deps, b.tk.w)
            self._merge(deps, b.tk.r)
        self._wait(q, deps)


class Phase:
    def __init__(self, C):
        self.C = C
        self.es = ExitStack()
        self.es.__enter__()
        C.pes = self.es
        self.semi0 = C.semi

    def close(self):
        C = self.C
        C.barrier()
        C.pes = C.es
        C.semi = self.semi0
        self.es.__exit__(None, None, None)


class Rot:
    def __init__(self, items):
        self.items = items
        self.i = 0

    def next(self):
        b = self.items[self.i % len(self.items)]
        self.i += 1
        return b


def build_nc(n_layers=DEPTH):
    nc = bass.Bass("TRN2", target_bir_lowering=False)
    with ExitStack() as es:
        es.enter_context(nc.allow_non_contiguous_dma("small strided parameter loads"))
        _build(nc, es, n_layers)
    return nc


def _build(nc, es, n_layers):
    C = Ctx(nc, es)
    op, dma = C.op, C.dma

    def dram_in(name, shape):
        return Buf(nc.dram_tensor(name, list(shape), F32, kind="ExternalInput").ap())

    x_in = dram_in("x", [S, D])
    p_in = dram_in("p", [DEPTH, S, W])
    norm_mix = dram_in("norm_mix", [DEPTH, D])
    w_in = dram_in("w_in", [DEPTH, D, INC])
    conv_w = dram_in("conv_w", [DEPTH, 3, W])
    conv_b = dram_in("conv_b", [DEPTH, W])
    fgate_bias = dram_in("fgate_bias", [DEPTH, 4])
    q_norm = dram_in("q_norm", [DEPTH, 64])
    k_norm = dram_in("k_norm", [DEPTH, 64])
    lb_logits = dram_in("lb_logits", [DEPTH, W])
    hgrn_norm = dram_in("hgrn_norm", [DEPTH, W])
    sgu_norm = dram_in("sgu_norm", [DEPTH, W])
    spatial_w = dram_in("spatial_w", [DEPTH, 4, 128, 128])
    spatial_b = dram_in("spatial_b", [DEPTH, 4, 128])
    w_up = dram_in("w_up", [DEPTH, 4, W, D])
    merge_b = dram_in("merge_b", [DEPTH, 4, D])
    w_o = dram_in("w_o", [DEPTH, D, D])
    norm_ple = dram_in("norm_ple", [DEPTH, D])
    w_pg = dram_in("w_ple_gate", [DEPTH, D, D])
    w_pp = dram_in("w_ple_proj", [DEPTH, W, D])
    y_out = Buf(nc.dram_tensor("y", [S, D], F32, kind="ExternalOutput").ap())

    def scratch(name, shape, dt):
        return Buf(nc.dram_tensor(name, list(shape), dt).ap())

    xres = scratch("xres", [S, D], F32)
    hT_d = scratch("hT_d", [D, S], BF16)
    y_d = [scratch(f"ybr{i}", [W, S], BF16) for i in range(4)]
    qaug_d = scratch("qaug", [4, 67, S], BF16)
    xs1a = scratch("xs1a", [134, S], BF16); xg1a = scratch("xg1a", [268, S], BF16)
    xs1b = scratch("xs1b", [134, S], BF16); xg1b = scratch("xg1b", [268, S], BF16)
    xs1c = scratch("xs1c", [256, S], BF16); xg1c = scratch("xg1c", [512, S], BF16)
    xs2 = scratch("xs2", [128, 132], F32)
    xg2 = scratch("xg2", [256, 132], F32)
    xs3 = scratch("xs3", [64, 256], F32)
    xg3 = scratch("xg3", [128, 256], F32)
    def kaug_h(h):
        b = xs1a if h < 2 else xs1b
        return b, b.t[67 * (h % 2):67 * (h % 2) + 67, :]

    def kaugP_h(h):
        b = xg1a if h < 2 else xg1b
        return b, b.t[67 * (h % 2):67 * (h % 2) + 67, :]

    def qaug_h(h):
        return qaug_d, qaug_d.t[h]
    vtok_d = Buf(xs1c.t[:, :].rearrange("r (q f) -> (r q) f", f=256)); vtok_d.tk = xs1c.tk
    vtokP_d = Buf(xg1c.t[0:256, :].rearrange("r (q f) -> (r q) f", f=256)); vtokP_d.tk = xg1c.tk
    flag_in = dram_in("flag", [128, 1])
    gB_d = scratch("gB", [W, S], BF16)
    gC_d = scratch("gC", [W, S], BF16)
    Qp_d = scratch("Qp", [W, S], BF16)
    Kp_d = scratch("Kp", [W, S], BF16)
    vH_d = scratch("vH", [S, W], BF16)
    mT_d = scratch("mT", [D, S], BF16)
    ut_d = Buf(nc.dram_tensor("ut_d", [64, NCH * 256], F32).ap().rearrange("p (c f) -> p c f", f=256))

    ident_bf = C.sb([128, 128], BF16)
    ident_f = C.sb([128, 128], F32)
    blockones = C.sb([128, 128], BF16)
    tri = C.sb([64, 64], F32)
    maskT = C.sb([128, 128], BF16)
    resetm = C.sb([128, 512], F32)
    ones_f = C.sb([128, 512], F32)
    onesb = C.sb([4, 512], BF16, dma=True)
    eps_t = C.sb([128, 1], F32)
    one_t = C.sb([128, 1], F32)

    pool = nc.gpsimd
    op("pool", lambda: pool.memset(ident_bf[:], 0.0), W=[ident_bf])
    op("pool", lambda: pool.affine_select(out=ident_bf[:], in_=ident_bf[:], pattern=[[-1, 128]], compare_op=ALU.not_equal, fill=1.0, base=0, channel_multiplier=1), W=[ident_bf])
    op("pool", lambda: pool.memset(ident_f[:], 0.0), W=[ident_f])
    op("pool", lambda: pool.affine_select(out=ident_f[:], in_=ident_f[:], pattern=[[-1, 128]], compare_op=ALU.not_equal, fill=1.0, base=0, channel_multiplier=1), W=[ident_f])
    op("pool", lambda: pool.memset(blockones[:], 0.0), W=[blockones])
    op("pool", lambda: pool.memset(blockones[0:64, 0:64], 1.0), W=[blockones])
    op("pool", lambda: pool.memset(blockones[64:128, 64:128], 1.0), W=[blockones])
    op("pool", lambda: pool.memset(tri[:], 1.0), W=[tri])
    op("pool", lambda: pool.affine_select(out=tri[:], in_=tri[:], pattern=[[1, 64]], compare_op=ALU.is_ge, fill=0.0, base=0, channel_multiplier=-1), W=[tri])
    op("pool", lambda: pool.memset(maskT[:], 0.0), W=[maskT])
    op("pool", lambda: pool.affine_select(out=maskT[:], in_=maskT[:], pattern=[[1, 128]], compare_op=ALU.is_ge, fill=-30000.0, base=0, channel_multiplier=-1), W=[maskT])
    op("pool", lambda: pool.memset(resetm[:], 1.0), W=[resetm])
    op("pool", lambda: pool.memset(resetm[:].rearrange("p (c t) -> p c t", t=64)[:, :, 0:1], 0.0), W=[resetm])
    op("pool", lambda: pool.memset(ones_f[:], 1.0), W=[ones_f])
    op("pool", lambda: pool.memset(onesb[:], 1.0), W=[onesb])
    op("pool", lambda: pool.memset(eps_t[:], EPS), W=[eps_t])
    op("pool", lambda: pool.memset(one_t[:], 1.0), W=[one_t])
    for h in range(4):
        for c4 in range(S // 512):
            dma("sp", kaug_h(h)[1][64:67, c4 * 512:(c4 + 1) * 512], onesb[0:3, :], onesb, kaug_h(h)[0], disjoint=True)

    flag_t = C.sb([128, 1], F32, dma=True)
    negbig = C.sb([128, 1], F32)
    dma("sp", flag_t[:], flag_in.t[:, :], flag_in, flag_t)
    op("dve", lambda: nc.vector.tensor_scalar(out=negbig[:], in0=flag_t[:], scalar1=-1.0, scalar2=30000.0, op0=ALU.add, op1=ALU.mult), R=[flag_t], W=[negbig])
    GF = C.sb([128, 2, 2], F32)
    YA0 = C.sb([128, 2, 2], F32)
    biasP = C.sb([128, NT, 4], F32)
    lbl = C.sb([128, 2, 4], F32, dma=True)
    for l4 in range(DEPTH):
        dma("sp", lbl[:, :, l4], lb_logits.t[l4].rearrange("(j p) -> p j", p=128), lb_logits, lbl, disjoint=(l4 > 0))
    lbe = C.sb([128, 2, 4], F32)
    lbs = C.sb([128, 2], F32)
    lbp = C.sb([128, 2, 4], F32)
    lbc = C.sb([128, 2, 4], F32)
    LB = C.sb([128, 2, 4], F32)
    OML = C.sb([128, 2, 4], F32)
    NOML = C.sb([128, 2, 4], F32)
    op("act", lambda: nc.scalar.activation(out=lbe[:], in_=lbl[:], func=AF.Exp), R=[lbl], W=[lbe])
    op("dve", lambda: nc.vector.reduce_sum(out=lbs[:], in_=lbe[:], axis=AX.X), R=[lbe], W=[lbs])
    op("dve", lambda: nc.vector.reciprocal(out=lbs[:], in_=lbs[:]), W=[lbs])
    op("dve", lambda: nc.vector.tensor_tensor(out=lbp[:], in0=lbe[:], in1=lbs[:].unsqueeze(2).to_broadcast([128, 2, 4]), op=ALU.mult), R=[lbe, lbs], W=[lbp])
    op("dve", lambda: nc.vector.memset(lbc[:, :, 0:1], 0.0), W=[lbc])
    op("dve", lambda: nc.vector.tensor_copy(out=lbc[:, :, 1:2], in_=lbp[:, :, 1:2]), R=[lbp], W=[lbc])
    op("dve", lambda: nc.vector.tensor_tensor(out=lbc[:, :, 2:3], in0=lbc[:, :, 1:2], in1=lbp[:, :, 2:3], op=ALU.add), R=[lbp], W=[lbc])
    op("dve", lambda: nc.vector.tensor_tensor(out=lbc[:, :, 3:4], in0=lbc[:, :, 2:3], in1=lbp[:, :, 3:4], op=ALU.add), R=[lbp], W=[lbc])
    op("dve", lambda: nc.vector.tensor_scalar(out=LB[:], in0=lbc[:], scalar1=0.0, scalar2=1.0, op0=ALU.max, op1=ALU.min), R=[lbc], W=[LB])
    op("dve", lambda: nc.vector.tensor_scalar(out=OML[:], in0=LB[:], scalar1=-1.0, scalar2=1.0, op0=ALU.mult, op1=ALU.add), R=[LB], W=[OML])
    op("dve", lambda: nc.vector.tensor_scalar(out=NOML[:], in0=LB[:], scalar1=-1.0, scalar2=None, op0=ALU.add), R=[LB], W=[NOML])

    WBIG = C.sb([128, 8 * NA], BF16, name="wbig")
    WB_A = Buf(WBIG.t)
    WB_T = Buf(WBIG.t)
    WB2 = C.sb([128, 10240], BF16, name="wb2")
    GMALL = C.sb([128, DEPTH, 16], F32, dma=True)
    for l4 in range(DEPTH):
        dma("sp", GMALL[:, l4, 0:8], norm_mix.t[l4].rearrange("(k p) -> p k", p=128), norm_mix, GMALL, disjoint=(l4 > 0))
        dma("sp", GMALL[:, l4, 8:16], norm_ple.t[l4].rearrange("(k p) -> p k", p=128), norm_ple, GMALL, disjoint=True)
    stg = Rot([C.sb([128, 1024], F32, dma=True) for _ in range(2)])
    banks = Rot([C.ps([128, 512], F32) for _ in range(4)])
    obanks = Rot([C.ps([128, 512], F32) for _ in range(2)])
    abanks = Rot(banks.items + obanks.items)
    tbanks = Rot([C.ps([128, 1024], BF16) for _ in range(2)])
    PRM = C.sb([128, 80], F32, dma=True)
    SG = C.sb([128, 256], F32, dma=True)
    WSP = C.sb([128, 4, 128], F32, dma=True)
    WT = C.sb([128, 4, 128], BF16)
    BT = C.sb([128, 2, 128], F32, dma=True)
    cposT = C.sb([128, NT, 4], F32)
    CV = C.sb([64, 3, 4, NCH], F32)

    cast_engs = Rot(["dve", "pool", "dve", "act"])

    def cast_mul(e, out, in_, sc):
        if e == "act":
            return nc.scalar.mul(out=out, in_=in_, mul=sc) if sc is not None else nc.scalar.copy(out=out, in_=in_)
        eng = nc.vector if e == "dve" else nc.gpsimd
        if sc is None:
            return eng.tensor_copy(out=out, in_=in_)
        return eng.tensor_scalar(out=out, in0=in_, scalar1=sc, scalar2=None, op0=ALU.mult)

    def load_w(dst_ap_fn, src_rows_fn, ncols, gain_col, wbufs, ks=range(8), engs=None, defer=None):
        for k in ks:
            c0 = 0
            while c0 < ncols:
                n = min(1024, ncols - c0)

                def emit(k=k, c0=c0, n=n):
                    st = stg.next()
                    srcb, srcap = src_rows_fn(k, c0, n)
                    dma("sp", st[:, 0:n], srcap, srcb, st)
                    e = cast_engs.next() if engs is None else engs.next()
                    g = gain_col(k) if gain_col is not None else None
                    d = dst_ap_fn(k, c0, n)
                    op(e, (lambda: cast_mul(e, d, st[:, 0:n], g)), R=[st, GMALL], Wd=list(wbufs))
                if defer is None:
                    emit()
                else:
                    defer.append(emit)
                c0 += n

    def drain(lst, n=None):
        cnt = 0
        while lst and (n is None or cnt < n):
            lst.pop(0)()
            cnt += 1

    def rms_rstd(src_ap, srcb, n, junk, st1):
        st = st1.next()
        op("act", lambda: nc.scalar.activation(out=junk[:, 0:n], in_=src_ap, func=AF.Square, accum_out=st[:, 0:1]), R=[srcb], W=[junk, st])
        op("act", lambda: nc.scalar.activation(out=st[:, 1:2], in_=st[:, 0:1], func=AF.Sqrt, bias=eps_t[:], scale=1.0 / n), R=[eps_t], W=[st])
        op("dve", lambda: nc.vector.reciprocal(out=st[:, 2:3], in_=st[:, 1:2]), W=[st])
        return st

    for l in range(n_layers):
        xsrc = x_in if l == 0 else xres
        xdst = y_out if l == n_layers - 1 else xres
        dma("sp", PRM[:, 0:8], norm_mix.t[l].rearrange("(k p) -> p k", p=128), norm_mix, PRM)
        dma("sp", PRM[:, 8:16], norm_ple.t[l].rearrange("(k p) -> p k", p=128), norm_ple, PRM, disjoint=True)
        for tp in range(3):
            dma("sp", PRM[:, 16:22].rearrange("p (j t) -> p j t", t=3)[:, :, tp], conv_w.t[l, tp].rearrange("(j p) -> p j", p=128), conv_w, PRM, disjoint=True)
        dma("sp", PRM[:, 22:24], conv_b.t[l].rearrange("(j p) -> p j", p=128), conv_b, PRM, disjoint=True)
        for hh in range(2):
            dma("sp", PRM[64 * hh:64 * hh + 64, 24:25], q_norm.t[l].rearrange("(p o) -> p o", o=1), q_norm, PRM, disjoint=True)
            dma("sp", PRM[64 * hh:64 * hh + 64, 25:26], k_norm.t[l].rearrange("(p o) -> p o", o=1), k_norm, PRM, disjoint=True)
        dma("sp", PRM[0:64, 32:36], hgrn_norm.t[l].rearrange("(h p) -> p h", p=64), hgrn_norm, PRM, disjoint=True)
        for br_ in range(4):
            dma("sp", PRM[:, 36 + 8 * br_:44 + 8 * br_], merge_b.t[l, br_].rearrange("(j p) -> p j", p=128), merge_b, PRM, disjoint=True)
        dma("sp", PRM[0:4, 68:69], fgate_bias.t[l].rearrange("(p o) -> p o", o=1), fgate_bias, PRM, disjoint=True)
        op("dve", lambda: nc.vector.tensor_scalar(out=PRM[:, 26:27], in0=PRM[:, 24:25], scalar1=0.125, scalar2=None, op0=ALU.mult), R=[PRM], Wd=[PRM])
        op("dve", lambda: nc.vector.tensor_scalar(out=PRM[0:4, 69:70], in0=PRM[0:4, 68:69], scalar1=-1.0, scalar2=None, op0=ALU.mult), R=[PRM], Wd=[PRM])
        gmix = lambda k, l=l: GMALL[:, l, k:k + 1]
        gple = lambda k, l=l: GMALL[:, l, 8 + k:9 + k]
        convw = lambda j, t: PRM[:, 16 + 3 * j + t:17 + 3 * j + t]
        convb = lambda j: PRM[:, 22 + j:23 + j]
        gq8 = PRM[:, 26:27]
        gk = PRM[:, 25:26]
        hgain = lambda h: PRM[0:64, 32 + h:33 + h]
        mergeb = lambda b, j: PRM[:, 36 + 8 * b + j:37 + 8 * b + j]
        nfgb = PRM[0:4, 69:70]
        lb_c = lambda j: LB[:, j, l:l + 1]
        oml_c = lambda j: OML[:, j, l:l + 1]
        noml_c = lambda j: NOML[:, j, l:l + 1]
        dma("sp", SG[:], sgu_norm.t[l:l + 1, :].partition_broadcast(128), sgu_norm, SG)
        dma("sp", WSP[:], spatial_w.t[l].rearrange("g t s -> t g s"), spatial_w, WSP)
        for gh in range(4):
            hh, j = gh % 2, gh // 2
            dma("sp", BT[64 * hh:64 * hh + 64, j, :], spatial_b.t[l, gh:gh + 1, :].partition_broadcast(64), spatial_b, BT, disjoint=(gh > 0))
        for gh in range(4):
            bk = abanks.next()
            op("pe", lambda bk=bk, gh=gh: nc.tensor.transpose(out=bk[:, 0:128], in_=WSP[:, gh, :], identity=ident_f[:]), R=[WSP, ident_f], W=[bk])
            op("dve", lambda bk=bk, gh=gh: nc.vector.scalar_tensor_tensor(out=WT[:, gh, :], in0=maskT[:], scalar=-1.0, in1=bk[:, 0:128], op0=ALU.is_gt, op1=ALU.mult), R=[bk, maskT], Wd=[WT])

        WA = WBIG[:, 0:8 * NA].rearrange("p (k c) -> p k c", k=8)

        def load_wa(lw, ks, defer=None, engs=None):
            load_w(lambda k, c0, n: WA[:, k, c0:c0 + n],
                   lambda k, c0, n: (w_in, w_in.t[lw, k * 128:(k + 1) * 128, c0:c0 + n]),
                   NA, (lambda k: GMALL[:, lw, k:k + 1]), ([WB_A] if max(ks) < 5 else [WB_A, WB_T]), ks=ks, defer=defer, engs=engs)
        if l == 0:
            load_wa(0, range(0, 5))
        else:
            drain(pre_a)
        load_wa(l, range(5, 8))
        pre_a = []

        ph = Phase(C)
        xin_r = Rot([C.sb([128, 1024], F32, dma=True) for _ in range(2)])
        junk = C.sb([128, 1024], BF16)
        hb_r = Rot([C.sb([128, 1024], BF16) for _ in range(2)])
        st1 = Rot([C.sb([128, 4], F32) for _ in range(4)])
        hTg_r = Rot([C.sb([128, 8, 512], BF16, dma=True) for _ in range(2)])
        f32w = Rot([C.sb([128, 512], F32) for _ in range(8)])
        bfw = Rot([C.sb([128, 512], BF16, dma=True) for _ in range(8)])
        zc_r = [C.sb([128, 514], F32, dma=True) for _ in range(2)]
        cp_r = Rot([C.sb([4, 512], F32) for _ in range(2)])
        cpx = Rot([C.sb([4, 512], F32) for _ in range(3)])
        cpb = Rot([C.sb([4, 512], BF16, dma=True) for _ in range(6)])
        ug_r = Rot([C.sb([128, 2, 512], F32) for _ in range(2)])
        ydb_r = Rot([C.sb([128, 2, 512], BF16, dma=True) for _ in range(2)])
        tokw = Rot([C.sb([128, 256], F32) for _ in range(4)])
        tokb = Rot([C.sb([128, 256], BF16, dma=True) for _ in range(4)])
        vnb_r = Rot([C.sb([128, 256], BF16) for _ in range(4)])
        vhb = Rot([C.sb([64, 2, 256], BF16, dma=True) for _ in range(3)])

        for j in range(2):
            op("pool", lambda j=j: nc.gpsimd.memset(zc_r[j][:, 0:2], 0.0), W=[zc_r[j]])
        prev_cp = None

        def norm_tile(ti):
            tok = slice(ti * 128, (ti + 1) * 128)
            xin = xin_r.next()
            dma("sp", xin[:], xsrc.t[tok, :], xsrc, xin)
            st = rms_rstd(xin[:], xin, 1024, junk, st1)
            hb = hb_r.next()
            op("dve", lambda: nc.vector.tensor_scalar(out=hb[:], in0=xin[:], scalar1=st[:, 2:3], scalar2=None, op0=ALU.mult), R=[xin, st], W=[hb])
            return hb

        hb_next = norm_tile(0)
        for g in range(NG):
            cols = slice(g * 512, (g + 1) * 512)
            hTg = hTg_r.next()
            for t in range(4):
                hb = hb_next
                if g * 4 + t + 1 < NT:
                    hb_next = norm_tile(g * 4 + t + 1)
                tb = tbanks.next()
                for k in range(8):
                    op("pe", lambda k=k, tb=tb, hb=hb: nc.tensor.transpose(out=tb[:, k * 128:(k + 1) * 128], in_=hb[:, k * 128:(k + 1) * 128], identity=ident_bf[:]), R=[hb, ident_bf], W=[tb], inc=(k == 7))
                op("act", lambda tb=tb, hTg=hTg, t=t: nc.scalar.copy(out=hTg[:, :, t * 128:(t + 1) * 128], in_=tb[:].rearrange("p (k t) -> p k t", k=8)), R=[tb], Wd=[hTg])
            dma("act", hT_d.t.rearrange("(k p) s -> p k s", p=128)[:, :, cols], hTg[:], hTg, hT_d, disjoint=True)

            def fm(c0, M=128):
                bk = abanks.next()
                for k in range(8):
                    op("pe", lambda k=k, bk=bk: nc.tensor.matmul(bk[0:M, :], lhsT=WA[:, k, c0:c0 + M], rhs=hTg[:, k, :], start=(k == 0), stop=(k == 7)), R=[WB_A, WB_T, hTg], W=[bk], inc=(k == 7))
                return bk

            def store(dst, row0, nrows, srcbuf, src_ap, q="pool"):
                dma(q, dst.t[row0:row0 + nrows, cols], src_ap, srcbuf, dst, disjoint=True)

            for j in range(2):
                bx = fm(A_X + 128 * j)
                tmp = f32w.next()
                op("act", lambda: nc.scalar.copy(out=tmp[:], in_=bx[:]), R=[bx], W=[tmp])
                bc = fm(A_C + 128 * j)
                zc = zc_r[j]
                op("dve", lambda: nc.vector.tensor_tensor(out=zc[:, 2:514], in0=tmp[:], in1=bc[:], op=ALU.mult), R=[tmp, bc], W=[zc])
                a1 = f32w.next()
                a2 = f32w.next()
                op("dve", lambda: nc.vector.tensor_scalar(out=a1[:], in0=zc[:, 2:514], scalar1=convw(j, 2), scalar2=convb(j), op0=ALU.mult, op1=ALU.add), R=[zc, PRM], W=[a1])
                op("dve", lambda: nc.vector.scalar_tensor_tensor(out=a2[:], in0=zc[:, 1:513], scalar=convw(j, 1), in1=a1[:], op0=ALU.mult, op1=ALU.add), R=[zc, a1, PRM], W=[a2])
                op("dve", lambda: nc.vector.scalar_tensor_tensor(out=a1[:], in0=zc[:, 0:512], scalar=convw(j, 0), in1=a2[:], op0=ALU.mult, op1=ALU.add), R=[zc, a2, PRM], W=[a1])
                op("pool", lambda: nc.gpsimd.tensor_copy(out=zc[:, 0:2], in_=zc[:, 512:514]), W=[zc])
                bb = fm(A_B + 128 * j)
                op("dve", lambda: nc.vector.tensor_tensor(out=a2[:], in0=a1[:], in1=bb[:], op=ALU.mult), R=[a1, bb], W=[a2])
                bg = fm(A_G + 128 * j)
                sg_ = f32w.next()
                op("act", lambda: nc.scalar.activation(out=sg_[:], in_=bg[:], func=AF.Silu), R=[bg], W=[sg_])
                yb_ = bfw.next()
                op("dve", lambda: nc.vector.tensor_tensor(out=yb_[:], in0=a2[:], in1=sg_[:], op=ALU.mult), R=[a2, sg_], W=[yb_])
                if g == 0:
                    op("dve", lambda: nc.vector.tensor_tensor(out=GF[:, j, :], in0=bb[:, 0:2], in1=sg_[:, 0:2], op=ALU.mult), R=[bb, sg_], Wd=[GF])
                    op("dve", lambda: nc.vector.tensor_tensor(out=YA0[:, j, :], in0=a2[:, 0:2], in1=sg_[:, 0:2], op=ALU.mult), R=[a2, sg_], Wd=[YA0])
                if g == NG - 1:
                    dma("pool", xs2.t[:, 128 + 2 * j:130 + 2 * j], zc[:, 512:514], zc, xs2, disjoint=True)
                store(y_d[0], 128 * j, 128, yb_, yb_[:])

            for (c0, gcol, dstf) in ((B_Q, gq8, qaug_h), (B_K, gk, kaug_h)):
                for j in range(2):
                    bq = fm(c0 + 128 * j)
                    sq = bfw.next()
                    op("act", lambda: nc.scalar.activation(out=sq[:], in_=bq[:], func=AF.Square), R=[bq], W=[sq])
                    b2 = abanks.next()
                    op("pe", lambda: nc.tensor.matmul(b2[:], lhsT=blockones[:], rhs=sq[:], start=True, stop=True), R=[blockones, sq], W=[b2])
                    rt = f32w.next()
                    op("act", lambda: nc.scalar.activation(out=rt[:], in_=b2[:], func=AF.Ln, bias=eps_t[:], scale=1.0 / 64), R=[b2, eps_t], W=[rt])
                    op("act", lambda: nc.scalar.activation(out=rt[:], in_=rt[:], func=AF.Exp, scale=-0.5), W=[rt])
                    qn = bfw.next()
                    op("dve", lambda: nc.vector.scalar_tensor_tensor(out=qn[:], in0=bq[:], scalar=gcol, in1=rt[:], op0=ALU.mult, op1=ALU.mult), R=[bq, rt, PRM], W=[qn])
                    for hh in range(2):
                        dstb, dstap = dstf(2 * j + hh)
                        dma("pool", dstap[0:64, cols], qn[64 * hh:64 * hh + 64, :], qn, dstb, disjoint=True)
            for j in range(2):
                bg = fm(B_G + 128 * j)
                gb = bfw.next()
                op("act", lambda: nc.scalar.activation(out=gb[:], in_=bg[:], func=AF.Silu), R=[bg], W=[gb])
                store(gB_d, 128 * j, 128, gb, gb[:], q="act")
            bf_ = fm(B_F, M=4)
            e_ = cpx.next()
            op("act", lambda: nc.scalar.activation(out=e_[:], in_=bf_[0:4, :], func=AF.Exp, bias=nfgb, scale=-1.0), R=[bf_, PRM], W=[e_])
            sp_ = cpx.next()
            op("act", lambda: nc.scalar.activation(out=sp_[:], in_=e_[:], func=AF.Ln, bias=one_t[0:4, :], scale=1.0), R=[e_, one_t], W=[sp_])
            cp = cp_r.next()
            init = 0.0 if prev_cp is None else prev_cp[:, 511:512]
            rr_ = [ones_f, sp_] + ([prev_cp] if prev_cp is not None else [])
            op("dve", lambda: nc.vector.tensor_tensor_scan(out=cp[:], data0=ones_f[0:4, :], data1=sp_[:], initial=init, op0=ALU.mult, op1=ALU.add), R=rr_, W=[cp])
            prev_cp = cp
            hi, mid, lo = cpb.next(), cpb.next(), cpb.next()
            r1, r2 = cpx.next(), e_
            op("dve", lambda: nc.vector.tensor_scalar(out=hi[:], in0=cp[:], scalar1=-1.0, scalar2=None, op0=ALU.mult), R=[cp], W=[hi])
            op("dve", lambda: nc.vector.scalar_tensor_tensor(out=r1[:], in0=cp[:], scalar=-1.0, in1=hi[:], op0=ALU.mult, op1=ALU.subtract), R=[cp, hi], W=[r1])
            op("dve", lambda: nc.vector.tensor_copy(out=mid[:], in_=r1[:]), R=[r1], W=[mid])
            op("dve", lambda: nc.vector.tensor_tensor(out=r2[:], in0=r1[:], in1=mid[:], op=ALU.subtract), R=[r1, mid], W=[r2])
            op("dve", lambda: nc.vector.tensor_copy(out=lo[:], in_=r2[:]), R=[r2], W=[lo])
            for i, bb_ in enumerate((hi, mid, lo)):
                dma("pool", qaug_d.t[:, 64 + i, cols], bb_[:], bb_, qaug_d, disjoint=True)
            def cpos_transposes(cp=cp, g=g):
                for t in range(4):
                    bk = abanks.next()
                    op("pe", lambda: nc.tensor.transpose(out=bk[:, 0:4], in_=cp[0:4, t * 128:(t + 1) * 128], identity=ident_f[0:4, 0:4]), R=[cp, ident_f], W=[bk])
                    op("act", lambda: nc.scalar.copy(out=cposT[:, g * 4 + t, :], in_=bk[:, 0:4]), R=[bk], Wd=[cposT])

            for j in range(2):
                bff = fm(C_F + 128 * j)
                sig = f32w.next()
                op("act", lambda: nc.scalar.activation(out=sig[:], in_=bff[:], func=AF.Sigmoid), R=[bff], W=[sig])
                gl = f32w.next()
                op("dve", lambda: nc.vector.tensor_scalar(out=gl[:], in0=sig[:], scalar1=oml_c(j), scalar2=lb_c(j), op0=ALU.mult, op1=ALU.add), R=[sig, OML, LB], W=[gl])
                op("act", lambda: nc.scalar.activation(out=gl[:], in_=gl[:], func=AF.Ln), W=[gl])
                kf = f32w.next()
                op("pool", lambda: nc.gpsimd.tensor_scalar(out=kf[:], in0=sig[:], scalar1=noml_c(j), scalar2=oml_c(j), op0=ALU.mult, op1=ALU.add), R=[sig, OML, NOML], W=[kf])
                b_ = f32w.next()
                op("dve", lambda: nc.vector.tensor_tensor_scan(out=b_[:], data0=resetm[:], data1=gl[:], initial=0.0, op0=ALU.mult, op1=ALU.add), R=[resetm, gl], W=[b_])
                b3 = b_[:].rearrange("p (c t) -> p c t", t=64)
                bm = f32w.next()
                bm3 = bm[:].rearrange("p (c t) -> p c t", t=64)
                op("dve", lambda: nc.vector.tensor_tensor(out=bm3, in0=b3, in1=b3[:, :, 32:33].to_broadcast([128, 8, 64]), op=ALU.subtract), R=[b_], W=[bm])
                e1 = f32w.next()
                op("act", lambda: nc.scalar.activation(out=e1[:], in_=bm[:], func=AF.Exp), R=[bm], W=[e1])
                e2 = gl
                op("act", lambda: nc.scalar.activation(out=e2[:], in_=bm[:], func=AF.Exp, scale=-1.0), R=[bm], W=[e2])
                for hh in range(2):
                    hd = 2 * j + hh
                    pr = slice(64 * hh, 64 * hh + 64)
                    ch = slice(g * 8, g * 8 + 8)
                    op("act", lambda: nc.scalar.activation(out=CV[:, 0, hd, ch], in_=b3[pr, :, 32], func=AF.Exp), R=[b_], Wd=[CV])
                    op("act", lambda: nc.scalar.activation(out=CV[:, 1, hd, ch], in_=b3[pr, :, 63], func=AF.Exp), R=[b_], Wd=[CV])
                    op("act", lambda: nc.scalar.activation(out=CV[:, 2, hd, ch], in_=bm3[pr, :, 63], func=AF.Exp), R=[bm], Wd=[CV])
                bqq = fm(C_Q + 128 * j)
                sq_ = sig
                op("act", lambda: nc.scalar.activation(out=sq_[:], in_=bqq[:], func=AF.Silu), R=[bqq], W=[sq_])
                Qp = bfw.next()
                op("dve", lambda: nc.vector.tensor_tensor(out=Qp[:], in0=sq_[:], in1=e1[:], op=ALU.mult), R=[sq_, e1], W=[Qp])
                store(Qp_d, 128 * j, 128, Qp, Qp[:])
                Kp = bfw.next()
                op("pool", lambda: nc.gpsimd.tensor_tensor(out=Kp[:], in0=kf[:], in1=e2[:], op=ALU.mult), R=[kf, e2], W=[Kp])
                store(Kp_d, 128 * j, 128, Kp, Kp[:])
                bgc = fm(C_G + 128 * j)
                gc = bfw.next()
                op("act", lambda: nc.scalar.activation(out=gc[:], in_=bgc[:], func=AF.Silu), R=[bgc], W=[gc])
                store(gC_d, 128 * j, 128, gc, gc[:], q="act")

            ug = ug_r.next()
            for j in range(2):
                bu = fm(D_U + 128 * j)
                us = f32w.next()
                op("act", lambda: nc.scalar.copy(out=us[:], in_=bu[:]), R=[bu], W=[us])
                bgd = fm(D_G + 128 * j)
                gd = f32w.next()
                op("act", lambda: nc.scalar.activation(out=gd[:], in_=bgd[:], func=AF.Silu), R=[bgd], W=[gd])
                op("pool", lambda: nc.gpsimd.tensor_tensor(out=ug[:, j, :], in0=us[:], in1=gd[:], op=ALU.mult), R=[us, gd], Wd=[ug])
            ydb = ydb_r.next()
            vnbs = []
            for t in range(4):
                tsl = slice(t * 128, (t + 1) * 128)
                bv = abanks.next()
                for k in range(8):
                    op("pe", lambda k=k: nc.tensor.matmul(bv[:, 0:256], lhsT=hTg[:, k, tsl], rhs=WA[:, k, D_V:D_V + 256], start=(k == 0), stop=(k == 7)), R=[WB_A, WB_T, hTg], W=[bv], inc=(k == 7))
                sqv = tokw.next()
                op("act", lambda: nc.scalar.activation(out=sqv[:], in_=bv[:, 0:256], func=AF.Square), R=[bv], W=[sqv])
                stv = st1.next()
                op("dve", lambda: nc.vector.reduce_sum(out=stv[:, 0:4], in_=sqv[:].rearrange("p (h c) -> p h c", c=64), axis=AX.X), R=[sqv], W=[stv])
                op("act", lambda: nc.scalar.activation(out=stv[:, 0:4], in_=stv[:, 0:4], func=AF.Sqrt, bias=eps_t[:], scale=1.0 / 64), R=[eps_t], W=[stv])
                op("dve", lambda: nc.vector.reciprocal(out=stv[:, 0:4], in_=stv[:, 0:4]), W=[stv])
                vn = tokw.next()
                op("dve", lambda: nc.vector.tensor_tensor(out=vn[:].rearrange("p (h c) -> p h c", c=64), in0=bv[:, 0:256].rearrange("p (h c) -> p h c", c=64), in1=stv[:, 0:4].unsqueeze(2).to_broadcast([128, 4, 64]), op=ALU.mult), R=[bv, stv], W=[vn])
                vnb = vnb_r.next()
                op("dve", lambda: nc.vector.tensor_tensor(out=vnb[:], in0=vn[:], in1=SG[:], op=ALU.mult), R=[vn, SG], W=[vnb])
                vnbs.append(vnb)
            for t in range(4):
                tsl = slice(t * 128, (t + 1) * 128)
                tok = slice(g * 512 + t * 128, g * 512 + (t + 1) * 128)
                bv2 = abanks.next()
                for k in range(8):
                    op("pe", lambda k=k: nc.tensor.matmul(bv2[:, 0:256], lhsT=hTg[:, k, tsl], rhs=WA[:, k, B_V:B_V + 256], start=(k == 0), stop=(k == 7)), R=[WB_A, WB_T, hTg], W=[bv2], inc=(k == 7))
                vb = tokb.next()
                op("act", lambda: nc.scalar.copy(out=vb[:], in_=bv2[:, 0:256]), R=[bv2], W=[vb])
                dma("act", vtok_d.t[tok, :], vb[:], vb, vtok_d, disjoint=True)
                bv3 = abanks.next()
                for c in range(2):
                    csl = slice(t * 128 + c * 64, t * 128 + (c + 1) * 64)
                    for k in range(8):
                        op("pe", lambda k=k, c=c, csl=csl: nc.tensor.matmul(bv3[0:64, c * 256:(c + 1) * 256], lhsT=hTg[:, k, csl], rhs=WA[:, k, C_I:C_I + 256], start=(k == 0), stop=(k == 7)), R=[WB_A, WB_T, hTg], W=[bv3], inc=(k == 7 and c == 1))
                vh = vhb.next()
                op("act", lambda: nc.scalar.copy(out=vh[:].rearrange("p c f -> p (c f)"), in_=bv3[0:64, :]), R=[bv3], W=[vh])
                dma("act", vH_d.t[tok, :].rearrange("(c s) f -> s c f", s=64), vh[:], vh, vH_d, disjoint=True)
            for t in range(4):
                tsl = slice(t * 128, (t + 1) * 128)
                vnb = vnbs[t]
                bs = abanks.next()
                for gh in range(4):
                    hh, j = gh % 2, gh // 2
                    op("pe", lambda gh=gh, hh=hh, j=j: nc.tensor.matmul(bs[64 * hh:64 * hh + 64, j * 128:(j + 1) * 128], lhsT=vnb[:, 64 * gh:64 * gh + 64], rhs=WT[:, gh, :], start=True, stop=True), R=[vnb, WT], W=[bs], inc=(gh == 3))
                ts_ = tokw.next()
                op("dve", lambda: nc.vector.tensor_tensor(out=ts_[:], in0=bs[:, 0:256], in1=BT[:].rearrange("p j t -> p (j t)"), op=ALU.add), R=[bs, BT], W=[ts_])
                op("dve", lambda: nc.vector.tensor_tensor(out=ydb[:, :, tsl], in0=ts_[:].rearrange("p (j t) -> p j t", j=2), in1=ug[:, :, tsl], op=ALU.mult), R=[ts_, ug], Wd=[ydb])
            for j in range(2):
                store(y_d[3], 128 * j, 128, ydb, ydb[:, j, :])
            cpos_transposes()

        dma("pool", xs2.t[:, 0:128], cposT[:].rearrange("p t h -> p (t h)"), cposT, xs2, disjoint=True, sem_owner=flag_t)
        ph.close()
        C.collective(xs1a, xg1a)
        C.collective(xs1b, xg1b)
        C.collective(xs1c, xg1c)
        C.collective(xs2, xg2)
        def phase_H(final):
            ph = Phase(C)
            Sst = C.sb([64, 4, 64], F32, dma=True)
            Sin = C.sb([64, 4, 64], F32, dma=True)
            Sbf_r = Rot([C.sb([64, 4, 64], BF16) for _ in range(10)])
            Qg_r = Rot([C.sb([64, 4, 512], BF16, dma=True) for _ in range(2)])
            Kg_r = Rot([C.sb([64, 4, 512], BF16, dma=True) for _ in range(2)])
            Gg_r = Rot([C.sb([64, 4, 512], BF16, dma=True) for _ in range(2)])
            Vg_r = Rot([C.sb([64, 8, 256], BF16, dma=True) for _ in range(2)])
            Kt_r = Rot([C.sb([64, 4, 64], BF16) for _ in range(8)])
            UT_r = Rot([C.sb([64, 8, 256], F32, dma=True) for _ in range(2)])
            ST_r = Rot([C.sb([64, 8, 64], BF16) for _ in range(2)])
            hq_r = Rot([C.sb([64, 512], BF16) for _ in range(2)])
            hr_r = Rot([C.sb([64, 512], F32) for _ in range(2)])
            hy_r = Rot([C.sb([64, 512], F32) for _ in range(2)])
            hyb_r = Rot([C.sb([64, 512], BF16, dma=True) for _ in range(2)])
            def scan_group(g, final):
                cols = slice(g * 512, (g + 1) * 512)
                Kg, Vg = Kg_r.next(), Vg_r.next()
                dma("sp", Kg[:], Kp_d.t[:, cols].rearrange("(h p) t -> p h t", p=64), Kp_d, Kg)
                dma("sp", Vg[:], vH_d.t[cols, :].rearrange("(c s) f -> s c f", s=64), vH_d, Vg)
                UT = UT_r.next()
                Sbfs = []
                if not final:
                    Kts = []
                    for c in range(8):
                        csl = slice(c * 64, (c + 1) * 64)
                        tb = tbanks.next()
                        for h in range(4):
                            op("pe", lambda h=h: nc.tensor.transpose(out=tb[0:64, h * 64:(h + 1) * 64], in_=Kg[:, h, csl], identity=ident_bf[0:64, 0:64]), R=[Kg, ident_bf], W=[tb], inc=(h == 3))
                        Kt = Kt_r.next()
                        op("act", lambda: nc.scalar.copy(out=Kt[:].rearrange("p h k -> p (h k)"), in_=tb[0:64, 0:256]), R=[tb], W=[Kt])
                        Kts.append(Kt)
                    for c in range(8):
                        chn = g * 8 + c
                        Kt = Kts[c]
                        UB = abanks.next()
                        for h in range(4):
                            op("pe", lambda h=h: nc.tensor.matmul(UB[0:64, h * 64:(h + 1) * 64], lhsT=Kt[:, h, :], rhs=Vg[:, c, 64 * h:64 * h + 64], start=True, stop=True), R=[Kt, Vg], W=[UB], inc=(h == 3))
                        op("dve", lambda: nc.vector.tensor_tensor(out=UT[:, c, :].rearrange("p (h v) -> p h v", h=4), in0=UB[0:64, 0:256].rearrange("p (h v) -> p h v", h=4), in1=CV[:, 2, :, chn:chn + 1].to_broadcast([64, 4, 64]), op=ALU.mult), R=[UB, CV], Wd=[UT])
                    dma("pool", ut_d.t[:, g * 8:(g + 1) * 8, :], UT[:], UT, ut_d, disjoint=True)
                else:
                    Qg, Gg = Qg_r.next(), Gg_r.next()
                    dma("sp", Qg[:], Qp_d.t[:, cols].rearrange("(h p) t -> p h t", p=64), Qp_d, Qg)
                    dma("sp", Gg[:], gC_d.t[:, cols].rearrange("(h p) t -> p h t", p=64), gC_d, Gg)
                    dma("sp", UT[:], ut_d.t[:, g * 8:(g + 1) * 8, :], ut_d, UT)
                for c in range(8):
                    chn = g * 8 + c
                    if final:
                        Sbf = Sbf_r.next()
                        op("pool", lambda: nc.gpsimd.tensor_tensor(out=Sbf[:], in0=Sst[:], in1=CV[:, 0, :, chn:chn + 1].to_broadcast([64, 4, 64]), op=ALU.mult), R=[Sst, CV], W=[Sbf])
                        Sbfs.append(Sbf)
                    op("dve", lambda: nc.vector.tensor_tensor(out=Sst[:], in0=Sst[:], in1=CV[:, 1, :, chn:chn + 1].to_broadcast([64, 4, 64]), op=ALU.mult), R=[CV], W=[Sst])
                    op("dve", lambda: nc.vector.tensor_tensor(out=Sst[:], in0=Sst[:], in1=UT[:, c, :].rearrange("p (h v) -> p h v", h=4), op=ALU.add), R=[UT], W=[Sst])
                if not final:
                    return
                for hp in range(2):
                    hs = (2 * hp, 2 * hp + 1)
                    SBk, STh, OBk, hq, b2, hr, hy, hyb = {}, {}, {}, {}, {}, {}, {}, {}
                    for h in hs:
                        SBk[h] = abanks.next()
                        for c in range(8):
                            csl = slice(c * 64, (c + 1) * 64)
                            op("pe", lambda c=c, csl=csl: nc.tensor.matmul(SBk[h][0:64, csl], lhsT=Kg[:, h, csl], rhs=Qg[:, h, csl], start=True, stop=True), R=[Kg, Qg], W=[SBk[h]], inc=(c == 7))
                    for h in hs:
                        STh[h] = ST_r.next()
                        op("dve", lambda: nc.vector.tensor_tensor(out=STh[h][:], in0=SBk[h][0:64, :].rearrange("p (c t) -> p c t", t=64), in1=tri[:].unsqueeze(1).to_broadcast([64, 8, 64]), op=ALU.mult), R=[SBk[h], tri], W=[STh[h]])
                    for h in hs:
                        OBk[h] = abanks.next()
                        for c in range(8):
                            csl = slice(c * 64, (c + 1) * 64)
                            op("pe", lambda c=c, csl=csl: nc.tensor.matmul(OBk[h][0:64, csl], lhsT=Sbfs[c][:, h, :], rhs=Qg[:, h, csl], start=True, stop=False), R=[Sbfs[c], Qg], W=[OBk[h]], inc=False)
                            op("pe", lambda c=c, csl=csl: nc.tensor.matmul(OBk[h][0:64, csl], lhsT=Vg[:, c, 64 * h:64 * h + 64], rhs=STh[h][:, c, :], start=False, stop=True), R=[Vg, STh[h]], W=[OBk[h]], inc=(c == 7))
                    for h in hs:
                        hq[h] = hq_r.next()
                        op("act", lambda: nc.scalar.activation(out=hq[h][:], in_=OBk[h][0:64, :], func=AF.Square), R=[OBk[h]], W=[hq[h]])
                    for h in hs:
                        b2[h] = abanks.next()
                        op("pe", lambda: nc.tensor.matmul(b2[h][0:64, :], lhsT=blockones[0:64, 0:64], rhs=hq[h][:], start=True, stop=True), R=[blockones, hq[h]], W=[b2[h]])
                    for h in hs:
                        hr[h] = hr_r.next()
                        op("act", lambda: nc.scalar.activation(out=hr[h][:], in_=b2[h][0:64, :], func=AF.Ln, bias=eps_t[0:64, :], scale=1.0 / 64), R=[b2[h], eps_t], W=[hr[h]])
                        op("act", lambda: nc.scalar.activation(out=hr[h][:], in_=hr[h][:], func=AF.Exp, scale=-0.5), W=[hr[h]])
                    for h in hs:
                        hy[h] = hy_r.next()
                        op("dve", lambda: nc.vector.scalar_tensor_tensor(out=hy[h][:], in0=OBk[h][0:64, :], scalar=hgain(h), in1=hr[h][:], op0=ALU.mult, op1=ALU.mult), R=[OBk[h], hr[h], PRM], W=[hy[h]])
                        hyb[h] = hyb_r.next()
                        op("pool", lambda: nc.gpsimd.tensor_tensor(out=hyb[h][:], in0=hy[h][:], in1=Gg[:, h, :], op=ALU.mult), R=[hy[h], Gg], W=[hyb[h]])
                        dma("pool", y_d[2].t[64 * h:64 * h + 64, cols], hyb[h][:], hyb[h], y_d[2], disjoint=True)

            if not final:
                op("dve", lambda: nc.vector.memset(Sst[:], 0.0), W=[Sst])
                for g in range(NG):
                    scan_group(g, False)
                dma("pool", xs3.t[:, :], Sst[:].rearrange("p h v -> p (h v)"), Sst, xs3)
                C.collective(xs3, xg3)
            else:
                dma("sp", Sin[:].rearrange("p h v -> p (h v)"), xg3.t[0:64, :], xg3, Sin)
                op("dve", lambda: nc.vector.tensor_scalar(out=Sst[:], in0=Sin[:], scalar1=flag_t[0:64, :], scalar2=None, op0=ALU.mult), R=[Sin, flag_t], W=[Sst])
                for g in range(NG):
                    scan_group(g, True)
            ph.close()
        phase_H(False)
        ph = Phase(C)
        R2 = C.sb([128, 132], F32, dma=True)
        Rtot = C.sb([128, 4], F32, dma=True)
        dma("sp", R2[:], xg2.t[0:128, :], xg2, R2)
        dma("sp", Rtot[:], xg2.t[127:128, 4 * NT - 4:4 * NT].partition_broadcast(128), xg2, Rtot)
        op("dve", lambda: nc.vector.tensor_tensor(out=biasP[:], in0=R2[:, 0:128].rearrange("p (t h) -> p t h", h=4), in1=Rtot[:].unsqueeze(1).to_broadcast([128, NT, 4]), op=ALU.subtract), R=[R2, Rtot], W=[biasP])
        op("dve", lambda: nc.vector.tensor_scalar(out=biasP[:], in0=biasP[:], scalar1=negbig[:], scalar2=None, op0=ALU.add), R=[negbig], W=[biasP])
        Hh = C.sb([128, 4], F32)
        op("dve", lambda: nc.vector.tensor_scalar(out=Hh[:], in0=R2[:, 128:132], scalar1=flag_t[:], scalar2=None, op0=ALU.mult), R=[R2, flag_t], W=[Hh])
        for j in range(2):
            cc = C.sb([128, 4], F32)
            yfx = C.sb([128, 2], BF16, dma=True)
            op("dve", lambda: nc.vector.tensor_scalar(out=cc[:, 0:1], in0=Hh[:, 2 * j:2 * j + 1], scalar1=convw(j, 0), scalar2=None, op0=ALU.mult), R=[Hh, PRM], W=[cc])
            op("dve", lambda: nc.vector.scalar_tensor_tensor(out=cc[:, 2:3], in0=Hh[:, 2 * j + 1:2 * j + 2], scalar=convw(j, 1), in1=cc[:, 0:1], op0=ALU.mult, op1=ALU.add), R=[Hh, PRM], W=[cc])
            op("dve", lambda: nc.vector.tensor_scalar(out=cc[:, 3:4], in0=Hh[:, 2 * j + 1:2 * j + 2], scalar1=convw(j, 0), scalar2=None, op0=ALU.mult), R=[Hh, PRM], W=[cc])
            op("dve", lambda: nc.vector.tensor_tensor(out=cc[:, 0:2], in0=cc[:, 2:4], in1=GF[:, j, :], op=ALU.mult), R=[GF], W=[cc])
            op("dve", lambda: nc.vector.tensor_tensor(out=yfx[:], in0=cc[:, 0:2], in1=YA0[:, j, :], op=ALU.add), R=[cc, YA0], W=[yfx])
            dma("pool", y_d[0].t[128 * j:128 * j + 128, 0:2], yfx[:], yfx, y_d[0], disjoint=True)
        ph.close()
        WG = WBIG[:, 0:8 * 2048].rearrange("p (k c) -> p k c", k=8)
        WU = WBIG[:, 8 * 2048:8 * 2048 + 8 * 512].rearrange("p (k c) -> p k c", k=8)

        def load_c1(hf, engs, defer=None, WGd=None, WUd=None, wb=None):
            WGd = WG if WGd is None else WGd
            WUd = WU if WUd is None else WUd
            wb = [WB_A] if wb is None else wb
            for br in range(4):
                load_w(lambda k, c0, n, br=br: WGd[:, k, br * 512 + c0:br * 512 + c0 + n],
                       lambda k, c0, n, br=br: (w_in, w_in.t[l, k * 128:(k + 1) * 128, M_G + br * 1024 + hf * 512 + c0:M_G + br * 1024 + hf * 512 + c0 + n]),
                       512, gmix, wb, engs=engs, defer=defer)
            load_w(lambda k, c0, n: WUd[:, k, c0:c0 + n],
                   lambda k, c0, n: (w_up, w_up.t[l, k // 2, (k % 2) * 128:(k % 2) * 128 + 128, hf * 512 + c0:hf * 512 + c0 + n]), 512, None, wb, engs=engs, defer=defer)

        pre_c1 = []
        load_c1(0, Rot(["dve", "pool"]), pre_c1)
        ph = Phase(C)
        Kaug = C.sb([128, S], BF16, dma=True)
        Vp = C.sb([128, S], BF16, dma=True)
        KaugP = C.sb([128, S], BF16, dma=True)
        VpP = C.sb([128, S], BF16, dma=True)
        VpP3 = VpP.t.rearrange("p (t f) -> p t f", f=128)
        op("pool", lambda: nc.gpsimd.memset(VpP3[:, :, 64:128], 1.0), Wd=[VpP])
        Vp3 = Vp.t.rearrange("p (t f) -> p t f", f=128)
        qa_r = Rot([C.sb([67, 512], BF16, dma=True) for _ in range(2)])
        gbt_r = Rot([C.sb([64, 512], BF16, dma=True) for _ in range(2)])
        P_r = Rot([C.sb([128, 512], BF16) for _ in range(4)])
        rd_r = Rot([C.sb([64, 512], F32) for _ in range(2)])
        of_r = Rot([C.sb([64, 512], F32) for _ in range(2)])
        yo_r = Rot([C.sb([64, 512], BF16, dma=True) for _ in range(2)])
        op("pool", lambda: nc.gpsimd.memset(Vp3[:, :, 64:128], 1.0), Wd=[Vp])
        for h in range(4):
            dma("sp", Kaug.t[0:67, :], kaug_h(h)[1], kaug_h(h)[0], Kaug)
            dma("sp", KaugP.t[0:67, :], kaugP_h(h)[1], kaugP_h(h)[0], KaugP)
            for c4 in range(S // 2048):
                dma("sp", Vp3[:, c4 * 16:(c4 + 1) * 16, 0:64], vtok_d.t[c4 * 2048:(c4 + 1) * 2048, 64 * h:64 * h + 64].rearrange("(t p) f -> p t f", p=128), vtok_d, Vp, disjoint=True)
                dma("sp", VpP3[:, c4 * 16:(c4 + 1) * 16, 0:64], vtokP_d.t[c4 * 2048:(c4 + 1) * 2048, 64 * h:64 * h + 64].rearrange("(t p) f -> p t f", p=128), vtokP_d, VpP, disjoint=True)
            pending = []

            def flush():
                while pending:
                    pending.pop(0)()

            for qg in range(NG):
                cols = slice(qg * 512, (qg + 1) * 512)
                qa = qa_r.next()
                dma("sp", qa[:], qaug_d.t[h, :, cols], qaug_d, qa)
                gbt = gbt_r.next()
                dma("sp", gbt[:], gB_d.t[64 * h:64 * h + 64, cols], gB_d, gbt)
                OB = obanks.next()
                nkt = NT + 4 * qg + 4
                for kt_all in range(nkt):
                    prevh = kt_all < NT
                    kt = kt_all if prevh else kt_all - NT
                    i = -1 if prevh else kt - 4 * qg
                    q0 = 0 if i < 0 else 128 * i
                    N = 512 - q0
                    SB = banks.next()
                    ksl = slice(kt * 128, (kt + 1) * 128)
                    Ksrc = KaugP if prevh else Kaug
                    Vsrc, Vsrc3 = (VpP, VpP3) if prevh else (Vp, Vp3)
                    bsrc = biasP if prevh else cposT
                    if i < 0:
                        op("pe", lambda: nc.tensor.matmul(SB[:, 0:N], lhsT=Ksrc.t[0:67, ksl], rhs=qa[:, q0:512], start=True, stop=True), R=[Ksrc, qa], W=[SB])
                    else:
                        op("pe", lambda: nc.tensor.matmul(SB[:, 0:N], lhsT=Kaug.t[0:67, ksl], rhs=qa[:, q0:512], start=True, stop=False), R=[Kaug, qa], W=[SB], inc=False)
                        op("pe", lambda: nc.tensor.matmul(SB[:, 0:128], lhsT=ident_bf[:], rhs=maskT[:], start=False, stop=True), R=[ident_bf, maskT], W=[SB])
                    while len(pending) >= 2:
                        pending.pop(0)()
                    last = (kt_all == nkt - 1)
                    first = (kt_all == 0)

                    def stage2(kt=kt, q0=q0, N=N, SB=SB, last=last, first=first, OB=OB, gbt=gbt, cols=cols, Vsrc=Vsrc, Vsrc3=Vsrc3, bsrc=bsrc):
                        Pt = P_r.next()
                        op("act", lambda: nc.scalar.activation(out=Pt[:, 0:N], in_=SB[:, 0:N], func=AF.Exp, bias=bsrc[:, kt, h:h + 1], scale=1.0), R=[SB, bsrc], W=[Pt])
                        op("pe", lambda: nc.tensor.matmul(OB[:, q0:512], lhsT=Vsrc3[:, kt, :], rhs=Pt[:, 0:N], start=first, stop=last), R=[Vsrc, Pt], W=[OB], inc=last)
                        if last:
                            rd = rd_r.next()
                            op("dve", lambda: nc.vector.reciprocal(out=rd[:], in_=OB[64:128, :]), R=[OB], W=[rd])
                            of = of_r.next()
                            op("dve", lambda: nc.vector.tensor_tensor(out=of[:], in0=OB[0:64, :], in1=rd[:], op=ALU.mult), R=[OB, rd], W=[of])
                            yo = yo_r.next()
                            op("pool", lambda: nc.gpsimd.tensor_tensor(out=yo[:], in0=of[:], in1=gbt[:], op=ALU.mult), R=[of, gbt], W=[yo])
                            dma("pool", y_d[1].t[64 * h:64 * h + 64, cols], yo[:], yo, y_d[1], disjoint=True)
                    pending.append(stage2)
                drain(pre_c1, 2)
            flush()

        ph.close()
        phase_H(True)
        WO = WBIG[:, 20480:20480 + 8 * 1024].rearrange("p (k c) -> p k c", k=8)
        WPG = WB2[:, 0:8 * 1024].rearrange("p (k c) -> p k c", k=8)
        WPP = WB2[:, 8 * 1024:10 * 1024].rearrange("p (k c) -> p k c", k=2)
        ph = Phase(C)
        WC1B = C.sb([128, 20480], BF16)
        WG1 = WC1B[:, 0:8 * 2048].rearrange("p (k c) -> p k c", k=8)
        WU1 = WC1B[:, 8 * 2048:8 * 2048 + 8 * 512].rearrange("p (k c) -> p k c", k=8)
        pre_c1b = []
        load_c1(1, Rot(["act", "dve"]), pre_c1b, WG1, WU1, [WC1B])
        hTg_r = Rot([C.sb([128, 8, 512], BF16, dma=True) for _ in range(2)])
        Yg_r = Rot([C.sb([128, 8, 512], BF16, dma=True) for _ in range(2)])
        mg_r = Rot([C.sb([128, 4, 512], BF16, dma=True) for _ in range(2)])
        acc_r = Rot([C.sb([128, 512], F32) for _ in range(3)])
        f32w = Rot([C.sb([128, 512], F32) for _ in range(6)])
        for hf in range(2):
            WGh, WUh, WBh = (WG, WU, WB_A) if hf == 0 else (WG1, WU1, WC1B)
            if hf == 1:
                drain(pre_c1b)
                pre_c2 = []
                load_w(lambda k, c0, n: WO[:, k, c0:c0 + n], lambda k, c0, n: (w_o, w_o.t[l, k * 128:(k + 1) * 128, c0:c0 + n]), 1024, None, [WB_T], engs=Rot(["act", "dve"]), defer=pre_c2)
                load_w(lambda k, c0, n: WPG[:, k, c0:c0 + n], lambda k, c0, n: (w_pg, w_pg.t[l, k * 128:(k + 1) * 128, c0:c0 + n]), 1024, gple, [WB2], engs=Rot(["act", "dve"]), defer=pre_c2)
                load_w(lambda k, c0, n: WPP[:, k, c0:c0 + n], lambda k, c0, n: (w_pp, w_pp.t[l, k * 128:(k + 1) * 128, c0:c0 + n]), 1024, None, [WB2], ks=range(2), engs=Rot(["act", "dve"]), defer=pre_c2)
            else:
                drain(pre_c1)
            for g in range(NG):
                cols = slice(g * 512, (g + 1) * 512)
                hTg = hTg_r.next()
                dma("sp", hTg[:], hT_d.t.rearrange("(k p) s -> p k s", p=128)[:, :, cols], hT_d, hTg)
                Yg = Yg_r.next()
                for br in range(4):
                    dma("sp", Yg[:, 2 * br:2 * br + 2, :], y_d[br].t[:, cols].rearrange("(k p) t -> p k t", p=128), y_d[br], Yg, disjoint=(br > 0))
                mg = mg_r.next()
                for db in range(4):
                    dsl = slice(db * 128, (db + 1) * 128)
                    acc = None
                    for br in range(4):
                        gbk = abanks.next()
                        for k in range(8):
                            op("pe", lambda k=k: nc.tensor.matmul(gbk[:], lhsT=WGh[:, k, br * 512 + db * 128:br * 512 + db * 128 + 128], rhs=hTg[:, k, :], start=(k == 0), stop=(k == 7)), R=[WBh, hTg], W=[gbk], inc=(k == 7))
                        sg_ = f32w.next()
                        op("act", lambda: nc.scalar.activation(out=sg_[:], in_=gbk[:], func=AF.Sigmoid, bias=mergeb(br, hf * 4 + db), scale=1.0), R=[gbk, PRM], W=[sg_])
                        ubk = abanks.next()
                        for kk in range(2):
                            op("pe", lambda kk=kk: nc.tensor.matmul(ubk[:], lhsT=WUh[:, 2 * br + kk, dsl], rhs=Yg[:, 2 * br + kk, :], start=(kk == 0), stop=(kk == 1)), R=[WBh, Yg], W=[ubk], inc=(kk == 1))
                        if br == 0:
                            acc = acc_r.next()
                            op("dve", lambda: nc.vector.tensor_tensor(out=acc[:], in0=sg_[:], in1=ubk[:], op=ALU.mult), R=[sg_, ubk], W=[acc])
                        else:
                            tt = f32w.next()
                            op("dve", lambda: nc.vector.tensor_tensor(out=tt[:], in0=sg_[:], in1=ubk[:], op=ALU.mult), R=[sg_, ubk], W=[tt])
                            if br == 1:
                                op("dve", lambda: nc.vector.tensor_tensor(out=acc[:], in0=acc[:], in1=tt[:], op=ALU.add), R=[tt], W=[acc])
                            elif br < 3:
                                op("pool", lambda: nc.gpsimd.tensor_tensor(out=acc[:], in0=acc[:], in1=tt[:], op=ALU.add), R=[tt], W=[acc])
                            else:
                                op("pool", lambda: nc.gpsimd.tensor_tensor(out=mg[:, db, :], in0=acc[:], in1=tt[:], op=ALU.add), R=[tt, acc], Wd=[mg])
                dma("pool", mT_d.t.rearrange("(k p) s -> p k s", p=128)[:, hf * 4:hf * 4 + 4, cols], mg[:], mg, mT_d, disjoint=True)
                if hf == 1:
                    drain(pre_c2, 3)
                else:
                    drain(pre_c1b, 5)
        drain(pre_c2)
        ph.close()

        ph = Phase(C)
        if l + 1 < n_layers:
            load_wa(l + 1, range(0, 5), defer=pre_a, engs=Rot(["act", "dve"]))
        mgl_r = Rot([C.sb([128, 8, 512], BF16, dma=True) for _ in range(2)])
        xin_r = Rot([C.sb([128, 1024], F32, dma=True) for _ in range(2)])
        xn_r = Rot([C.sb([128, 1024], F32) for _ in range(2)])
        xo_r = Rot([C.sb([128, 1024], F32, dma=True) for _ in range(2)])
        hb_r = Rot([C.sb([128, 1024], BF16) for _ in range(2)])
        junk = C.sb([128, 1024], BF16)
        st1 = Rot([C.sb([128, 4], F32) for _ in range(4)])
        hpT_r = Rot([C.sb([128, 8, 128], BF16) for _ in range(2)])
        pin_r = Rot([C.sb([128, 256], F32, dma=True) for _ in range(2)])
        pb_r = Rot([C.sb([128, 256], BF16) for _ in range(2)])
        pT_r = Rot([C.sb([128, 2, 128], BF16) for _ in range(2)])
        f32w = Rot([C.sb([128, 512], F32) for _ in range(6)])
        mgs = {}

        def c2_stage1(ti):
            g, t = ti // 4, ti % 4
            if t == 0:
                mgs[g] = mgl_r.next()
                dma("sp", mgs[g][:], mT_d.t.rearrange("(k p) s -> p k s", p=128)[:, :, g * 512:(g + 1) * 512], mT_d, mgs[g])
            mg = mgs[g]
            tsl = slice(t * 128, (t + 1) * 128)
            tok = slice(ti * 128, (ti + 1) * 128)
            xin = xin_r.next()
            dma("sp", xin[:], xsrc.t[tok, :], xsrc, xin)
            pin = pin_r.next()
            dma("sp", pin[:], p_in.t[l, tok, :], p_in, pin)
            xn = xn_r.next()
            for half in range(2):
                hs = slice(half * 512, (half + 1) * 512)
                obk = abanks.next()
                for db in range(8):
                    op("pe", lambda db=db: nc.tensor.matmul(obk[:], lhsT=mg[:, db, tsl], rhs=WO[:, db, hs], start=(db == 0), stop=(db == 7)), R=[WB_T, mg], W=[obk], inc=(db == 7))
                op("dve", lambda: nc.vector.tensor_tensor(out=xn[:, hs], in0=xin[:, hs], in1=obk[:], op=ALU.add), R=[xin, obk], Wd=[xn])
            pb = pb_r.next()
            op("pool", lambda: nc.gpsimd.tensor_copy(out=pb[:], in_=pin[:]), R=[pin], W=[pb])
            return xn, pb

        def c2_stage1b(xn):
            st = rms_rstd(xn[:], xn, 1024, junk, st1)
            hb = hb_r.next()
            op("dve", lambda: nc.vector.tensor_scalar(out=hb[:], in0=xn[:], scalar1=st[:, 2:3], scalar2=None, op0=ALU.mult), R=[xn, st], W=[hb])
            return hb

        xn0, pb0 = c2_stage1(0)
        nxt = (xn0, c2_stage1b(xn0), pb0)
        for ti in range(NT):
            tok = slice(ti * 128, (ti + 1) * 128)
            xn, hb, pb = nxt
            nx = c2_stage1(ti + 1) if ti + 1 < NT else None
            if True:
                tb = tbanks.next()
                for k in range(8):
                    op("pe", lambda k=k: nc.tensor.transpose(out=tb[:, k * 128:(k + 1) * 128], in_=hb[:, k * 128:(k + 1) * 128], identity=ident_bf[:]), R=[hb, ident_bf], W=[tb], inc=(k == 7))
                hpT = hpT_r.next()
                op("act", lambda: nc.scalar.copy(out=hpT[:], in_=tb[:].rearrange("p (k t) -> p k t", k=8)), R=[tb], W=[hpT])
                tb2 = tbanks.next()
                for k in range(2):
                    op("pe", lambda k=k: nc.tensor.transpose(out=tb2[:, k * 128:(k + 1) * 128], in_=pb[:, k * 128:(k + 1) * 128], identity=ident_bf[:]), R=[pb, ident_bf], W=[tb2], inc=(k == 1))
                pT = pT_r.next()
                op("act", lambda: nc.scalar.copy(out=pT[:], in_=tb2[:, 0:256].rearrange("p (k t) -> p k t", k=2)), R=[tb2], W=[pT])
                if nx is not None:
                    nxt = (nx[0], c2_stage1b(nx[0]), nx[1])
                xo = xo_r.next()
                for half in range(2):
                    hs = slice(half * 512, (half + 1) * 512)
                    gbk = abanks.next()
                    for k in range(8):
                        op("pe", lambda k=k: nc.tensor.matmul(gbk[:], lhsT=hpT[:, k, :], rhs=WPG[:, k, hs], start=(k == 0), stop=(k == 7)), R=[WB2, hpT], W=[gbk], inc=(k == 7))
                    sgp = f32w.next()
                    op("act", lambda: nc.scalar.activation(out=sgp[:], in_=gbk[:], func=AF.Sigmoid), R=[gbk], W=[sgp])
                    pbk = abanks.next()
                    for k in range(2):
                        op("pe", lambda k=k: nc.tensor.matmul(pbk[:], lhsT=pT[:, k, :], rhs=WPP[:, k, hs], start=(k == 0), stop=(k == 1)), R=[WB2, pT], W=[pbk], inc=(k == 1))
                    tt = f32w.next()
                    op("dve", lambda: nc.vector.tensor_tensor(out=tt[:], in0=sgp[:], in1=pbk[:], op=ALU.mult), R=[sgp, pbk], W=[tt])
                    if half == 0:
                        op("dve", lambda: nc.vector.tensor_tensor(out=xo[:, hs], in0=xn[:, hs], in1=tt[:], op=ALU.add), R=[xn, tt], Wd=[xo])
                    else:
                        op("pool", lambda: nc.gpsimd.tensor_tensor(out=xo[:, hs], in0=xn[:, hs], in1=tt[:], op=ALU.add), R=[xn, tt], Wd=[xo])
                dma("pool", xdst.t[tok, :], xo[:], xo, xdst, disjoint=True)
            if ti % 4 == 3:
                drain(pre_a, 3)
        ph.close()

    C.wait_all("pool", [y_out, xres])
    C.wait_all("sp", [y_out])


_NC_CACHE = {}


def kernel(**inputs):
    n_cores = 8
    if "nc" not in _NC_CACHE:
        _NC_CACHE["nc"] = build_nc(DEPTH)
    nc = _NC_CACHE["nc"]
    names = ["norm_mix", "w_in", "conv_w", "conv_b", "fgate_bias", "q_norm", "k_norm", "lb_logits", "hgrn_norm",
             "sgu_norm", "spatial_w", "spatial_b", "w_up", "merge_b", "w_o", "norm_ple", "w_ple_gate", "w_ple_proj"]
    shared = {n: np.ascontiguousarray(np.asarray(inputs[n], dtype=np.float32)) for n in names}
    x = np.asarray(inputs["x"], dtype=np.float32)
    p = np.asarray(inputs["p"], dtype=np.float32)
    in_maps = []
    for c in range(n_cores):
        b, hf = c // 2, c % 2
        m = dict(shared)
        m["x"] = np.ascontiguousarray(x[b, hf * S:(hf + 1) * S])
        m["p"] = np.ascontiguousarray(p[:, b, hf * S:(hf + 1) * S])
        m["flag"] = np.full((128, 1), float(hf), dtype=np.float32)
        in_maps.append(m)
    res = run_bass_kernel_spmd(nc, in_maps, core_ids=list(range(n_cores)))
    out = np.empty((4, 2 * S, D), dtype=np.float32)
    for c in range(n_cores):
        out[c // 2, (c % 2) * S:(c % 2 + 1) * S] = res.results[c]["y"]
    return out
```

```python
import numpy as np
from contextlib import ExitStack
import concourse.bass as bass
import concourse.mybir as mybir
from concourse.bass_utils import run_bass_kernel_spmd

F32 = mybir.dt.float32
BF16 = mybir.dt.bfloat16
AF = mybir.ActivationFunctionType
ALU = mybir.AluOpType
AX = mybir.AxisListType

D = 1024
S = 4096
RG = [[0, 1], [2, 3], [4, 5], [6, 7]]
DEPTH = 4
W = 256
NG = S // 512
NT = S // 128
NCH = S // 64
INC = 7940
EPS = 1e-6
A_X, A_B, A_C, A_G = 0, 256, 512, 768
B_Q, B_K, B_V, B_G, B_F = 1024, 1280, 1536, 1792, 2048
C_Q, C_F, C_I, C_G = 2052, 2308, 2564, 2820
D_U, D_V, D_G = 3076, 3332, 3588
M_G = 3844
NA = 3844


class TK:
    __slots__ = ("w", "r")

    def __init__(self):
        self.w = {}
        self.r = {}


class Sem:
    def __init__(self, h):
        self.h = h
        self.cnt = 0


class Buf:
    def __init__(self, t, sem=None):
        self.t = t
        self.tk = TK()
        self.sem = sem

    def __getitem__(self, k):
        return self.t[k]


class Ctx:
    def __init__(self, nc, es):
        self.nc = nc
        self.es = es
        self.E = {"pe": nc.tensor, "act": nc.scalar, "dve": nc.vector, "pool": nc.gpsimd, "sp": nc.sync}
        self.sem = {e: es.enter_context(nc.semaphore("s_" + e)) for e in ("pe", "act", "dve", "pool")}
        self.cnt = {e: 0 for e in self.sem}
        self.seen = {e: {} for e in self.E}
        self.semobj = dict(self.sem)
        self.nbuf = 0
        self.sems = [Sem(es.enter_context(nc.semaphore(f"d{i}"))) for i in range(64)]
        for sm in self.sems:
            self.semobj[id(sm)] = sm.h
        self.semi = 0
        self.pes = es
        self.ccsem = self.sems.pop()

    def sb(self, shape, dt, dma=False, name=None):
        self.nbuf += 1
        t = self.pes.enter_context(self.nc.sbuf_tensor(name or f"b{self.nbuf}", list(shape), dt))
        b = Buf(t)
        if dma:
            self.add_sem(b)
        return b

    def add_sem(self, b):
        b.sem = self.sems[self.semi]
        self.semi += 1
        return b

    def barrier(self):
        allk = {e: self.cnt[e] for e in self.sem}
        for sm in self.sems + [self.ccsem]:
            allk[id(sm)] = sm.cnt
        for e in self.E:
            self._wait(e, allk)

    def ps(self, shape, dt):
        self.nbuf += 1
        t = self.es.enter_context(self.nc.psum_tensor(f"p{self.nbuf}", list(shape), dt))
        return Buf(t)

    def _wait(self, e, deps):
        eng = self.E[e]
        seen = self.seen[e]
        for key, val in deps.items():
            if e == "pe" and key == "pe":
                continue
            if seen.get(key, 0) >= val:
                continue
            eng.wait_ge(self.semobj[key], val)
            seen[key] = val

    @staticmethod
    def _merge(d, src):
        for k, v in src.items():
            if d.get(k, 0) < v:
                d[k] = v

    def op(self, e, fn, R=(), W=(), Wd=(), inc=True):
        deps = {}
        for b in R:
            self._merge(deps, b.tk.w)
        for b in W:
            self._merge(deps, b.tk.w)
            self._merge(deps, b.tk.r)
        for b in Wd:
            self._merge(deps, b.tk.r)
            self._merge(deps, {k: v for k, v in b.tk.w.items() if k != e})
        self._wait(e, deps)
        ins = fn()
        if inc:
            self.cnt[e] += 1
            ins.then_inc(self.sem[e], 1)
            tick = self.cnt[e]
        else:
            tick = self.cnt[e] + 1
        for b in R:
            if b.tk.r.get(e, 0) < tick:
                b.tk.r[e] = tick
        for b in W:
            b.tk.w = {e: tick}
            b.tk.r = {}
        for b in Wd:
            b.tk.w[e] = tick
        return ins

    def dma(self, q, out, in_, src, dst, disjoint=False, sem_owner=None, **kw):
        owner = (sem_owner or (dst if dst.sem is not None else src)).sem
        key = id(owner)
        deps = {}
        self._merge(deps, src.tk.w)
        self._merge(deps, dst.tk.r)
        if disjoint:
            self._merge(deps, {k: v for k, v in dst.tk.w.items() if k != key})
        else:
            self._merge(deps, dst.tk.w)
        self._wait(q, deps)
        owner.cnt += 16
        self.E[q].dma_start(out=out, in_=in_, **kw).then_inc(owner.h, 16)
        if src.tk.r.get(key, 0) < owner.cnt:
            src.tk.r[key] = owner.cnt
        if disjoint:
            dst.tk.w[key] = owner.cnt
        else:
            dst.tk.w = {key: owner.cnt}
            dst.tk.r = {}

    def collective(self, src, dst):
        deps = {}
        self._merge(deps, src.tk.w)
        self._merge(deps, dst.tk.r)
        self._merge(deps, dst.tk.w)
        self._wait("pool", deps)
        sm = self.ccsem
        sm.cnt += 1
        self.nc.gpsimd.collective_compute("AllGather", ALU.bypass, replica_groups=RG,
                                          ins=[src.t.opt()], outs=[dst.t.opt()]).then_inc(sm.h, 1)
        key = id(sm)
        src.tk.r[key] = sm.cnt
        dst.tk.w = {key: sm.cnt}
        dst.tk.r = {}

    def wait_all(self, q, bufs):
        deps = {}
        for b in bufs:
            self._merge(deps, b.tk.w)
            self._merge(deps, b.tk.r)
        self._wait(q, deps)


class Phase:
    def __init__(self, C):
        self.C = C
        self.es = ExitStack()
        self.es.__enter__()
        C.pes = self.es
        self.semi0 = C.semi

    def close(self):
        C = self.C
        C.barrier()
        C.pes = C.es
        C.semi = self.semi0
        self.es.__exit__(None, None, None)


class Rot:
    def __init__(self, items):
        self.items = items
        self.i = 0

    def next(self):
        b = self.items[self.i % len(self.items)]
        self.i += 1
        return b


def build_nc(n_layers=DEPTH):
    nc = bass.Bass("TRN2", target_bir_lowering=False)
    with ExitStack() as es:
        es.enter_context(nc.allow_non_contiguous_dma("small strided parameter loads"))
        _build(nc, es, n_layers)
    return nc


def _build(nc, es, n_layers):
    C = Ctx(nc, es)
    op, dma = C.op, C.dma

    def dram_in(name, shape):
        return Buf(nc.dram_tensor(name, list(shape), F32, kind="ExternalInput").ap())

    x_in = dram_in("x", [S, D])
    p_in = dram_in("p", [DEPTH, S, W])
    norm_mix = dram_in("norm_mix", [DEPTH, D])
    w_in = dram_in("w_in", [DEPTH, D, INC])
    conv_w = dram_in("conv_w", [DEPTH, 3, W])
    conv_b = dram_in("conv_b", [DEPTH, W])
    fgate_bias = dram_in("fgate_bias", [DEPTH, 4])
    q_norm = dram_in("q_norm", [DEPTH, 64])
    k_norm = dram_in("k_norm", [DEPTH, 64])
    lb_logits = dram_in("lb_logits", [DEPTH, W])
    hgrn_norm = dram_in("hgrn_norm", [DEPTH, W])
    sgu_norm = dram_in("sgu_norm", [DEPTH, W])
    spatial_w = dram_in("spatial_w", [DEPTH, 4, 128, 128])
    spatial_b = dram_in("spatial_b", [DEPTH, 4, 128])
    w_up = dram_in("w_up", [DEPTH, 4, W, D])
    merge_b = dram_in("merge_b", [DEPTH, 4, D])
    w_o = dram_in("w_o", [DEPTH, D, D])
    norm_ple = dram_in("norm_ple", [DEPTH, D])
    w_pg = dram_in("w_ple_gate", [DEPTH, D, D])
    w_pp = dram_in("w_ple_proj", [DEPTH, W, D])
    y_out = Buf(nc.dram_tensor("y", [S, D], F32, kind="ExternalOutput").ap())

    def scratch(name, shape, dt):
        return Buf(nc.dram_tensor(name, list(shape), dt).ap())

    xres = scratch("xres", [S, D], F32)
    hT_d = scratch("hT_d", [D, S], BF16)
    y_d = [scratch(f"ybr{i}", [W, S], BF16) for i in range(4)]
    qaug_d = scratch("qaug", [4, 67, S], BF16)
    xs1a = scratch("xs1a", [134, S], BF16); xg1a = scratch("xg1a", [268, S], BF16)
    xs1b = scratch("xs1b", [134, S], BF16); xg1b = scratch("xg1b", [268, S], BF16)
    xs1c = scratch("xs1c", [256, S], BF16); xg1c = scratch("xg1c", [512, S], BF16)
    xs2 = scratch("xs2", [128, 132], F32)
    xg2 = scratch("xg2", [256, 132], F32)
    xs3 = scratch("xs3", [64, 256], F32)
    xg3 = scratch("xg3", [128, 256], F32)
    def kaug_h(h):
        b = xs1a if h < 2 else xs1b
        return b, b.t[67 * (h % 2):67 * (h % 2) + 67, :]

    def kaugP_h(h):
        b = xg1a if h < 2 else xg1b
        return b, b.t[67 * (h % 2):67 * (h % 2) + 67, :]

    def qaug_h(h):
        return qaug_d, qaug_d.t[h]
    vtok_d = Buf(xs1c.t[:, :].rearrange("r (q f) -> (r q) f", f=256)); vtok_d.tk = xs1c.tk
    vtokP_d = Buf(xg1c.t[0:256, :].rearrange("r (q f) -> (r q) f", f=256)); vtokP_d.tk = xg1c.tk
    flag_in = dram_in("flag", [128, 1])
    gB_d = scratch("gB", [W, S], BF16)
    gC_d = scratch("gC", [W, S], BF16)
    Qp_d = scratch("Qp", [W, S], BF16)
    Kp_d = scratch("Kp", [W, S], BF16)
    vH_d = scratch("vH", [S, W], BF16)
    mT_d = scratch("mT", [D, S], BF16)
    ut_d = Buf(nc.dram_tensor("ut_d", [64, NCH * 256], F32).ap().rearrange("p (c f) -> p c f", f=256))

    ident_bf = C.sb([128, 128], BF16)
    ident_f = C.sb([128, 128], F32)
    blockones = C.sb([128, 128], BF16)
    tri = C.sb([64, 64], F32)
    maskT = C.sb([128, 128], BF16)
    resetm = C.sb([128, 512], F32)
    ones_f = C.sb([128, 512], F32)
    onesb = C.sb([4, 512], BF16, dma=True)
    eps_t = C.sb([128, 1], F32)
    one_t = C.sb([128, 1], F32)

    pool = nc.gpsimd
    op("pool", lambda: pool.memset(ident_bf[:], 0.0), W=[ident_bf])
    op("pool", lambda: pool.affine_select(out=ident_bf[:], in_=ident_bf[:], pattern=[[-1, 128]], compare_op=ALU.not_equal, fill=1.0, base=0, channel_multiplier=1), W=[ident_bf])
    op("pool", lambda: pool.memset(ident_f[:], 0.0), W=[ident_f])
    op("pool", lambda: pool.affine_select(out=ident_f[:], in_=ident_f[:], pattern=[[-1, 128]], compare_op=ALU.not_equal, fill=1.0, base=0, channel_multiplier=1), W=[ident_f])
    op("pool", lambda: pool.memset(blockones[:], 0.0), W=[blockones])
    op("pool", lambda: pool.memset(blockones[0:64, 0:64], 1.0), W=[blockones])
    op("pool", lambda: pool.memset(blockones[64:128, 64:128], 1.0), W=[blockones])
    op("pool", lambda: pool.memset(tri[:], 1.0), W=[tri])
    op("pool", lambda: pool.affine_select(out=tri[:], in_=tri[:], pattern=[[1, 64]], compare_op=ALU.is_ge, fill=0.0, base=0, channel_multiplier=-1), W=[tri])
    op("pool", lambda: pool.memset(maskT[:], 0.0), W=[maskT])
    op("pool", lambda: pool.affine_select(out=maskT[:], in_=maskT[:], pattern=[[1, 128]], compare_op=ALU.is_ge, fill=-30000.0, base=0, channel_multiplier=-1), W=[maskT])
    op("pool", lambda: pool.memset(resetm[:], 1.0), W=[resetm])
    op("pool", lambda: pool.memset(resetm[:].rearrange("p (c t) -> p c t", t=64)[:, :, 0:1], 0.0), W=[resetm])
    op("pool", lambda: pool.memset(ones_f[:], 1.0), W=[ones_f])
    op("pool", lambda: pool.memset(onesb[:], 1.0), W=[onesb])
    op("pool", lambda: pool.memset(eps_t[:], EPS), W=[eps_t])
    op("pool", lambda: pool.memset(one_t[:], 1.0), W=[one_t])
    for h in range(4):
        for c4 in range(S // 512):
            dma("sp", kaug_h(h)[1][64:67, c4 * 512:(c4 + 1) * 512], onesb[0:3, :], onesb, kaug_h(h)[0], disjoint=True)

    flag_t = C.sb([128, 1], F32, dma=True)
    negbig = C.sb([128, 1], F32)
    dma("sp", flag_t[:], flag_in.t[:, :], flag_in, flag_t)
    op("dve", lambda: nc.vector.tensor_scalar(out=negbig[:], in0=flag_t[:], scalar1=-1.0, scalar2=30000.0, op0=ALU.add, op1=ALU.mult), R=[flag_t], W=[negbig])
    GF = C.sb([128, 2, 2], F32)
    YA0 = C.sb([128, 2, 2], F32)
    biasP = C.sb([128, NT, 4], F32)
    lbl = C.sb([128, 2, 4], F32, dma=True)
    for l4 in range(DEPTH):
        dma("sp", lbl[:, :, l4], lb_logits.t[l4].rearrange("(j p) -> p j", p=128), lb_logits, lbl, disjoint=(l4 > 0))
    lbe = C.sb([128, 2, 4], F32)
    lbs = C.sb([128, 2], F32)
    lbp = C.sb([128, 2, 4], F32)
    lbc = C.sb([128, 2, 4], F32)
    LB = C.sb([128, 2, 4], F32)
    OML = C.sb([128, 2, 4], F32)
    NOML = C.sb([128, 2, 4], F32)
    op("act", lambda: nc.scalar.activation(out=lbe[:], in_=lbl[:], func=AF.Exp), R=[lbl], W=[lbe])
    op("dve", lambda: nc.vector.reduce_sum(out=lbs[:], in_=lbe[:], axis=AX.X), R=[lbe], W=[lbs])
    op("dve", lambda: nc.vector.reciprocal(out=lbs[:], in_=lbs[:]), W=[lbs])
    op("dve", lambda: nc.vector.tensor_tensor(out=lbp[:], in0=lbe[:], in1=lbs[:].unsqueeze(2).to_broadcast([128, 2, 4]), op=ALU.mult), R=[lbe, lbs], W=[lbp])
    op("dve", lambda: nc.vector.memset(lbc[:, :, 0:1], 0.0), W=[lbc])
    op("dve", lambda: nc.vector.tensor_copy(out=lbc[:, :, 1:2], in_=lbp[:, :, 1:2]), R=[lbp], W=[lbc])
    op("dve", lambda: nc.vector.tensor_tensor(out=lbc[:, :, 2:3], in0=lbc[:, :, 1:2], in1=lbp[:, :, 2:3], op=ALU.add), R=[lbp], W=[lbc])
    op("dve", lambda: nc.vector.tensor_tensor(out=lbc[:, :, 3:4], in0=lbc[:, :, 2:3], in1=lbp[:, :, 3:4], op=ALU.add), R=[lbp], W=[lbc])
    op("dve", lambda: nc.vector.tensor_scalar(out=LB[:], in0=lbc[:], scalar1=0.0, scalar2=1.0, op0=ALU.max, op1=ALU.min), R=[lbc], W=[LB])
    op("dve", lambda: nc.vector.tensor_scalar(out=OML[:], in0=LB[:], scalar1=-1.0, scalar2=1.0, op0=ALU.mult, op1=ALU.add), R=[LB], W=[OML])
    op("dve", lambda: nc.vector.tensor_scalar(out=NOML[:], in0=LB[:], scalar1=-1.0, scalar2=None, op0=ALU.add), R=[LB], W=[NOML])

    WBIG = C.sb([128, 8 * NA], BF16, name="wbig")
    WB_A = Buf(WBIG.t)
    WB_T = Buf(WBIG.t)
    WB2 = C.sb([128, 10240], BF16, name="wb2")
    GMALL = C.sb([128, DEPTH, 16], F32, dma=True)
    for l4 in range(DEPTH):
        dma("sp", GMALL[:, l4, 0:8], norm_mix.t[l4].rearrange("(k p) -> p k", p=128), norm_mix, GMALL, disjoint=(l4 > 0))
        dma("sp", GMALL[:, l4, 8:16], norm_ple.t[l4].rearrange("(k p) -> p k", p=128), norm_ple, GMALL, disjoint=True)
    stg = Rot([C.sb([128, 1024], F32, dma=True) for _ in range(2)])
    banks = Rot([C.ps([128, 512], F32) for _ in range(4)])
    obanks = Rot([C.ps([128, 512], F32) for _ in range(2)])
    abanks = Rot(banks.items + obanks.items)
    tbanks = Rot([C.ps([128, 1024], BF16) for _ in range(2)])
    PRM = C.sb([128, 80], F32, dma=True)
    SG = C.sb([128, 256], F32, dma=True)
    WSP = C.sb([128, 4, 128], F32, dma=True)
    WT = C.sb([128, 4, 128], BF16)
    BT = C.sb([128, 2, 128], F32, dma=True)
    cposT = C.sb([128, NT, 4], F32)
    CV = C.sb([64, 3, 4, NCH], F32)

    cast_engs = Rot(["dve", "pool", "dve", "act"])

    def cast_mul(e, out, in_, sc):
        if e == "act":
            return nc.scalar.mul(out=out, in_=in_, mul=sc) if sc is not None else nc.scalar.copy(out=out, in_=in_)
        eng = nc.vector if e == "dve" else nc.gpsimd
        if sc is None:
            return eng.tensor_copy(out=out, in_=in_)
        return eng.tensor_scalar(out=out, in0=in_, scalar1=sc, scalar2=None, op0=ALU.mult)

    def load_w(dst_ap_fn, src_rows_fn, ncols, gain_col, wbufs, ks=range(8), engs=None, defer=None):
        for k in ks:
            c0 = 0
            while c0 < ncols:
                n = min(1024, ncols - c0)

                def emit(k=k, c0=c0, n=n):
                    st = stg.next()
                    srcb, srcap = src_rows_fn(k, c0, n)
                    dma("sp", st[:, 0:n], srcap, srcb, st)
                    e = cast_engs.next() if engs is None else engs.next()
                    g = gain_col(k) if gain_col is not None else None
                    d = dst_ap_fn(k, c0, n)
                    op(e, (lambda: cast_mul(e, d, st[:, 0:n], g)), R=[st, GMALL], Wd=list(wbufs))
                if defer is None:
                    emit()
                else:
                    defer.append(emit)
                c0 += n

    def drain(lst, n=None):
        cnt = 0
        while lst and (n is None or cnt < n):
            lst.pop(0)()
            cnt += 1

    def rms_rstd(src_ap, srcb, n, junk, st1):
        st = st1.next()
        op("act", lambda: nc.scalar.activation(out=junk[:, 0:n], in_=src_ap, func=AF.Square, accum_out=st[:, 0:1]), R=[srcb], W=[junk, st])
        op("act", lambda: nc.scalar.activation(out=st[:, 1:2], in_=st[:, 0:1], func=AF.Sqrt, bias=eps_t[:], scale=1.0 / n), R=[eps_t], W=[st])
        op("dve", lambda: nc.vector.reciprocal(out=st[:, 2:3], in_=st[:, 1:2]), W=[st])
        return st

    for l in range(n_layers):
        xsrc = x_in if l == 0 else xres
        xdst = y_out if l == n_layers - 1 else xres
        dma("sp", PRM[:, 0:8], norm_mix.t[l].rearrange("(k p) -> p k", p=128), norm_mix, PRM)
        dma("sp", PRM[:, 8:16], norm_ple.t[l].rearrange("(k p) -> p k", p=128), norm_ple, PRM, disjoint=True)
        for tp in range(3):
            dma("sp", PRM[:, 16:22].rearrange("p (j t) -> p j t", t=3)[:, :, tp], conv_w.t[l, tp].rearrange("(j p) -> p j", p=128), conv_w, PRM, disjoint=True)
        dma("sp", PRM[:, 22:24], conv_b.t[l].rearrange("(j p) -> p j", p=128), conv_b, PRM, disjoint=True)
        for hh in range(2):
            dma("sp", PRM[64 * hh:64 * hh + 64, 24:25], q_norm.t[l].rearrange("(p o) -> p o", o=1), q_norm, PRM, disjoint=True)
            dma("sp", PRM[64 * hh:64 * hh + 64, 25:26], k_norm.t[l].rearrange("(p o) -> p o", o=1), k_norm, PRM, disjoint=True)
        dma("sp", PRM[0:64, 32:36], hgrn_norm.t[l].rearrange("(h p) -> p h", p=64), hgrn_norm, PRM, disjoint=True)
        for br_ in range(4):
            dma("sp", PRM[:, 36 + 8 * br_:44 + 8 * br_], merge_b.t[l, br_].rearrange("(j p) -> p j", p=128), merge_b, PRM, disjoint=True)
        dma("sp", PRM[0:4, 68:69], fgate_bias.t[l].rearrange("(p o) -> p o", o=1), fgate_bias, PRM, disjoint=True)
        op("dve", lambda: nc.vector.tensor_scalar(out=PRM[:, 26:27], in0=PRM[:, 24:25], scalar1=0.125, scalar2=None, op0=ALU.mult), R=[PRM], Wd=[PRM])
        op("dve", lambda: nc.vector.tensor_scalar(out=PRM[0:4, 69:70], in0=PRM[0:4, 68:69], scalar1=-1.0, scalar2=None, op0=ALU.mult), R=[PRM], Wd=[PRM])
        gmix = lambda k, l=l: GMALL[:, l, k:k + 1]
        gple = lambda k, l=l: GMALL[:, l, 8 + k:9 + k]
        convw = lambda j, t: PRM[:, 16 + 3 * j + t:17 + 3 * j + t]
        convb = lambda j: PRM[:, 22 + j:23 + j]
        gq8 = PRM[:, 26:27]
        gk = PRM[:, 25:26]
        hgain = lambda h: PRM[0:64, 32 + h:33 + h]
        mergeb = lambda b, j: PRM[:, 36 + 8 * b + j:37 + 8 * b + j]
        nfgb = PRM[0:4, 69:70]
        lb_c = lambda j: LB[:, j, l:l + 1]
        oml_c = lambda j: OML[:, j, l:l + 1]
        noml_c = lambda j: NOML[:, j, l:l + 1]
        dma("sp", SG[:], sgu_norm.t[l:l + 1, :].partition_broadcast(128), sgu_norm, SG)
        dma("sp", WSP[:], spatial_w.t[l].rearrange("g t s -> t g s"), spatial_w, WSP)
        for gh in range(4):
            hh, j = gh % 2, gh // 2
            dma("sp", BT[64 * hh:64 * hh + 64, j, :], spatial_b.t[l, gh:gh + 1, :].partition_broadcast(64), spatial_b, BT, disjoint=(gh > 0))
        for gh in range(4):
            bk = abanks.next()
            op("pe", lambda bk=bk, gh=gh: nc.tensor.transpose(out=bk[:, 0:128], in_=WSP[:, gh, :], identity=ident_f[:]), R=[WSP, ident_f], W=[bk])
            op("dve", lambda bk=bk, gh=gh: nc.vector.scalar_tensor_tensor(out=WT[:, gh, :], in0=maskT[:], scalar=-1.0, in1=bk[:, 0:128], op0=ALU.is_gt, op1=ALU.mult), R=[bk, maskT], Wd=[WT])

        WA = WBIG[:, 0:8 * NA].rearrange("p (k c) -> p k c", k=8)

        def load_wa(lw, ks, defer=None, engs=None):
            load_w(lambda k, c0, n: WA[:, k, c0:c0 + n],
                   lambda k, c0, n: (w_in, w_in.t[lw, k * 128:(k + 1) * 128, c0:c0 + n]),
                   NA, (lambda k: GMALL[:, lw, k:k + 1]), ([WB_A] if max(ks) < 5 else [WB_A, WB_T]), ks=ks, defer=defer, engs=engs)
        if l == 0:
            load_wa(0, range(0, 5))
        else:
            drain(pre_a)
        load_wa(l, range(5, 8))
        pre_a = []

        ph = Phase(C)
        xin_r = Rot([C.sb([128, 1024], F32, dma=True) for _ in range(2)])
        junk = C.sb([128, 1024], BF16)
        hb_r = Rot([C.sb([128, 1024], BF16) for _ in range(2)])
        st1 = Rot([C.sb([128, 4], F32) for _ in range(4)])
        hTg_r = Rot([C.sb([128, 8, 512], BF16, dma=True) for _ in range(2)])
        f32w = Rot([C.sb([128, 512], F32) for _ in range(8)])
        bfw = Rot([C.sb([128, 512], BF16, dma=True) for _ in range(8)])
        zc_r = [C.sb([128, 514], F32, dma=True) for _ in range(2)]
        cp_r = Rot([C.sb([4, 512], F32) for _ in range(2)])
        cpx = Rot([C.sb([4, 512], F32) for _ in range(3)])
        cpb = Rot([C.sb([4, 512], BF16, dma=True) for _ in range(6)])
        ug_r = Rot([C.sb([128, 2, 512], F32) for _ in range(2)])
        ydb_r = Rot([C.sb([128, 2, 512], BF16, dma=True) for _ in range(2)])
        tokw = Rot([C.sb([128, 256], F32) for _ in range(4)])
        tokb = Rot([C.sb([128, 256], BF16, dma=True) for _ in range(4)])
        vnb_r = Rot([C.sb([128, 256], BF16) for _ in range(4)])
        vhb = Rot([C.sb([64, 2, 256], BF16, dma=True) for _ in range(3)])

        for j in range(2):
            op("pool", lambda j=j: nc.gpsimd.memset(zc_r[j][:, 0:2], 0.0), W=[zc_r[j]])
        prev_cp = None

        def norm_tile(ti):
            tok = slice(ti * 128, (ti + 1) * 128)
            xin = xin_r.next()
            dma("sp", xin[:], xsrc.t[tok, :], xsrc, xin)
            st = rms_rstd(xin[:], xin, 1024, junk, st1)
            hb = hb_r.next()
            op("dve", lambda: nc.vector.tensor_scalar(out=hb[:], in0=xin[:], scalar1=st[:, 2:3], scalar2=None, op0=ALU.mult), R=[xin, st], W=[hb])
            return hb

        hb_next = norm_tile(0)
        for g in range(NG):
            cols = slice(g * 512, (g + 1) * 512)
            hTg = hTg_r.next()
            for t in range(4):
                hb = hb_next
                if g * 4 + t + 1 < NT:
                    hb_next = norm_tile(g * 4 + t + 1)
                tb = tbanks.next()
                for k in range(8):
                    op("pe", lambda k=k, tb=tb, hb=hb: nc.tensor.transpose(out=tb[:, k * 128:(k + 1) * 128], in_=hb[:, k * 128:(k + 1) * 128], identity=ident_bf[:]), R=[hb, ident_bf], W=[tb], inc=(k == 7))
                op("dve", lambda tb=tb, hTg=hTg, t=t: nc.vector.tensor_copy(out=hTg[:, :, t * 128:(t + 1) * 128], in_=tb[:].rearrange("p (k t) -> p k t", k=8)), R=[tb], Wd=[hTg])
            dma("pool", hT_d.t.rearrange("(k p) s -> p k s", p=128)[:, :, cols], hTg[:], hTg, hT_d, disjoint=True)

            def fm(c0, M=128):
                bk = abanks.next()
                for k in range(8):
                    op("pe", lambda k=k, bk=bk: nc.tensor.matmul(bk[0:M, :], lhsT=WA[:, k, c0:c0 + M], rhs=hTg[:, k, :], start=(k == 0), stop=(k == 7)), R=[WB_A, WB_T, hTg], W=[bk], inc=(k == 7))
                return bk

            def store(dst, row0, nrows, srcbuf, src_ap):
                dma("pool", dst.t[row0:row0 + nrows, cols], src_ap, srcbuf, dst, disjoint=True)

            for j in range(2):
                bx = fm(A_X + 128 * j)
                tmp = f32w.next()
                op("act", lambda: nc.scalar.copy(out=tmp[:], in_=bx[:]), R=[bx], W=[tmp])
                bc = fm(A_C + 128 * j)
                zc = zc_r[j]
                op("dve", lambda: nc.vector.tensor_tensor(out=zc[:, 2:514], in0=tmp[:], in1=bc[:], op=ALU.mult), R=[tmp, bc], W=[zc])
                a1 = f32w.next()
                a2 = f32w.next()
                op("dve", lambda: nc.vector.tensor_scalar(out=a1[:], in0=zc[:, 2:514], scalar1=convw(j, 2), scalar2=convb(j), op0=ALU.mult, op1=ALU.add), R=[zc, PRM], W=[a1])
                op("dve", lambda: nc.vector.scalar_tensor_tensor(out=a2[:], in0=zc[:, 1:513], scalar=convw(j, 1), in1=a1[:], op0=ALU.mult, op1=ALU.add), R=[zc, a1, PRM], W=[a2])
                op("dve", lambda: nc.vector.scalar_tensor_tensor(out=a1[:], in0=zc[:, 0:512], scalar=convw(j, 0), in1=a2[:], op0=ALU.mult, op1=ALU.add), R=[zc, a2, PRM], W=[a1])
                op("pool", lambda: nc.gpsimd.tensor_copy(out=zc[:, 0:2], in_=zc[:, 512:514]), W=[zc])
                bb = fm(A_B + 128 * j)
                op("dve", lambda: nc.vector.tensor_tensor(out=a2[:], in0=a1[:], in1=bb[:], op=ALU.mult), R=[a1, bb], W=[a2])
                bg = fm(A_G + 128 * j)
                sg_ = f32w.next()
                op("act", lambda: nc.scalar.activation(out=sg_[:], in_=bg[:], func=AF.Silu), R=[bg], W=[sg_])
                yb_ = bfw.next()
                op("dve", lambda: nc.vector.tensor_tensor(out=yb_[:], in0=a2[:], in1=sg_[:], op=ALU.mult), R=[a2, sg_], W=[yb_])
                if g == 0:
                    op("dve", lambda: nc.vector.tensor_tensor(out=GF[:, j, :], in0=bb[:, 0:2], in1=sg_[:, 0:2], op=ALU.mult), R=[bb, sg_], Wd=[GF])
                    op("dve", lambda: nc.vector.tensor_tensor(out=YA0[:, j, :], in0=a2[:, 0:2], in1=sg_[:, 0:2], op=ALU.mult), R=[a2, sg_], Wd=[YA0])
                if g == NG - 1:
                    dma("pool", xs2.t[:, 128 + 2 * j:130 + 2 * j], zc[:, 512:514], zc, xs2, disjoint=True)
                store(y_d[0], 128 * j, 128, yb_, yb_[:])

            for (c0, gcol, dstf) in ((B_Q, gq8, qaug_h), (B_K, gk, kaug_h)):
                for j in range(2):
                    bq = fm(c0 + 128 * j)
                    sq = bfw.next()
                    op("act", lambda: nc.scalar.activation(out=sq[:], in_=bq[:], func=AF.Square), R=[bq], W=[sq])
                    b2 = abanks.next()
                    op("pe", lambda: nc.tensor.matmul(b2[:], lhsT=blockones[:], rhs=sq[:], start=True, stop=True), R=[blockones, sq], W=[b2])
                    rt = f32w.next()
                    op("act", lambda: nc.scalar.activation(out=rt[:], in_=b2[:], func=AF.Ln, bias=eps_t[:], scale=1.0 / 64), R=[b2, eps_t], W=[rt])
                    op("act", lambda: nc.scalar.activation(out=rt[:], in_=rt[:], func=AF.Exp, scale=-0.5), W=[rt])
                    qn = bfw.next()
                    op("dve", lambda: nc.vector.scalar_tensor_tensor(out=qn[:], in0=bq[:], scalar=gcol, in1=rt[:], op0=ALU.mult, op1=ALU.mult), R=[bq, rt, PRM], W=[qn])
                    for hh in range(2):
                        dstb, dstap = dstf(2 * j + hh)
                        dma("pool", dstap[0:64, cols], qn[64 * hh:64 * hh + 64, :], qn, dstb, disjoint=True)
            for j in range(2):
                bg = fm(B_G + 128 * j)
                gb = bfw.next()
                op("act", lambda: nc.scalar.activation(out=gb[:], in_=bg[:], func=AF.Silu), R=[bg], W=[gb])
                store(gB_d, 128 * j, 128, gb, gb[:])
            bf_ = fm(B_F, M=4)
            e_ = cpx.next()
            op("act", lambda: nc.scalar.activation(out=e_[:], in_=bf_[0:4, :], func=AF.Exp, bias=nfgb, scale=-1.0), R=[bf_, PRM], W=[e_])
            sp_ = cpx.next()
            op("act", lambda: nc.scalar.activation(out=sp_[:], in_=e_[:], func=AF.Ln, bias=one_t[0:4, :], scale=1.0), R=[e_, one_t], W=[sp_])
            cp = cp_r.next()
            init = 0.0 if prev_cp is None else prev_cp[:, 511:512]
            rr_ = [ones_f, sp_] + ([prev_cp] if prev_cp is not None else [])
            op("dve", lambda: nc.vector.tensor_tensor_scan(out=cp[:], data0=ones_f[0:4, :], data1=sp_[:], initial=init, op0=ALU.mult, op1=ALU.add), R=rr_, W=[cp])
            prev_cp = cp
            hi, mid, lo = cpb.next(), cpb.next(), cpb.next()
            r1, r2 = cpx.next(), e_
            op("dve", lambda: nc.vector.tensor_scalar(out=hi[:], in0=cp[:], scalar1=-1.0, scalar2=None, op0=ALU.mult), R=[cp], W=[hi])
            op("dve", lambda: nc.vector.scalar_tensor_tensor(out=r1[:], in0=cp[:], scalar=-1.0, in1=hi[:], op0=ALU.mult, op1=ALU.subtract), R=[cp, hi], W=[r1])
            op("dve", lambda: nc.vector.tensor_copy(out=mid[:], in_=r1[:]), R=[r1], W=[mid])
            op("dve", lambda: nc.vector.tensor_tensor(out=r2[:], in0=r1[:], in1=mid[:], op=ALU.subtract), R=[r1, mid], W=[r2])
            op("dve", lambda: nc.vector.tensor_copy(out=lo[:], in_=r2[:]), R=[r2], W=[lo])
            for i, bb_ in enumerate((hi, mid, lo)):
                dma("pool", qaug_d.t[:, 64 + i, cols], bb_[:], bb_, qaug_d, disjoint=True)
            def cpos_transposes(cp=cp, g=g):
                for t in range(4):
                    bk = abanks.next()
                    op("pe", lambda: nc.tensor.transpose(out=bk[:, 0:4], in_=cp[0:4, t * 128:(t + 1) * 128], identity=ident_f[0:4, 0:4]), R=[cp, ident_f], W=[bk])
                    op("act", lambda: nc.scalar.copy(out=cposT[:, g * 4 + t, :], in_=bk[:, 0:4]), R=[bk], Wd=[cposT])

            for j in range(2):
                bff = fm(C_F + 128 * j)
                sig = f32w.next()
                op("act", lambda: nc.scalar.activation(out=sig[:], in_=bff[:], func=AF.Sigmoid), R=[bff], W=[sig])
                gl = f32w.next()
                op("dve", lambda: nc.vector.tensor_scalar(out=gl[:], in0=sig[:], scalar1=oml_c(j), scalar2=lb_c(j), op0=ALU.mult, op1=ALU.add), R=[sig, OML, LB], W=[gl])
                op("act", lambda: nc.scalar.activation(out=gl[:], in_=gl[:], func=AF.Ln), W=[gl])
                kf = f32w.next()
                op("pool", lambda: nc.gpsimd.tensor_scalar(out=kf[:], in0=sig[:], scalar1=noml_c(j), scalar2=oml_c(j), op0=ALU.mult, op1=ALU.add), R=[sig, OML, NOML], W=[kf])
                b_ = f32w.next()
                op("dve", lambda: nc.vector.tensor_tensor_scan(out=b_[:], data0=resetm[:], data1=gl[:], initial=0.0, op0=ALU.mult, op1=ALU.add), R=[resetm, gl], W=[b_])
                b3 = b_[:].rearrange("p (c t) -> p c t", t=64)
                bm = f32w.next()
                bm3 = bm[:].rearrange("p (c t) -> p c t", t=64)
                op("dve", lambda: nc.vector.tensor_tensor(out=bm3, in0=b3, in1=b3[:, :, 32:33].to_broadcast([128, 8, 64]), op=ALU.subtract), R=[b_], W=[bm])
                e1 = f32w.next()
                op("act", lambda: nc.scalar.activation(out=e1[:], in_=bm[:], func=AF.Exp), R=[bm], W=[e1])
                e2 = gl
                op("act", lambda: nc.scalar.activation(out=e2[:], in_=bm[:], func=AF.Exp, scale=-1.0), R=[bm], W=[e2])
                for hh in range(2):
                    hd = 2 * j + hh
                    pr = slice(64 * hh, 64 * hh + 64)
                    ch = slice(g * 8, g * 8 + 8)
                    op("act", lambda: nc.scalar.activation(out=CV[:, 0, hd, ch], in_=b3[pr, :, 32], func=AF.Exp), R=[b_], Wd=[CV])
                    op("act", lambda: nc.scalar.activation(out=CV[:, 1, hd, ch], in_=b3[pr, :, 63], func=AF.Exp), R=[b_], Wd=[CV])
                    op("act", lambda: nc.scalar.activation(out=CV[:, 2, hd, ch], in_=bm3[pr, :, 63], func=AF.Exp), R=[bm], Wd=[CV])
                bqq = fm(C_Q + 128 * j)
                sq_ = sig
                op("act", lambda: nc.scalar.activation(out=sq_[:], in_=bqq[:], func=AF.Silu), R=[bqq], W=[sq_])
                Qp = bfw.next()
                op("dve", lambda: nc.vector.tensor_tensor(out=Qp[:], in0=sq_[:], in1=e1[:], op=ALU.mult), R=[sq_, e1], W=[Qp])
                store(Qp_d, 128 * j, 128, Qp, Qp[:])
                Kp = bfw.next()
                op("pool", lambda: nc.gpsimd.tensor_tensor(out=Kp[:], in0=kf[:], in1=e2[:], op=ALU.mult), R=[kf, e2], W=[Kp])
                store(Kp_d, 128 * j, 128, Kp, Kp[:])
                bgc = fm(C_G + 128 * j)
                gc = bfw.next()
                op("act", lambda: nc.scalar.activation(out=gc[:], in_=bgc[:], func=AF.Silu), R=[bgc], W=[gc])
                store(gC_d, 128 * j, 128, gc, gc[:])

            ug = ug_r.next()
            for j in range(2):
                bu = fm(D_U + 128 * j)
                us = f32w.next()
                op("act", lambda: nc.scalar.copy(out=us[:], in_=bu[:]), R=[bu], W=[us])
                bgd = fm(D_G + 128 * j)
                gd = f32w.next()
                op("act", lambda: nc.scalar.activation(out=gd[:], in_=bgd[:], func=AF.Silu), R=[bgd], W=[gd])
                op("pool", lambda: nc.gpsimd.tensor_tensor(out=ug[:, j, :], in0=us[:], in1=gd[:], op=ALU.mult), R=[us, gd], Wd=[ug])
            ydb = ydb_r.next()
            vnbs = []
            for t in range(4):
                tsl = slice(t * 128, (t + 1) * 128)
                bv = abanks.next()
                for k in range(8):
                    op("pe", lambda k=k: nc.tensor.matmul(bv[:, 0:256], lhsT=hTg[:, k, tsl], rhs=WA[:, k, D_V:D_V + 256], start=(k == 0), stop=(k == 7)), R=[WB_A, WB_T, hTg], W=[bv], inc=(k == 7))
                sqv = tokw.next()
                op("act", lambda: nc.scalar.activation(out=sqv[:], in_=bv[:, 0:256], func=AF.Square), R=[bv], W=[sqv])
                stv = st1.next()
                op("dve", lambda: nc.vector.reduce_sum(out=stv[:, 0:4], in_=sqv[:].rearrange("p (h c) -> p h c", c=64), axis=AX.X), R=[sqv], W=[stv])
                op("act", lambda: nc.scalar.activation(out=stv[:, 0:4], in_=stv[:, 0:4], func=AF.Sqrt, bias=eps_t[:], scale=1.0 / 64), R=[eps_t], W=[stv])
                op("dve", lambda: nc.vector.reciprocal(out=stv[:, 0:4], in_=stv[:, 0:4]), W=[stv])
                vn = tokw.next()
                op("dve", lambda: nc.vector.tensor_tensor(out=vn[:].rearrange("p (h c) -> p h c", c=64), in0=bv[:, 0:256].rearrange("p (h c) -> p h c", c=64), in1=stv[:, 0:4].unsqueeze(2).to_broadcast([128, 4, 64]), op=ALU.mult), R=[bv, stv], W=[vn])
                vnb = vnb_r.next()
                op("dve", lambda: nc.vector.tensor_tensor(out=vnb[:], in0=vn[:], in1=SG[:], op=ALU.mult), R=[vn, SG], W=[vnb])
                vnbs.append(vnb)
            for t in range(4):
                tsl = slice(t * 128, (t + 1) * 128)
                tok = slice(g * 512 + t * 128, g * 512 + (t + 1) * 128)
                bv2 = abanks.next()
                for k in range(8):
                    op("pe", lambda k=k: nc.tensor.matmul(bv2[:, 0:256], lhsT=hTg[:, k, tsl], rhs=WA[:, k, B_V:B_V + 256], start=(k == 0), stop=(k == 7)), R=[WB_A, WB_T, hTg], W=[bv2], inc=(k == 7))
                vb = tokb.next()
                op("dve", lambda: nc.vector.tensor_copy(out=vb[:], in_=bv2[:, 0:256]), R=[bv2], W=[vb])
                dma("pool", vtok_d.t[tok, :], vb[:], vb, vtok_d, disjoint=True)
                bv3 = abanks.next()
                for c in range(2):
                    csl = slice(t * 128 + c * 64, t * 128 + (c + 1) * 64)
                    for k in range(8):
                        op("pe", lambda k=k, c=c, csl=csl: nc.tensor.matmul(bv3[0:64, c * 256:(c + 1) * 256], lhsT=hTg[:, k, csl], rhs=WA[:, k, C_I:C_I + 256], start=(k == 0), stop=(k == 7)), R=[WB_A, WB_T, hTg], W=[bv3], inc=(k == 7 and c == 1))
                vh = vhb.next()
                op("dve", lambda: nc.vector.tensor_copy(out=vh[:].rearrange("p c f -> p (c f)"), in_=bv3[0:64, :]), R=[bv3], W=[vh])
                dma("pool", vH_d.t[tok, :].rearrange("(c s) f -> s c f", s=64), vh[:], vh, vH_d, disjoint=True)
            for t in range(4):
                tsl = slice(t * 128, (t + 1) * 128)
                vnb = vnbs[t]
                bs = abanks.next()
                for gh in range(4):
                    hh, j = gh % 2, gh // 2
                    op("pe", lambda gh=gh, hh=hh, j=j: nc.tensor.matmul(bs[64 * hh:64 * hh + 64, j * 128:(j + 1) * 128], lhsT=vnb[:, 64 * gh:64 * gh + 64], rhs=WT[:, gh, :], start=True, stop=True), R=[vnb, WT], W=[bs], inc=(gh == 3))
                ts_ = tokw.next()
                op("dve", lambda: nc.vector.tensor_tensor(out=ts_[:], in0=bs[:, 0:256], in1=BT[:].rearrange("p j t -> p (j t)"), op=ALU.add), R=[bs, BT], W=[ts_])
                op("dve", lambda: nc.vector.tensor_tensor(out=ydb[:, :, tsl], in0=ts_[:].rearrange("p (j t) -> p j t", j=2), in1=ug[:, :, tsl], op=ALU.mult), R=[ts_, ug], Wd=[ydb])
            for j in range(2):
                store(y_d[3], 128 * j, 128, ydb, ydb[:, j, :])
            cpos_transposes()

        dma("pool", xs2.t[:, 0:128], cposT[:].rearrange("p t h -> p (t h)"), cposT, xs2, disjoint=True, sem_owner=flag_t)
        ph.close()
        C.collective(xs1a, xg1a)
        C.collective(xs1b, xg1b)
        C.collective(xs1c, xg1c)
        C.collective(xs2, xg2)
        def phase_H(final):
            ph = Phase(C)
            Sst = C.sb([64, 4, 64], F32, dma=True)
            Sin = C.sb([64, 4, 64], F32, dma=True)
            Sbf_r = Rot([C.sb([64, 4, 64], BF16) for _ in range(10)])
            Qg_r = Rot([C.sb([64, 4, 512], BF16, dma=True) for _ in range(2)])
            Kg_r = Rot([C.sb([64, 4, 512], BF16, dma=True) for _ in range(2)])
            Gg_r = Rot([C.sb([64, 4, 512], BF16, dma=True) for _ in range(2)])
            Vg_r = Rot([C.sb([64, 8, 256], BF16, dma=True) for _ in range(2)])
            Kt_r = Rot([C.sb([64, 4, 64], BF16) for _ in range(8)])
            UT_r = Rot([C.sb([64, 8, 256], F32, dma=True) for _ in range(2)])
            ST_r = Rot([C.sb([64, 8, 64], BF16) for _ in range(2)])
            hq_r = Rot([C.sb([64, 512], BF16) for _ in range(2)])
            hr_r = Rot([C.sb([64, 512], F32) for _ in range(2)])
            hy_r = Rot([C.sb([64, 512], F32) for _ in range(2)])
            hyb_r = Rot([C.sb([64, 512], BF16, dma=True) for _ in range(2)])
            def scan_group(g, final):
                cols = slice(g * 512, (g + 1) * 512)
                Kg, Vg = Kg_r.next(), Vg_r.next()
                dma("sp", Kg[:], Kp_d.t[:, cols].rearrange("(h p) t -> p h t", p=64), Kp_d, Kg)
                dma("sp", Vg[:], vH_d.t[cols, :].rearrange("(c s) f -> s c f", s=64), vH_d, Vg)
                UT = UT_r.next()
                Sbfs = []
                if not final:
                    Kts = []
                    for c in range(8):
                        csl = slice(c * 64, (c + 1) * 64)
                        tb = tbanks.next()
                        for h in range(4):
                            op("pe", lambda h=h: nc.tensor.transpose(out=tb[0:64, h * 64:(h + 1) * 64], in_=Kg[:, h, csl], identity=ident_bf[0:64, 0:64]), R=[Kg, ident_bf], W=[tb], inc=(h == 3))
                        Kt = Kt_r.next()
                        op("act", lambda: nc.scalar.copy(out=Kt[:].rearrange("p h k -> p (h k)"), in_=tb[0:64, 0:256]), R=[tb], W=[Kt])
                        Kts.append(Kt)
                    for c in range(8):
                        chn = g * 8 + c
                        Kt = Kts[c]
                        UB = abanks.next()
                        for h in range(4):
                            op("pe", lambda h=h: nc.tensor.matmul(UB[0:64, h * 64:(h + 1) * 64], lhsT=Kt[:, h, :], rhs=Vg[:, c, 64 * h:64 * h + 64], start=True, stop=True), R=[Kt, Vg], W=[UB], inc=(h == 3))
                        op("dve", lambda: nc.vector.tensor_tensor(out=UT[:, c, :].rearrange("p (h v) -> p h v", h=4), in0=UB[0:64, 0:256].rearrange("p (h v) -> p h v", h=4), in1=CV[:, 2, :, chn:chn + 1].to_broadcast([64, 4, 64]), op=ALU.mult), R=[UB, CV], Wd=[UT])
                    dma("pool", ut_d.t[:, g * 8:(g + 1) * 8, :], UT[:], UT, ut_d, disjoint=True)
                else:
                    Qg, Gg = Qg_r.next(), Gg_r.next()
                    dma("sp", Qg[:], Qp_d.t[:, cols].rearrange("(h p) t -> p h t", p=64), Qp_d, Qg)
                    dma("sp", Gg[:], gC_d.t[:, cols].rearrange("(h p) t -> p h t", p=64), gC_d, Gg)
                    dma("sp", UT[:], ut_d.t[:, g * 8:(g + 1) * 8, :], ut_d, UT)
                for c in range(8):
                    chn = g * 8 + c
                    if final:
                        Sbf = Sbf_r.next()
                        op("pool", lambda: nc.gpsimd.tensor_tensor(out=Sbf[:], in0=Sst[:], in1=CV[:, 0, :, chn:chn + 1].to_broadcast([64, 4, 64]), op=ALU.mult), R=[Sst, CV], W=[Sbf])
                        Sbfs.append(Sbf)
                    op("dve", lambda: nc.vector.tensor_tensor(out=Sst[:], in0=Sst[:], in1=CV[:, 1, :, chn:chn + 1].to_broadcast([64, 4, 64]), op=ALU.mult), R=[CV], W=[Sst])
                    op("dve", lambda: nc.vector.tensor_tensor(out=Sst[:], in0=Sst[:], in1=UT[:, c, :].rearrange("p (h v) -> p h v", h=4), op=ALU.add), R=[UT], W=[Sst])
                if not final:
                    return
                for hp in range(2):
                    hs = (2 * hp, 2 * hp + 1)
                    SBk, STh, OBk, hq, b2, hr, hy, hyb = {}, {}, {}, {}, {}, {}, {}, {}
                    for h in hs:
                        SBk[h] = abanks.next()
                        for c in range(8):
                            csl = slice(c * 64, (c + 1) * 64)
                            op("pe", lambda c=c, csl=csl: nc.tensor.matmul(SBk[h][0:64, csl], lhsT=Kg[:, h, csl], rhs=Qg[:, h, csl], start=True, stop=True), R=[Kg, Qg], W=[SBk[h]], inc=(c == 7))
                    for h in hs:
                        STh[h] = ST_r.next()
                        op("dve", lambda: nc.vector.tensor_tensor(out=STh[h][:], in0=SBk[h][0:64, :].rearrange("p (c t) -> p c t", t=64), in1=tri[:].unsqueeze(1).to_broadcast([64, 8, 64]), op=ALU.mult), R=[SBk[h], tri], W=[STh[h]])
                    for h in hs:
                        OBk[h] = abanks.next()
                        for c in range(8):
                            csl = slice(c * 64, (c + 1) * 64)
                            op("pe", lambda c=c, csl=csl: nc.tensor.matmul(OBk[h][0:64, csl], lhsT=Sbfs[c][:, h, :], rhs=Qg[:, h, csl], start=True, stop=False), R=[Sbfs[c], Qg], W=[OBk[h]], inc=False)
                            op("pe", lambda c=c, csl=csl: nc.tensor.matmul(OBk[h][0:64, csl], lhsT=Vg[:, c, 64 * h:64 * h + 64], rhs=STh[h][:, c, :], start=False, stop=True), R=[Vg, STh[h]], W=[OBk[h]], inc=(c == 7))
                    for h in hs:
                        hq[h] = hq_r.next()
                        op("act", lambda: nc.scalar.activation(out=hq[h][:], in_=OBk[h][0:64, :], func=AF.Square), R=[OBk[h]], W=[hq[h]])
                    for h in hs:
                        b2[h] = abanks.next()
                        op("pe", lambda: nc.tensor.matmul(b2[h][0:64, :], lhsT=blockones[0:64, 0:64], rhs=hq[h][:], start=True, stop=True), R=[blockones, hq[h]], W=[b2[h]])
                    for h in hs:
                        hr[h] = hr_r.next()
                        op("act", lambda: nc.scalar.activation(out=hr[h][:], in_=b2[h][0:64, :], func=AF.Ln, bias=eps_t[0:64, :], scale=1.0 / 64), R=[b2[h], eps_t], W=[hr[h]])
                        op("act", lambda: nc.scalar.activation(out=hr[h][:], in_=hr[h][:], func=AF.Exp, scale=-0.5), W=[hr[h]])
                    for h in hs:
                        hy[h] = hy_r.next()
                        op("dve", lambda: nc.vector.scalar_tensor_tensor(out=hy[h][:], in0=OBk[h][0:64, :], scalar=hgain(h), in1=hr[h][:], op0=ALU.mult, op1=ALU.mult), R=[OBk[h], hr[h], PRM], W=[hy[h]])
                        hyb[h] = hyb_r.next()
                        op("pool", lambda: nc.gpsimd.tensor_tensor(out=hyb[h][:], in0=hy[h][:], in1=Gg[:, h, :], op=ALU.mult), R=[hy[h], Gg], W=[hyb[h]])
                        dma("pool", y_d[2].t[64 * h:64 * h + 64, cols], hyb[h][:], hyb[h], y_d[2], disjoint=True)

            if not final:
                op("dve", lambda: nc.vector.memset(Sst[:], 0.0), W=[Sst])
                for g in range(NG):
                    scan_group(g, False)
                dma("pool", xs3.t[:, :], Sst[:].rearrange("p h v -> p (h v)"), Sst, xs3)
                C.collective(xs3, xg3)
            else:
                dma("sp", Sin[:].rearrange("p h v -> p (h v)"), xg3.t[0:64, :], xg3, Sin)
                op("dve", lambda: nc.vector.tensor_scalar(out=Sst[:], in0=Sin[:], scalar1=flag_t[0:64, :], scalar2=None, op0=ALU.mult), R=[Sin, flag_t], W=[Sst])
                for g in range(NG):
                    scan_group(g, True)
            ph.close()
        phase_H(False)
        ph = Phase(C)
        R2 = C.sb([128, 132], F32, dma=True)
        Rtot = C.sb([128, 4], F32, dma=True)
        dma("sp", R2[:], xg2.t[0:128, :], xg2, R2)
        dma("sp", Rtot[:], xg2.t[127:128, 4 * NT - 4:4 * NT].partition_broadcast(128), xg2, Rtot)
        op("dve", lambda: nc.vector.tensor_tensor(out=biasP[:], in0=R2[:, 0:128].rearrange("p (t h) -> p t h", h=4), in1=Rtot[:].unsqueeze(1).to_broadcast([128, NT, 4]), op=ALU.subtract), R=[R2, Rtot], W=[biasP])
        op("dve", lambda: nc.vector.tensor_scalar(out=biasP[:], in0=biasP[:], scalar1=negbig[:], scalar2=None, op0=ALU.add), R=[negbig], W=[biasP])
        Hh = C.sb([128, 4], F32)
        op("dve", lambda: nc.vector.tensor_scalar(out=Hh[:], in0=R2[:, 128:132], scalar1=flag_t[:], scalar2=None, op0=ALU.mult), R=[R2, flag_t], W=[Hh])
        for j in range(2):
            cc = C.sb([128, 4], F32)
            yfx = C.sb([128, 2], BF16, dma=True)
            op("dve", lambda: nc.vector.tensor_scalar(out=cc[:, 0:1], in0=Hh[:, 2 * j:2 * j + 1], scalar1=convw(j, 0), scalar2=None, op0=ALU.mult), R=[Hh, PRM], W=[cc])
            op("dve", lambda: nc.vector.scalar_tensor_tensor(out=cc[:, 2:3], in0=Hh[:, 2 * j + 1:2 * j + 2], scalar=convw(j, 1), in1=cc[:, 0:1], op0=ALU.mult, op1=ALU.add), R=[Hh, PRM], W=[cc])
            op("dve", lambda: nc.vector.tensor_scalar(out=cc[:, 3:4], in0=Hh[:, 2 * j + 1:2 * j + 2], scalar1=convw(j, 0), scalar2=None, op0=ALU.mult), R=[Hh, PRM], W=[cc])
            op("dve", lambda: nc.vector.tensor_tensor(out=cc[:, 0:2], in0=cc[:, 2:4], in1=GF[:, j, :], op=ALU.mult), R=[GF], W=[cc])
            op("dve", lambda: nc.vector.tensor_tensor(out=yfx[:], in0=cc[:, 0:2], in1=YA0[:, j, :], op=ALU.add), R=[cc, YA0], W=[yfx])
            dma("pool", y_d[0].t[128 * j:128 * j + 128, 0:2], yfx[:], yfx, y_d[0], disjoint=True)
        ph.close()
        WG = WBIG[:, 0:8 * 2048].rearrange("p (k c) -> p k c", k=8)
        WU = WBIG[:, 8 * 2048:8 * 2048 + 8 * 512].rearrange("p (k c) -> p k c", k=8)

        def load_c1(hf, engs, defer=None, WGd=None, WUd=None, wb=None):
            WGd = WG if WGd is None else WGd
            WUd = WU if WUd is None else WUd
            wb = [WB_A] if wb is None else wb
            for br in range(4):
                load_w(lambda k, c0, n, br=br: WGd[:, k, br * 512 + c0:br * 512 + c0 + n],
                       lambda k, c0, n, br=br: (w_in, w_in.t[l, k * 128:(k + 1) * 128, M_G + br * 1024 + hf * 512 + c0:M_G + br * 1024 + hf * 512 + c0 + n]),
                       512, gmix, wb, engs=engs, defer=defer)
            load_w(lambda k, c0, n: WUd[:, k, c0:c0 + n],
                   lambda k, c0, n: (w_up, w_up.t[l, k // 2, (k % 2) * 128:(k % 2) * 128 + 128, hf * 512 + c0:hf * 512 + c0 + n]), 512, None, wb, engs=engs, defer=defer)

        pre_c1 = []
        load_c1(0, Rot(["dve", "pool"]), pre_c1)
        ph = Phase(C)
        Kaug = C.sb([128, S], BF16, dma=True)
        Vp = C.sb([128, S], BF16, dma=True)
        KaugP = C.sb([128, S], BF16, dma=True)
        VpP = C.sb([128, S], BF16, dma=True)
        VpP3 = VpP.t.rearrange("p (t f) -> p t f", f=128)
        op("pool", lambda: nc.gpsimd.memset(VpP3[:, :, 64:128], 1.0), Wd=[VpP])
        Vp3 = Vp.t.rearrange("p (t f) -> p t f", f=128)
        qa_r = Rot([C.sb([67, 512], BF16, dma=True) for _ in range(2)])
        gbt_r = Rot([C.sb([64, 512], BF16, dma=True) for _ in range(2)])
        P_r = Rot([C.sb([128, 512], BF16) for _ in range(4)])
        rd_r = Rot([C.sb([64, 512], F32) for _ in range(2)])
        of_r = Rot([C.sb([64, 512], F32) for _ in range(2)])
        yo_r = Rot([C.sb([64, 512], BF16, dma=True) for _ in range(2)])
        op("pool", lambda: nc.gpsimd.memset(Vp3[:, :, 64:128], 1.0), Wd=[Vp])
        for h in range(4):
            dma("sp", Kaug.t[0:67, :], kaug_h(h)[1], kaug_h(h)[0], Kaug)
            dma("sp", KaugP.t[0:67, :], kaugP_h(h)[1], kaugP_h(h)[0], KaugP)
            for c4 in range(S // 2048):
                dma("sp", Vp3[:, c4 * 16:(c4 + 1) * 16, 0:64], vtok_d.t[c4 * 2048:(c4 + 1) * 2048, 64 * h:64 * h + 64].rearrange("(t p) f -> p t f", p=128), vtok_d, Vp, disjoint=True)
                dma("sp", VpP3[:, c4 * 16:(c4 + 1) * 16, 0:64], vtokP_d.t[c4 * 2048:(c4 + 1) * 2048, 64 * h:64 * h + 64].rearrange("(t p) f -> p t f", p=128), vtokP_d, VpP, disjoint=True)
            pending = []

            def flush():
                while pending:
                    pending.pop(0)()

            for qg in range(NG):
                cols = slice(qg * 512, (qg + 1) * 512)
                qa = qa_r.next()
                dma("sp", qa[:], qaug_d.t[h, :, cols], qaug_d, qa)
                gbt = gbt_r.next()
                dma("sp", gbt[:], gB_d.t[64 * h:64 * h + 64, cols], gB_d, gbt)
                OB = obanks.next()
                nkt = NT + 4 * qg + 4
                for kt_all in range(nkt):
                    prevh = kt_all < NT
                    kt = kt_all if prevh else kt_all - NT
                    i = -1 if prevh else kt - 4 * qg
                    q0 = 0 if i < 0 else 128 * i
                    N = 512 - q0
                    SB = banks.next()
                    ksl = slice(kt * 128, (kt + 1) * 128)
                    Ksrc = KaugP if prevh else Kaug
                    Vsrc, Vsrc3 = (VpP, VpP3) if prevh else (Vp, Vp3)
                    bsrc = biasP if prevh else cposT
                    if i < 0:
                        op("pe", lambda: nc.tensor.matmul(SB[:, 0:N], lhsT=Ksrc.t[0:67, ksl], rhs=qa[:, q0:512], start=True, stop=True), R=[Ksrc, qa], W=[SB])
                    else:
                        op("pe", lambda: nc.tensor.matmul(SB[:, 0:N], lhsT=Kaug.t[0:67, ksl], rhs=qa[:, q0:512], start=True, stop=False), R=[Kaug, qa], W=[SB], inc=False)
                        op("pe", lambda: nc.tensor.matmul(SB[:, 0:128], lhsT=ident_bf[:], rhs=maskT[:], start=False, stop=True), R=[ident_bf, maskT], W=[SB])
                    while len(pending) >= 2:
                        pending.pop(0)()
                    last = (kt_all == nkt - 1)
                    first = (kt_all == 0)

                    def stage2(kt=kt, q0=q0, N=N, SB=SB, last=last, first=first, OB=OB, gbt=gbt, cols=cols, Vsrc=Vsrc, Vsrc3=Vsrc3, bsrc=bsrc):
                        Pt = P_r.next()
                        op("act", lambda: nc.scalar.activation(out=Pt[:, 0:N], in_=SB[:, 0:N], func=AF.Exp, bias=bsrc[:, kt, h:h + 1], scale=1.0), R=[SB, bsrc], W=[Pt])
                        op("pe", lambda: nc.tensor.matmul(OB[:, q0:512], lhsT=Vsrc3[:, kt, :], rhs=Pt[:, 0:N], start=first, stop=last), R=[Vsrc, Pt], W=[OB], inc=last)
                        if last:
                            rd = rd_r.next()
                            op("dve", lambda: nc.vector.reciprocal(out=rd[:], in_=OB[64:128, :]), R=[OB], W=[rd])
                            of = of_r.next()
                            op("dve", lambda: nc.vector.tensor_tensor(out=of[:], in0=OB[0:64, :], in1=rd[:], op=ALU.mult), R=[OB, rd], W=[of])
                            yo = yo_r.next()
                            op("pool", lambda: nc.gpsimd.tensor_tensor(out=yo[:], in0=of[:], in1=gbt[:], op=ALU.mult), R=[of, gbt], W=[yo])
                            dma("pool", y_d[1].t[64 * h:64 * h + 64, cols], yo[:], yo, y_d[1], disjoint=True)
                    pending.append(stage2)
                drain(pre_c1, 2)
            flush()

        ph.close()
        phase_H(True)
        WO = WBIG[:, 20480:20480 + 8 * 1024].rearrange("p (k c) -> p k c", k=8)
        WPG = WB2[:, 0:8 * 1024].rearrange("p (k c) -> p k c", k=8)
        WPP = WB2[:, 8 * 1024:10 * 1024].rearrange("p (k c) -> p k c", k=2)
        ph = Phase(C)
        WC1B = C.sb([128, 20480], BF16)
        WG1 = WC1B[:, 0:8 * 2048].rearrange("p (k c) -> p k c", k=8)
        WU1 = WC1B[:, 8 * 2048:8 * 2048 + 8 * 512].rearrange("p (k c) -> p k c", k=8)
        pre_c1b = []
        load_c1(1, Rot(["act", "dve"]), pre_c1b, WG1, WU1, [WC1B])
        hTg_r = Rot([C.sb([128, 8, 512], BF16, dma=True) for _ in range(2)])
        Yg_r = Rot([C.sb([128, 8, 512], BF16, dma=True) for _ in range(2)])
        mg_r = Rot([C.sb([128, 4, 512], BF16, dma=True) for _ in range(2)])
        acc_r = Rot([C.sb([128, 512], F32) for _ in range(3)])
        f32w = Rot([C.sb([128, 512], F32) for _ in range(6)])
        for hf in range(2):
            WGh, WUh, WBh = (WG, WU, WB_A) if hf == 0 else (WG1, WU1, WC1B)
            if hf == 1:
                drain(pre_c1b)
                pre_c2 = []
                load_w(lambda k, c0, n: WO[:, k, c0:c0 + n], lambda k, c0, n: (w_o, w_o.t[l, k * 128:(k + 1) * 128, c0:c0 + n]), 1024, None, [WB_T], engs=Rot(["act", "dve"]), defer=pre_c2)
                load_w(lambda k, c0, n: WPG[:, k, c0:c0 + n], lambda k, c0, n: (w_pg, w_pg.t[l, k * 128:(k + 1) * 128, c0:c0 + n]), 1024, gple, [WB2], engs=Rot(["act", "dve"]), defer=pre_c2)
                load_w(lambda k, c0, n: WPP[:, k, c0:c0 + n], lambda k, c0, n: (w_pp, w_pp.t[l, k * 128:(k + 1) * 128, c0:c0 + n]), 1024, None, [WB2], ks=range(2), engs=Rot(["act", "dve"]), defer=pre_c2)
            else:
                drain(pre_c1)
            for g in range(NG):
                cols = slice(g * 512, (g + 1) * 512)
                hTg = hTg_r.next()
                dma("sp", hTg[:], hT_d.t.rearrange("(k p) s -> p k s", p=128)[:, :, cols], hT_d, hTg)
                Yg = Yg_r.next()
                for br in range(4):
                    dma("sp", Yg[:, 2 * br:2 * br + 2, :], y_d[br].t[:, cols].rearrange("(k p) t -> p k t", p=128), y_d[br], Yg, disjoint=(br > 0))
                mg = mg_r.next()
                for db in range(4):
                    dsl = slice(db * 128, (db + 1) * 128)
                    acc = None
                    for br in range(4):
                        gbk = abanks.next()
                        for k in range(8):
                            op("pe", lambda k=k: nc.tensor.matmul(gbk[:], lhsT=WGh[:, k, br * 512 + db * 128:br * 512 + db * 128 + 128], rhs=hTg[:, k, :], start=(k == 0), stop=(k == 7)), R=[WBh, hTg], W=[gbk], inc=(k == 7))
                        sg_ = f32w.next()
                        op("act", lambda: nc.scalar.activation(out=sg_[:], in_=gbk[:], func=AF.Sigmoid, bias=mergeb(br, hf * 4 + db), scale=1.0), R=[gbk, PRM], W=[sg_])
                        ubk = abanks.next()
                        for kk in range(2):
                            op("pe", lambda kk=kk: nc.tensor.matmul(ubk[:], lhsT=WUh[:, 2 * br + kk, dsl], rhs=Yg[:, 2 * br + kk, :], start=(kk == 0), stop=(kk == 1)), R=[WBh, Yg], W=[ubk], inc=(kk == 1))
                        if br == 0:
                            acc = acc_r.next()
                            op("dve", lambda: nc.vector.tensor_tensor(out=acc[:], in0=sg_[:], in1=ubk[:], op=ALU.mult), R=[sg_, ubk], W=[acc])
                        else:
                            tt = f32w.next()
                            op("dve", lambda: nc.vector.tensor_tensor(out=tt[:], in0=sg_[:], in1=ubk[:], op=ALU.mult), R=[sg_, ubk], W=[tt])
                            if br == 1:
                                op("dve", lambda: nc.vector.tensor_tensor(out=acc[:], in0=acc[:], in1=tt[:], op=ALU.add), R=[tt], W=[acc])
                            elif br < 3:
                                op("pool", lambda: nc.gpsimd.tensor_tensor(out=acc[:], in0=acc[:], in1=tt[:], op=ALU.add), R=[tt], W=[acc])
                            else:
                                op("pool", lambda: nc.gpsimd.tensor_tensor(out=mg[:, db, :], in0=acc[:], in1=tt[:], op=ALU.add), R=[tt, acc], Wd=[mg])
                dma("pool", mT_d.t.rearrange("(k p) s -> p k s", p=128)[:, hf * 4:hf * 4 + 4, cols], mg[:], mg, mT_d, disjoint=True)
                if hf == 1:
                    drain(pre_c2, 3)
                else:
                    drain(pre_c1b, 5)
        drain(pre_c2)
        ph.close()

        ph = Phase(C)
        if l + 1 < n_layers:
            load_wa(l + 1, range(0, 5), defer=pre_a, engs=Rot(["act", "dve"]))
        mgl_r = Rot([C.sb([128, 8, 512], BF16, dma=True) for _ in range(2)])
        xin_r = Rot([C.sb([128, 1024], F32, dma=True) for _ in range(2)])
        xn_r = Rot([C.sb([128, 1024], F32) for _ in range(2)])
        xo_r = Rot([C.sb([128, 1024], F32, dma=True) for _ in range(2)])
        hb_r = Rot([C.sb([128, 1024], BF16) for _ in range(2)])
        junk = C.sb([128, 1024], BF16)
        st1 = Rot([C.sb([128, 4], F32) for _ in range(4)])
        hpT_r = Rot([C.sb([128, 8, 128], BF16) for _ in range(2)])
        pin_r = Rot([C.sb([128, 256], F32, dma=True) for _ in range(2)])
        pb_r = Rot([C.sb([128, 256], BF16) for _ in range(2)])
        pT_r = Rot([C.sb([128, 2, 128], BF16) for _ in range(2)])
        f32w = Rot([C.sb([128, 512], F32) for _ in range(6)])
        mgs = {}

        def c2_stage1(ti):
            g, t = ti // 4, ti % 4
            if t == 0:
                mgs[g] = mgl_r.next()
                dma("sp", mgs[g][:], mT_d.t.rearrange("(k p) s -> p k s", p=128)[:, :, g * 512:(g + 1) * 512], mT_d, mgs[g])
            mg = mgs[g]
            tsl = slice(t * 128, (t + 1) * 128)
            tok = slice(ti * 128, (ti + 1) * 128)
            xin = xin_r.next()
            dma("sp", xin[:], xsrc.t[tok, :], xsrc, xin)
            pin = pin_r.next()
            dma("sp", pin[:], p_in.t[l, tok, :], p_in, pin)
            xn = xn_r.next()
            for half in range(2):
                hs = slice(half * 512, (half + 1) * 512)
                obk = abanks.next()
                for db in range(8):
                    op("pe", lambda db=db: nc.tensor.matmul(obk[:], lhsT=mg[:, db, tsl], rhs=WO[:, db, hs], start=(db == 0), stop=(db == 7)), R=[WB_T, mg], W=[obk], inc=(db == 7))
                op("dve", lambda: nc.vector.tensor_tensor(out=xn[:, hs], in0=xin[:, hs], in1=obk[:], op=ALU.add), R=[xin, obk], Wd=[xn])
            pb = pb_r.next()
            op("pool", lambda: nc.gpsimd.tensor_copy(out=pb[:], in_=pin[:]), R=[pin], W=[pb])
            return xn, pb

        def c2_stage1b(xn):
            st = rms_rstd(xn[:], xn, 1024, junk, st1)
            hb = hb_r.next()
            op("dve", lambda: nc.vector.tensor_scalar(out=hb[:], in0=xn[:], scalar1=st[:, 2:3], scalar2=None, op0=ALU.mult), R=[xn, st], W=[hb])
            return hb

        xn0, pb0 = c2_stage1(0)
        nxt = (xn0, c2_stage1b(xn0), pb0)
        for ti in range(NT):
            tok = slice(ti * 128, (ti + 1) * 128)
            xn, hb, pb = nxt
            nx = c2_stage1(ti + 1) if ti + 1 < NT else None
            if True:
                tb = tbanks.next()
                for k in range(8):
                    op("pe", lambda k=k: nc.tensor.transpose(out=tb[:, k * 128:(k + 1) * 128], in_=hb[:, k * 128:(k + 1) * 128], identity=ident_bf[:]), R=[hb, ident_bf], W=[tb], inc=(k == 7))
                hpT = hpT_r.next()
                op("act", lambda: nc.scalar.copy(out=hpT[:], in_=tb[:].rearrange("p (k t) -> p k t", k=8)), R=[tb], W=[hpT])
                tb2 = tbanks.next()
                for k in range(2):
                    op("pe", lambda k=k: nc.tensor.transpose(out=tb2[:, k * 128:(k + 1) * 128], in_=pb[:, k * 128:(k + 1) * 128], identity=ident_bf[:]), R=[pb, ident_bf], W=[tb2], inc=(k == 1))
                pT = pT_r.next()
                op("act", lambda: nc.scalar.copy(out=pT[:], in_=tb2[:, 0:256].rearrange("p (k t) -> p k t", k=2)), R=[tb2], W=[pT])
                if nx is not None:
                    nxt = (nx[0], c2_stage1b(nx[0]), nx[1])
                xo = xo_r.next()
                for half in range(2):
                    hs = slice(half * 512, (half + 1) * 512)
                    gbk = abanks.next()
                    for k in range(8):
                        op("pe", lambda k=k: nc.tensor.matmul(gbk[:], lhsT=hpT[:, k, :], rhs=WPG[:, k, hs], start=(k == 0), stop=(k == 7)), R=[WB2, hpT], W=[gbk], inc=(k == 7))
                    sgp = f32w.next()
                    op("act", lambda: nc.scalar.activation(out=sgp[:], in_=gbk[:], func=AF.Sigmoid), R=[gbk], W=[sgp])
                    pbk = abanks.next()
                    for k in range(2):
                        op("pe", lambda k=k: nc.tensor.matmul(pbk[:], lhsT=pT[:, k, :], rhs=WPP[:, k, hs], start=(k == 0), stop=(k == 1)), R=[WB2, pT], W=[pbk], inc=(k == 1))
                    tt = f32w.next()
                    op("dve", lambda: nc.vector.tensor_tensor(out=tt[:], in0=sgp[:], in1=pbk[:], op=ALU.mult), R=[sgp, pbk], W=[tt])
                    if half == 0:
                        op("dve", lambda: nc.vector.tensor_tensor(out=xo[:, hs], in0=xn[:, hs], in1=tt[:], op=ALU.add), R=[xn, tt], Wd=[xo])
                    else:
                        op("pool", lambda: nc.gpsimd.tensor_tensor(out=xo[:, hs], in0=xn[:, hs], in1=tt[:], op=ALU.add), R=[xn, tt], Wd=[xo])
                dma("pool", xdst.t[tok, :], xo[:], xo, xdst, disjoint=True)
            if ti % 4 == 3:
                drain(pre_a, 3)
        ph.close()

    C.wait_all("pool", [y_out, xres])
    C.wait_all("sp", [y_out])


_NC_CACHE = {}


def kernel(**inputs):
    n_cores = 8
    if "nc" not in _NC_CACHE:
        _NC_CACHE["nc"] = build_nc(DEPTH)
    nc = _NC_CACHE["nc"]
    names = ["norm_mix", "w_in", "conv_w", "conv_b", "fgate_bias", "q_norm", "k_norm", "lb_logits", "hgrn_norm",
             "sgu_norm", "spatial_w", "spatial_b", "w_up", "merge_b", "w_o", "norm_ple", "w_ple_gate", "w_ple_proj"]
    shared = {n: np.ascontiguousarray(np.asarray(inputs[n], dtype=np.float32)) for n in names}
    x = np.asarray(inputs["x"], dtype=np.float32)
    p = np.asarray(inputs["p"], dtype=np.float32)
    in_maps = []
    for c in range(n_cores):
        b, hf = c // 2, c % 2
        m = dict(shared)
        m["x"] = np.ascontiguousarray(x[b, hf * S:(hf + 1) * S])
        m["p"] = np.ascontiguousarray(p[:, b, hf * S:(hf + 1) * S])
        m["flag"] = np.full((128, 1), float(hf), dtype=np.float32)
        in_maps.append(m)
    res = run_bass_kernel_spmd(nc, in_maps, core_ids=list(range(n_cores)))
    out = np.empty((4, 2 * S, D), dtype=np.float32)
    for c in range(n_cores):
        out[c // 2, (c % 2) * S:(c % 2 + 1) * S] = res.results[c]["y"]
    return out
```
